# Optimizing a Trainium2 kernel written in Bass

```python
import math
import jax, jax.numpy as jnp
from jax import lax
import numpy as np

D_MODEL = 2048
BATCH = 1
SEQ = 16384
DEPTH = 2
DEC_BATCH = 16
DEC_SEQ = 16
PAST_LEN = 4096

CHUNK = 64
N_MIXERS = 2
S5_GROUP = 16
S5_GROUPS = D_MODEL // S5_GROUP
S5_STATE = 64
DT_MIN = 1e-3
DT_MAX = 1e-1
N_HEADS = 16
HEAD_DIM = D_MODEL // N_HEADS
PAST_CHUNKS = 8
BAND_PAST = PAST_CHUNKS * CHUNK
BAND = BAND_PAST + CHUNK
REL_CLIP = 128
ATTN_SCALE = HEAD_DIM ** -0.5
NEG_INF = -1e30
D_FF = 5632
PLE_DIM = 256
N_NORMS = 4
DN_ALPHA = (2 * DEPTH) ** 0.25
DN_BETA = (8 * DEPTH) ** -0.25
LN_EPS = 1e-5

kernel_name = "hybrid_s5_chunkattn_streaming_step"


def layer_norm(x, g, b):
    xf = x.astype(jnp.float32)
    mu = jnp.mean(xf, axis=-1, keepdims=True)
    var = jnp.mean(jnp.square(xf - mu), axis=-1, keepdims=True)
    y = (xf - mu) * lax.rsqrt(var + LN_EPS) * g.astype(jnp.float32) + b.astype(jnp.float32)
    return y.astype(x.dtype)


def post_norm(x, sub, g, b):
    return layer_norm(DN_ALPHA * x + sub, g, b)


def swiglu(x, w_in, w_out):
    gate, up = jnp.split(x @ w_in, 2, axis=-1)
    return (jax.nn.silu(gate) * up) @ w_out


def per_layer_embed(x, p, w_proj, w_gate):
    return (p.astype(x.dtype) @ w_proj) * jax.nn.sigmoid(x @ w_gate)


def s5_discretize(a_re, a_im, log_dt):
    f32 = jnp.float32
    ar, ai = a_re.astype(f32), a_im.astype(f32)
    dt = jnp.exp(log_dt.astype(f32))[:, None]
    mag = jnp.exp(ar * dt)
    ab_re, ab_im = mag * jnp.cos(ai * dt), mag * jnp.sin(ai * dt)
    den = ar * ar + ai * ai
    nr, ni = ab_re - 1.0, ab_im
    coef_re = (nr * ar + ni * ai) / den
    coef_im = (ni * ar - nr * ai) / den
    return ab_re, ab_im, coef_re, coef_im


def _ssm_combine(e1, e2):
    a1r, a1i, b1r, b1i = e1
    a2r, a2i, b2r, b2i = e2
    return (a2r * a1r - a2i * a1i,
            a2r * a1i + a2i * a1r,
            a2r * b1r - a2i * b1i + b2r,
            a2r * b1i + a2i * b1r + b2i)


def s5_mixer(x, h0_re, h0_im, a_re, a_im, log_dt, b_re, b_im, c_re, c_im, d_skip, w_glu):
    f32 = jnp.float32
    bsz, L, _ = x.shape
    T = min(CHUNK, L)
    nblk = L // T
    ab_re, ab_im, coef_re, coef_im = s5_discretize(a_re, a_im, log_dt)
    br, bi = b_re.astype(f32), b_im.astype(f32)
    cr, ci = c_re.astype(f32), c_im.astype(f32)
    dd = d_skip.astype(f32)
    u = x.astype(f32).reshape(bsz, nblk, T, S5_GROUPS, S5_GROUP).transpose(1, 0, 2, 3, 4)
    a_shape = (bsz, T, S5_GROUPS, S5_STATE)
    a_r = jnp.broadcast_to(ab_re, a_shape)
    a_i = jnp.broadcast_to(ab_im, a_shape)

    def block(h, ub):
        hr, hi = h
        pr = jnp.einsum('btgc,gpc->btgp', ub, br)
        pi = jnp.einsum('btgc,gpc->btgp', ub, bi)
        bu_r = coef_re * pr - coef_im * pi
        bu_i = coef_re * pi + coef_im * pr
        acr, aci, sr, si = lax.associative_scan(_ssm_combine, (a_r, a_i, bu_r, bu_i), axis=1)
        sr = acr * hr[:, None] - aci * hi[:, None] + sr
        si = acr * hi[:, None] + aci * hr[:, None] + si
        y = (jnp.einsum('btgp,gcp->btgc', sr, cr) - jnp.einsum('btgp,gcp->btgc', si, ci)
             + dd * ub)
        return (sr[:, -1], si[:, -1]), y

    (hr, hi), y = lax.scan(block, (h0_re.astype(f32), h0_im.astype(f32)), u)
    y = y.transpose(1, 0, 2, 3, 4).reshape(bsz, L, D_MODEL).astype(x.dtype)
    z = jax.nn.gelu(y)
    za, zb = jnp.split(z @ w_glu, 2, axis=-1)
    return za * jax.nn.sigmoid(zb), hr, hi


def rel_bias(table, q_off, nq, nk):
    dist = q_off + jnp.arange(nq)[:, None] - jnp.arange(nk)[None, :]
    idx = jnp.clip(dist, -REL_CLIP, REL_CLIP) + REL_CLIP
    return table[:, idx].astype(jnp.float32)


def band_attend(q, k, v, bias, valid):
    s = jnp.einsum('bqhd,bkhd->bhqk', q, k).astype(jnp.float32) * ATTN_SCALE + bias
    s = jnp.where(valid, s, NEG_INF)
    p = jax.nn.softmax(s, axis=-1).astype(v.dtype)
    return jnp.einsum('bhqk,bkhd->bqhd', p, v)


def split_qkv(x, w_qkv):
    bsz, L, _ = x.shape
    q, k, v = jnp.split(x @ w_qkv, 3, axis=-1)
    shp = (bsz, L, N_HEADS, HEAD_DIM)
    return q.reshape(shp), k.reshape(shp), v.reshape(shp)


def attn_prompt(x, w_qkv, w_o, table):
    bsz, L, _ = x.shape
    q, k, v = split_qkv(x, w_qkv)
    pad = ((0, 0), (BAND_PAST, 0), (0, 0), (0, 0))
    kp, vp = jnp.pad(k, pad), jnp.pad(v, pad)
    bias = rel_bias(table, BAND_PAST, CHUNK, BAND)

    def one_chunk(c):
        start = c * CHUNK
        qc = lax.dynamic_slice_in_dim(q, start, CHUNK, axis=1)
        kc = lax.dynamic_slice_in_dim(kp, start, BAND, axis=1)
        vc = lax.dynamic_slice_in_dim(vp, start, BAND, axis=1)
        valid = start - BAND_PAST + jnp.arange(BAND) >= 0
        return band_attend(qc, kc, vc, bias, valid)

    o = lax.map(one_chunk, jnp.arange(L // CHUNK))
    o = o.transpose(1, 0, 2, 3, 4).reshape(bsz, L, D_MODEL)
    rows = min(BAND_PAST, L)
    return o @ w_o, k[:, L - rows:], v[:, L - rows:]


def attn_sample(x, cache_k, cache_v, w_qkv, w_o, table):
    bsz, S, _ = x.shape
    q, k, v = split_qkv(x, w_qkv)
    R = cache_k.shape[1]
    kk = jnp.concatenate([cache_k.astype(k.dtype), k], axis=1)
    vv = jnp.concatenate([cache_v.astype(v.dtype), v], axis=1)
    bias = rel_bias(table, R, S, R + S)
    valid = jnp.ones((R + S,), dtype=bool)
    o = band_attend(q, kk, vv, bias, valid).reshape(bsz, S, D_MODEL)
    return o @ w_o, k, v


def setup_inputs(seed: int = 0) -> dict:
    key = jax.random.key(seed)
    ks = jax.random.split(key, 32)
    f32 = jnp.float32
    nrm = lambda k, shp, s: jax.random.normal(k, shp, f32) * s
    cache_rows = min(BAND_PAST, PAST_LEN)
    n_idx = jnp.arange(S5_STATE, dtype=f32)
    return {
        "x_prompt": nrm(ks[0], (BATCH, SEQ, D_MODEL), 1.0),
        "x_sample": nrm(ks[1], (DEC_BATCH, DEC_SEQ, D_MODEL), 1.0),
        "state_s5_re": nrm(ks[2], (DEC_BATCH, S5_GROUPS, S5_STATE), 0.1),
        "state_s5_im": nrm(ks[3], (DEC_BATCH, S5_GROUPS, S5_STATE), 0.1),
        "cache_k": nrm(ks[4], (DEC_BATCH, cache_rows, N_HEADS, HEAD_DIM), 1.0),
        "cache_v": nrm(ks[5], (DEC_BATCH, cache_rows, N_HEADS, HEAD_DIM), 1.0),
        "p_prompt": nrm(ks[6], (DEPTH, BATCH, SEQ, PLE_DIM), 1.0),
        "p_sample": nrm(ks[7], (DEPTH, DEC_BATCH, DEC_SEQ, PLE_DIM), 1.0),
        "ffn1_w_in": nrm(ks[8], (DEPTH, D_MODEL, 2 * D_FF), D_MODEL ** -0.5),
        "ffn1_w_out": nrm(ks[9], (DEPTH, D_FF, D_MODEL), DN_BETA * D_FF ** -0.5),
        "ffn2_w_in": nrm(ks[10], (DEPTH, D_MODEL, 2 * D_FF), D_MODEL ** -0.5),
        "ffn2_w_out": nrm(ks[11], (DEPTH, D_FF, D_MODEL), DN_BETA * D_FF ** -0.5),
        "ln_g": 1.0 + nrm(ks[12], (DEPTH, N_NORMS, D_MODEL), 0.02),
        "ln_b": nrm(ks[13], (DEPTH, N_NORMS, D_MODEL), 0.02),
        "ple_w_proj": nrm(ks[14], (DEPTH, PLE_DIM, D_MODEL), DN_BETA * PLE_DIM ** -0.5),
        "ple_w_gate": nrm(ks[15], (DEPTH, D_MODEL, D_MODEL), D_MODEL ** -0.5),
        "s5_a_re": -0.5 + nrm(ks[16], (S5_GROUPS, S5_STATE), 0.01),
        "s5_a_im": math.pi * n_idx[None, :] + nrm(ks[17], (S5_GROUPS, S5_STATE), 0.01),
        "s5_log_dt": jax.random.uniform(ks[18], (S5_GROUPS,), f32, math.log(DT_MIN), math.log(DT_MAX)),
        "s5_b_re": nrm(ks[19], (S5_GROUPS, S5_STATE, S5_GROUP), (2 * S5_GROUP) ** -0.5),
        "s5_b_im": nrm(ks[20], (S5_GROUPS, S5_STATE, S5_GROUP), (2 * S5_GROUP) ** -0.5),
        "s5_c_re": nrm(ks[21], (S5_GROUPS, S5_GROUP, S5_STATE), (2 * S5_STATE) ** -0.5),
        "s5_c_im": nrm(ks[22], (S5_GROUPS, S5_GROUP, S5_STATE), (2 * S5_STATE) ** -0.5),
        "s5_d": nrm(ks[23], (S5_GROUPS, S5_GROUP), 1.0),
        "s5_w_glu": nrm(ks[24], (D_MODEL, 2 * D_MODEL), DN_BETA * D_MODEL ** -0.5),
        "attn_w_qkv": nrm(ks[25], (D_MODEL, 3 * D_MODEL), D_MODEL ** -0.5),
        "attn_w_o": nrm(ks[26], (D_MODEL, D_MODEL), DN_BETA * D_MODEL ** -0.5),
        "attn_rel_bias": nrm(ks[27], (N_HEADS, 2 * REL_CLIP + 1), 0.5),
    }


def reference(x_prompt, x_sample, state_s5_re, state_s5_im, cache_k, cache_v, p_prompt, p_sample,
              ffn1_w_in, ffn1_w_out, ffn2_w_in, ffn2_w_out, ln_g, ln_b, ple_w_proj, ple_w_gate,
              s5_a_re, s5_a_im, s5_log_dt, s5_b_re, s5_b_im, s5_c_re, s5_c_im, s5_d, s5_w_glu,
              attn_w_qkv, attn_w_o, attn_rel_bias):
    def macaron(x, w_in, w_out, i, slot):
        return post_norm(x, 0.5 * swiglu(x, w_in, w_out), ln_g[i, slot], ln_b[i, slot])

    s5_w = (s5_a_re, s5_a_im, s5_log_dt, s5_b_re, s5_b_im, s5_c_re, s5_c_im, s5_d, s5_w_glu)
    xp, xs = x_prompt, x_sample
    for i in range(DEPTH):
        xp = macaron(xp, ffn1_w_in[i], ffn1_w_out[i], i, 0)
        xs = macaron(xs, ffn1_w_in[i], ffn1_w_out[i], i, 0)
        if i % N_MIXERS == 0:
            h0 = jnp.zeros((xp.shape[0], S5_GROUPS, S5_STATE), jnp.float32)
            mp, s5_re_p, s5_im_p = s5_mixer(xp, h0, h0, *s5_w)
            ms, s5_re_s, s5_im_s = s5_mixer(xs, state_s5_re, state_s5_im, *s5_w)
        else:
            mp, k_p, v_p = attn_prompt(xp, attn_w_qkv, attn_w_o, attn_rel_bias)
            ms, k_s, v_s = attn_sample(xs, cache_k, cache_v, attn_w_qkv, attn_w_o, attn_rel_bias)
        xp = post_norm(xp, mp, ln_g[i, 1], ln_b[i, 1])
        xs = post_norm(xs, ms, ln_g[i, 1], ln_b[i, 1])
        xp = macaron(xp, ffn2_w_in[i], ffn2_w_out[i], i, 2)
        xs = macaron(xs, ffn2_w_in[i], ffn2_w_out[i], i, 2)
        xp = post_norm(xp, per_layer_embed(xp, p_prompt[i], ple_w_proj[i], ple_w_gate[i]),
                       ln_g[i, 3], ln_b[i, 3])
        xs = post_norm(xs, per_layer_embed(xs, p_sample[i], ple_w_proj[i], ple_w_gate[i]),
                       ln_g[i, 3], ln_b[i, 3])
    return (xp, xs, s5_re_p, s5_im_p, k_p, v_p, s5_re_s, s5_im_s, k_s, v_s)
```

```python
import math
import numpy as np
import concourse.bass as bass
import concourse.mybir as mybir
from concourse.bass_utils import run_bass_kernel_spmd

F32 = mybir.dt.float32
BF16 = mybir.dt.bfloat16
AF = mybir.ActivationFunctionType
ALU = mybir.AluOpType
AX = mybir.AxisListType

D = 2048
KT = 16
FF = 5632
FT = 44
NT = 512
NCORES = 8
ALPHA = float((2 * 2) ** 0.25)
LN_EPS = 1e-5
N_PRE = 27
N_OWN = 5
NS = 32
RING = 16


class Buf:
    __slots__ = ("name", "w", "r", "wl", "multi")

    def __init__(self, name="", multi=False):
        self.name = name
        self.w = None
        self.r = {}
        self.wl = {}
        self.multi = multi


class Sched:
    ENG = ("pe", "act", "dve", "pool", "sp")

    def __init__(self, nc):
        self.nc = nc
        self.h = {"pe": nc.tensor, "act": nc.scalar, "dve": nc.vector, "pool": nc.gpsimd, "sp": nc.sync}
        self.nsem = 0
        self.psem = {}
        self.pekeys = set()
        for e in self.ENG:
            self._newsem(e)
        self.seen = {e: {} for e in self.ENG}
        self.ring = {q: [self._alloc() for _ in range(RING)] for q in ("sp", "pool")}
        self.ridx = {"sp": 0, "pool": 0}
        self.nins = {e: 0 for e in self.ENG}

    def _alloc(self):
        self.nsem += 1
        return [self.nsem, self.nc.alloc_semaphore(f"sm{self.nsem}"), 0]

    def _newsem(self, e):
        k = self._alloc()
        self.psem[e] = k
        if e == "pe":
            self.pekeys.add(k[0])

    def _wait(self, e, key, sem, val):
        if self.seen[e].get(key, 0) >= val:
            return
        self.h[e].wait_ge(sem, val)
        self.nins[e] += 1
        self.seen[e][key] = val

    def _deps(self, e, reads, writes, accum=False):
        need = {}

        def add(t):
            if t[0] not in need or need[t[0]][2] < t[2]:
                need[t[0]] = t
        for b in reads:
            if b.w is not None:
                add(b.w)
            for t in b.wl.values():
                add(t)
        for b in writes:
            if b.w is not None:
                add(b.w)
            if not accum:
                for t in b.wl.values():
                    add(t)
            for t in b.r.values():
                add(t)
        for key, t in need.items():
            if e == "pe" and key in self.pekeys:
                continue
            self._wait(e, t[0], t[1], t[2])

    def op(self, e, fn, reads=(), writes=()):
        self._deps(e, reads, writes)
        ins = fn(self.h[e])
        k = self.psem[e]
        k[2] += 1
        ins.then_inc(k[1], 1)
        self.nins[e] += 1
        tok = (k[0], k[1], k[2])
        for b in writes:
            b.w = tok
            b.r = {}
            b.wl = {}
        for b in reads:
            b.r[e] = tok
        if k[2] >= 30000:
            self._newsem(e)
        return tok

    def dma(self, q, out, in_, reads=(), writes=(), accum=False, **kw):
        if writes and all(b.multi for b in writes):
            accum = True
        self._deps(q, reads, writes, accum)
        i = self.ridx[q]
        self.ridx[q] = (i + 1) % RING
        slot = self.ring[q][i]
        if slot[2] > 0:
            self._wait(q, slot[0], slot[1], slot[2])
        if slot[2] >= 30000:
            slot = self._alloc()
            self.ring[q][i] = slot
        ins = self.h[q].dma_start(out=out, in_=in_, **kw)
        slot[2] += 16
        ins.then_inc(slot[1], 16)
        self.nins[q] += 1
        tok = (slot[0], slot[1], slot[2])
        for b in writes:
            if accum:
                b.wl[slot[0]] = tok
            else:
                b.w = tok
                b.r = {}
                b.wl = {}
        for b in reads:
            b.r[("d", slot[0])] = tok
        return tok

    def barrier(self):
        toks = []
        for e in self.ENG:
            k = self.psem[e]
            if k[2] > 0:
                toks.append((k[0], k[1], k[2]))
        for q in ("sp", "pool"):
            for slot in self.ring[q]:
                if slot[2] > 0:
                    toks.append((slot[0], slot[1], slot[2]))
        for e in self.ENG:
            for t in toks:
                if t[0] == self.psem[e][0]:
                    continue
                self._wait(e, t[0], t[1], t[2])


class Ctx:
    pass


def _mm_group(pe, out, pairs):
    n = len(pairs)
    ins = None
    for i, (l, r) in enumerate(pairs):
        ins = pe.matmul(out, l, r, start=(i == 0), stop=(i == n - 1))
    return ins


def build(cfg):
    nc = bass.Bass("TRN2", target_bir_lowering=False)
    S = Sched(nc)
    C = Ctx()
    C.nc, C.S, C.cfg = nc, S, cfg
    dbg = cfg.get("debug", False)

    def din(name, shape, dt=F32):
        return nc.dram_tensor(name, list(shape), dt, kind="ExternalInput").ap()

    def dout(name, shape, dt=F32):
        return nc.dram_tensor(name, list(shape), dt, kind="ExternalOutput").ap()

    def dscr(name, shape, dt=F32, out=False):
        return nc.dram_tensor(name, list(shape), dt, kind=("ExternalOutput" if (out and dbg) else "Internal")).ap()

    n_pre = cfg.get("n_pre", N_PRE)
    n_own = cfg.get("n_own", N_OWN)
    C.n_pre, C.n_own = n_pre, n_own

    I = {}
    I["x_full"] = din("x_full", [max(n_pre, 1) * NT, D])
    I["x_own"] = din("x_own", [n_own * NT, D])
    I["x_smp"] = din("x_smp", [NS, D])
    I["ident"] = din("ident", [128, 128])
    I["ffn1_w_in"] = din("ffn1_w_in", [2, D, 2 * FF])
    I["ffn1_w_out"] = din("ffn1_w_out", [2, FF, D])
    I["ln_g"] = din("ln_g", [2, 4, D])
    I["ln_b"] = din("ln_b", [2, 4, D])
    C.I = I

    def sb(name, shape, dt):
        return nc.alloc_sbuf_tensor("sb_" + name, shape, dt)
    C.sb = sb
    C.ident = sb("ident", [128, 128], F32)
    C.identB = Buf("ident")
    C.ones_bf = sb("ones_bf", [128, 128], BF16)
    C.onesB = Buf("ones")
    C.lng = sb("lng", [128, 8, KT], F32)
    C.lnb = sb("lnb", [128, 8, KT], F32)
    C.lnga = sb("lnga", [128, 8, KT], F32)
    C.lnba = sb("lnba", [128, 8, KT], F32)
    C.lnB = Buf("ln")
    C.ps = nc.alloc_psum_tensor("ps", [128, 8, 512], F32)
    C.PB = [Buf(f"ps{i}") for i in range(8)]

    S.dma("sp", out=C.ident[:], in_=I["ident"][:, :], writes=[C.identB])
    S.op("dve", lambda v: v.memset(C.ones_bf[:], 1.0), writes=[C.onesB])
    with nc.allow_non_contiguous_dma(reason="tiny ln param transposes"):
        S.dma("sp", out=C.lng[:], in_=I["ln_g"].rearrange("l s (kt p) -> p (l s) kt", p=128), writes=[C.lnB])
        S.dma("sp", out=C.lnb[:], in_=I["ln_b"].rearrange("l s (kt p) -> p (l s) kt", p=128), writes=[C.lnB])
    S.op("dve", lambda v: v.tensor_scalar(out=C.lnga[:], in0=C.lng[:], scalar1=ALPHA, scalar2=None, op0=ALU.mult),
         reads=[C.lnB], writes=[C.lnB])
    S.op("dve", lambda v: v.tensor_scalar(out=C.lnba[:], in0=C.lnb[:], scalar1=ALPHA, scalar2=None, op0=ALU.mult),
         reads=[C.lnB], writes=[C.lnB])

    C.psi = 0
    C.wsi = 0
    C.tmpi = 0
    C.toki = 0

    C.Ud_pre = dscr("Ud_pre", [max(n_pre, 1), 8, D, 64], BF16, out=True)
    C.Ud_own = dscr("Ud_own", [n_own, 8, D, 64], BF16, out=True)
    C.Ud_smp = dscr("Ud_smp", [8, D, 4], BF16, out=True)
    C.X1_own = dscr("X1_own", [n_own, 8, D, 64], F32, out=True)
    C.X1_smp = dscr("X1_smp", [8, D, 4], F32, out=True)
    C.UdB = Buf("Ud", multi=True)
    C.X1B = Buf("X1d", multi=True)

    C.ZdB = Buf("Zd", multi=True)
    C.outB = Buf("outs", multi=True)
    C.QdB, C.KdB, C.VdB, C.X2B, C.OdB = (Buf("Qd", multi=True), Buf("Kd", multi=True), Buf("Vd", multi=True),
                                            Buf("X2", multi=True), Buf("Od", multi=True))
    for nm, shp in [("s5_a_re", [128, 64]), ("s5_a_im", [128, 64]), ("s5_log_dt", [128]), ("s5_b_re", [128, 64, 16]),
                    ("s5_b_im", [128, 64, 16]), ("s5_c_re", [128, 16, 64]), ("s5_c_im", [128, 16, 64]), ("s5_d", [128, 16]),
                    ("hsel", [128, 8]), ("valid", [128, 1]), ("ident64", [128, 64]), ("mask_ts", [128, 128]),
                    ("st_re", [2, 128, 64]), ("st_im", [2, 128, 64]),
                    ("p_own", [2, n_own * NT, 256]), ("p_smp", [2, NS, 256]),
                    ("cache_k", [2, 512, D]), ("cache_v", [2, 512, D]),
                    ("ffn2_w_in", [2, D, 2 * FF]), ("ffn2_w_out", [2, FF, D]),
                    ("ple_w_proj", [2, 256, D]), ("ple_w_gate", [2, D, D]),
                    ("s5_w_glu", [D, 2 * D]), ("attn_w_qkv", [D, 3 * D]), ("attn_w_o", [D, D]), ("attn_rel_bias", [16, 257])]:
        I[nm] = din(nm, shp)
    O = {}
    O["y_p"] = dout("y_p", [(n_own - 1) * NT, D]); O["y_s"] = dout("y_s", [NS, D])
    O["s5_re_p"] = dout("s5_re_p", [128, 64]); O["s5_im_p"] = dout("s5_im_p", [128, 64])
    O["k_p"] = dout("k_p", [NT, D]); O["v_p"] = dout("v_p", [NT, D])
    O["s5_re_s"] = dout("s5_re_s", [2, 128, 64]); O["s5_im_s"] = dout("s5_im_s", [2, 128, 64])
    O["k_s"] = dout("k_s", [NS, D]); O["v_s"] = dout("v_s", [NS, D])
    C.O = O
    C.Zd_own = dscr("Zd_own", [n_own, 8, D, 64], BF16, out=True)
    C.Zd_smp = dscr("Zd_smp", [8, D, 4], BF16, out=True)
    C.X2_own = dscr("X2_own", [n_own, D, NT], F32, out=True)
    C.X2_smp = dscr("X2_smp", [D, NS], F32, out=True)
    C.Qd_own = dscr("Qd_own", [n_own - 1, D, NT], BF16, out=True)
    C.Qd_smp = dscr("Qd_smp", [D, NS], BF16, out=True)
    C.Kd_own = dscr("Kd_own", [n_own, D, NT], BF16, out=True)
    C.Kd_smp = dscr("Kd_smp", [D, NS], BF16, out=True)
    C.Vd_own = dscr("Vd_own", [n_own, NT, D], BF16, out=True)
    C.Vd_smp = dscr("Vd_smp", [NS, D], BF16, out=True)
    C.Od_own = dscr("Od_own", [n_own - 1, D, NT], BF16, out=True)
    C.Od_smp = dscr("Od_smp", [D, NS], BF16, out=True)
    from contextlib import ExitStack
    mode = cfg.get("mode", "full")
    if mode == "s5test":
        Ud_pre = din("Ud_pre_in", [max(n_pre, 1), 8, D, 64], BF16)
        Ud_own = din("Ud_own_in", [n_own, 8, D, 64], BF16)
        Ud_smp = din("Ud_smp_in", [8, D, 4], BF16)
        s5_phase(C, Ud_pre, Ud_own, Ud_smp, C.Zd_own, C.Zd_smp, O)
        C.nins = dict(S.nins)
        return nc, C
    smp = cfg.get("sample", True)
    stk = ExitStack()
    alloc_rowlocal(C, stk)
    setup_weights(C)
    emit_conv(C, C.conv_upfront)
    tiles = []
    for t in range(n_pre):
        tiles.append(("pre", t, NT))
    for t in range(n_own):
        tiles.append(("own", t, NT))
    if smp:
        tiles.append(("smp", 0, NS))
    for kind, t, N in tiles:
        if kind == "pre":
            src = I["x_full"][t * NT:(t + 1) * NT, :]
        elif kind == "own":
            src = I["x_own"][t * NT:(t + 1) * NT, :]
        else:
            src = I["x_smp"][:, :]
        load_xT(C, src, N)
        ffn(C, N, "f1i0", "f1o0")
        emit_conv(C, 8)
        layer_norm(C, N, 0, perm="deint")
        if kind == "pre":
            ud = C.Ud_pre[t]
        elif kind == "own":
            ud = C.Ud_own[t]
        else:
            ud = C.Ud_smp
        for kt in range(KT):
            S.dma("sp", out=ud[:, kt * 128:(kt + 1) * 128, :].rearrange("s p j -> p s j"),
                  in_=C.xb[:, kt, 0:N].rearrange("p (s j) -> p s j", s=8),
                  reads=[C.xbB[kt]], writes=[C.UdB])
        if kind != "pre":
            xd = C.X1_own[t] if kind == "own" else C.X1_smp
            for kt in range(KT):
                S.dma("sp", out=xd[:, kt * 128:(kt + 1) * 128, :].rearrange("s p j -> p s j"),
                      in_=C.X[:, kt, 0:N].rearrange("p (s j) -> p s j", s=8),
                      reads=[C.XB[kt]], writes=[C.X1B])
    emit_conv(C, 10 ** 6)
    S.barrier()
    stk.close()
    if mode == "p01":
        C.nins = dict(S.nins)
        return nc, C
    s5_phase(C, C.Ud_pre, C.Ud_own, C.Ud_smp if smp else None, C.Zd_own, C.Zd_smp, O)
    stk = ExitStack()
    alloc_rowlocal(C, stk)
    for t in range(n_own):
        phase3(C, "own", t, NT, O)
    if smp:
        phase3(C, "smp", 0, NS, O)
    S.barrier()
    stk.close()
    if mode == "p3":
        C.nins = dict(S.nins)
        return nc, C
    attention_phase(C, O)
    stk = ExitStack()
    alloc_rowlocal(C, stk)
    for t in range(1, n_own):
        phase4b(C, "own", t, NT, O)
    if smp:
        phase4b(C, "smp", 0, NS, O)
    S.barrier()
    stk.close()
    C.nins = dict(S.nins)
    return nc, C


def alloc_rowlocal(C, stk):
    nc = C.nc
    C.rlgen = getattr(C, "rlgen", 0) + 1
    gen = C.rlgen

    def sb(name, shape, dt):
        return stk.enter_context(nc.sbuf_tensor(f"rl{gen}_" + name, shape, dt))
    C.X = sb("X", [128, KT, NT], F32)
    C.XB = [Buf(f"X{k}") for k in range(KT)]
    C.xb = sb("xb", [128, KT, NT], BF16)
    C.xbB = [Buf(f"xb{k}") for k in range(KT)]
    C.gT = sb("gT", [128, FT, NT], BF16)
    C.gTB = [Buf(f"gT{k}") for k in range(FT)]
    C.tmp = [sb(f"tmp{i}", [128, NT], F32) for i in range(3)]
    C.tmpB = [Buf(f"tmp{i}") for i in range(3)]
    C.st = sb("st", [128, 6, NT], F32)
    C.stB = Buf("st")
    C.tok = [sb(f"tok{i}", [128, D], F32) for i in range(2)]
    C.tokB = [Buf(f"tok{i}", multi=True) for i in range(2)]
    C.pT = sb("pT", [128, 2, NT], BF16)
    C.pTB = Buf("pT")
    C.Wp = sb("Wp", [128, 2, D], BF16)
    C.WpB = Buf("Wp")
    NSLOT = 4
    C.wsl = [sb(f"wsl{i}", [128, 6144], BF16) for i in range(NSLOT)]
    C.wslB = [Buf(f"wsl{i}") for i in range(NSLOT)]


def next_tmp(C):
    i = C.tmpi
    C.tmpi = (i + 1) % len(C.tmp)
    return C.tmp[i], C.tmpB[i]


def next_ps(C, n=1):
    i = C.psi
    C.psi = (i + 1) % 8
    return i


def setup_weights(C):
    nc, I = C.nc, C.I
    C.W = {}
    C.convq = []

    def reg(name, ap2d, ktn, cw):
        nblk = ap2d.shape[1] // cw
        scr = nc.dram_tensor("wb_" + name, [nblk, 128, ktn * cw], BF16, kind="Internal").ap()
        C.W[name] = dict(src=ap2d, scr=scr, ktn=ktn, cw=cw, nblk=nblk, buf=Buf("wb_" + name, multi=True))
        for b in range(nblk):
            C.convq.append((name, b))
    reg("f1i0", I["ffn1_w_in"][0], KT, 256)
    reg("f1o0", I["ffn1_w_out"][0], FT, 128)
    C.conv_upfront = len(C.convq)
    reg("glu", I["s5_w_glu"], KT, 256)
    reg("f2i0", I["ffn2_w_in"][0], KT, 256)
    reg("f2o0", I["ffn2_w_out"][0], FT, 128)
    reg("pg0", I["ple_w_gate"][0], KT, 256)
    reg("f1i1", I["ffn1_w_in"][1], KT, 256)
    reg("f1o1", I["ffn1_w_out"][1], FT, 128)
    reg("qkv", I["attn_w_qkv"], KT, 256)
    reg("wo", I["attn_w_o"], KT, 256)
    reg("f2i1", I["ffn2_w_in"][1], KT, 256)
    reg("f2o1", I["ffn2_w_out"][1], FT, 128)
    reg("pg1", I["ple_w_gate"][1], KT, 256)
    C.convi = 0


def emit_conv(C, n):
    S = C.S
    while n > 0 and C.convi < len(C.convq):
        name, b = C.convq[C.convi]
        C.convi += 1
        n -= 1
        w = C.W[name]
        cw = w["cw"]
        S.dma("pool", out=w["scr"][b].rearrange("p (k c) -> p k c", c=cw),
              in_=w["src"][:, b * cw:(b + 1) * cw].rearrange("(k p) c -> p k c", p=128), writes=[w["buf"]])


def load_wb(C, name, blk):
    S = C.S
    w = C.W[name]
    ktn, cw = w["ktn"], w["cw"]
    i = C.wsi
    C.wsi = (i + 1) % len(C.wsl)
    S.dma("pool", out=C.wsl[i][:, 0:ktn * cw], in_=w["scr"][blk], reads=[w["buf"]], writes=[C.wslB[i]])
    return C.wsl[i][:, 0:ktn * cw].rearrange("p (k c) -> p k c", c=cw), C.wslB[i]


def load_w(C, src_ap, ktn, cw):
    S = C.S
    i = C.wsi
    C.wsi = (i + 1) % len(C.wsl)
    view = C.wsl[i][:, 0:ktn * cw].rearrange("p (k c) -> p k c", c=cw)
    S.dma("pool", out=view, in_=src_ap.rearrange("(k p) c -> p k c", p=128), writes=[C.wslB[i]])
    return view, C.wslB[i]


def load_xT(C, src, N):
    S = C.S
    nb = (N + 127) // 128
    for b in range(nb):
        rows = min(128, N - b * 128)
        ti = C.toki
        C.toki = (ti + 1) % 2
        tok, tokB = C.tok[ti], C.tokB[ti]
        S.dma("sp", out=tok[0:rows, :], in_=src[b * 128:b * 128 + rows, :], writes=[tokB])
        for k0 in range(0, KT, 4):
            bank = next_ps(C)

            def tr(pe, bank=bank, k0=k0, tok=tok, rows=rows):
                ins = None
                for j in range(4):
                    ins = pe.matmul(C.ps[:, bank, j * 128:j * 128 + rows],
                                    tok[0:rows, (k0 + j) * 128:(k0 + j + 1) * 128], C.ident[0:rows, 0:rows],
                                    start=True, stop=True)
                return ins
            S.op("pe", tr, reads=[tokB, C.identB], writes=[C.PB[bank]])
            src_ps = C.ps[:, bank, :].rearrange("p (j c) -> p j c", j=4)[:, :, 0:rows]
            S.op("act", lambda a, k0=k0, b=b, rows=rows, src_ps=src_ps: a.activation(
                out=C.X[:, k0:k0 + 4, b * 128:b * 128 + rows], in_=src_ps, func=AF.Copy, scale=ALPHA),
                writes=[C.PB[bank]] + [C.XB[k] for k in range(k0, k0 + 4)])
            S.op("dve", lambda v, k0=k0, b=b, rows=rows, src_ps=src_ps: v.tensor_copy(
                out=C.xb[:, k0:k0 + 4, b * 128:b * 128 + rows], in_=src_ps),
                writes=[C.PB[bank]] + [C.xbB[k] for k in range(k0, k0 + 4)])


def ffn(C, N, wi, wo_):
    S = C.S
    ps = C.ps
    for fp in range(FT // 2):
        wg, wgB = load_wb(C, wi, fp)
        wu, wuB = load_wb(C, wi, FT // 2 + fp)
        for j in range(2):
            f = fp * 2 + j
            bg = next_ps(C)
            bu = next_ps(C)
            S.op("pe", lambda pe, bg=bg, j=j, wg=wg: _mm_group(
                pe, ps[:, bg, 0:N], [(wg[:, kt, j * 128:(j + 1) * 128], C.xb[:, kt, 0:N]) for kt in range(KT)]),
                reads=[wgB] + C.xbB, writes=[C.PB[bg]])
            S.op("pe", lambda pe, bu=bu, j=j, wu=wu: _mm_group(
                pe, ps[:, bu, 0:N], [(wu[:, kt, j * 128:(j + 1) * 128], C.xb[:, kt, 0:N]) for kt in range(KT)]),
                reads=[wuB] + C.xbB, writes=[C.PB[bu]])
            tmp, tmpB = next_tmp(C)
            S.op("act", lambda a, bg=bg, tmp=tmp: a.activation(out=tmp[:, 0:N], in_=ps[:, bg, 0:N], func=AF.Silu),
                 writes=[C.PB[bg], tmpB])
            S.op("dve", lambda v, bu=bu, tmp=tmp, f=f: v.tensor_tensor(
                out=C.gT[:, f, 0:N], in0=ps[:, bu, 0:N], in1=tmp[:, 0:N], op=ALU.mult),
                reads=[tmpB], writes=[C.PB[bu], C.gTB[f]])
    for m in range(KT):
        wo, woB = load_wb(C, wo_, m)
        bo = next_ps(C)
        S.op("pe", lambda pe, bo=bo, wo=wo: _mm_group(
            pe, ps[:, bo, 0:N], [(wo[:, f, :], C.gT[:, f, 0:N]) for f in range(FT)]),
            reads=[woB] + C.gTB, writes=[C.PB[bo]])
        S.op("dve", lambda v, bo=bo, m=m: v.scalar_tensor_tensor(
            out=C.X[:, m, 0:N], in0=ps[:, bo, 0:N], scalar=0.5, in1=C.X[:, m, 0:N], op0=ALU.mult, op1=ALU.add),
            writes=[C.PB[bo], C.XB[m]])


def layer_norm(C, N, idx, perm=None, final=False):
    S = C.S
    ps = C.ps
    rb = C.gT[:, 0:KT, :]
    rsq = C.gT[:, KT:2 * KT, :]
    for kt in range(KT):
        S.op("dve", lambda v, kt=kt: v.tensor_copy(out=rb[:, kt, 0:N], in_=C.X[:, kt, 0:N]),
             reads=[C.XB[kt]], writes=[C.gTB[kt]])
        S.op("act", lambda a, kt=kt: a.activation(out=rsq[:, kt, 0:N], in_=C.X[:, kt, 0:N], func=AF.Square),
             reads=[C.XB[kt]], writes=[C.gTB[KT + kt]])
    b1 = next_ps(C)
    b2 = next_ps(C)
    S.op("pe", lambda pe: _mm_group(pe, ps[:, b1, 0:N], [(C.ones_bf[:], rb[:, kt, 0:N]) for kt in range(KT)]),
         reads=[C.onesB] + C.gTB[0:KT], writes=[C.PB[b1]])
    S.op("pe", lambda pe: _mm_group(pe, ps[:, b2, 0:N], [(C.ones_bf[:], rsq[:, kt, 0:N]) for kt in range(KT)]),
         reads=[C.onesB] + C.gTB[KT:2 * KT], writes=[C.PB[b2]])
    mu = C.st[:, 0, 0:N]
    ex2 = C.st[:, 1, 0:N]
    var = C.st[:, 2, 0:N]
    sd = C.st[:, 3, 0:N]
    rstd = C.st[:, 4, 0:N]
    S.op("act", lambda a: a.activation(out=mu, in_=ps[:, b1, 0:N], func=AF.Copy, scale=1.0 / D),
         writes=[C.PB[b1], C.stB])
    S.op("act", lambda a: a.activation(out=ex2, in_=ps[:, b2, 0:N], func=AF.Copy, scale=1.0 / D),
         writes=[C.PB[b2], C.stB])
    S.op("dve", lambda v: v.tensor_tensor(out=var, in0=mu, in1=mu, op=ALU.mult), reads=[C.stB], writes=[C.stB])
    S.op("dve", lambda v: v.tensor_tensor(out=var, in0=ex2, in1=var, op=ALU.subtract), reads=[C.stB], writes=[C.stB])
    S.op("dve", lambda v: v.tensor_scalar(out=var, in0=var, scalar1=LN_EPS, scalar2=None, op0=ALU.add),
         reads=[C.stB], writes=[C.stB])
    S.op("act", lambda a: a.activation(out=sd, in_=var, func=AF.Sqrt), reads=[C.stB], writes=[C.stB])
    S.op("dve", lambda v: v.reciprocal(out=rstd, in_=sd), reads=[C.stB], writes=[C.stB])

    def pv(ap):
        if perm is None:
            return ap
        if perm == "deint":
            return ap.rearrange("p (s j) -> p j s", s=8)
        return ap.rearrange("p (j s) -> p s j", s=8)

    def pin(ap):
        if perm is None:
            return ap
        if perm == "deint":
            return ap.rearrange("p (j s) -> p j s", s=8)
        return ap.rearrange("p (s j) -> p s j", s=8)

    for kt in range(KT):
        tmp, tmpB = next_tmp(C)
        S.op("dve", lambda v, kt=kt, tmp=tmp: v.tensor_tensor(out=tmp[:, 0:N], in0=C.X[:, kt, 0:N], in1=mu, op=ALU.subtract),
             reads=[C.XB[kt], C.stB], writes=[tmpB])
        S.op("dve", lambda v, tmp=tmp: v.tensor_tensor(out=tmp[:, 0:N], in0=tmp[:, 0:N], in1=rstd, op=ALU.mult),
             reads=[tmpB, C.stB], writes=[tmpB])
        S.op("act", lambda a, kt=kt, tmp=tmp: a.activation(
            out=pv(C.X[:, kt, 0:N]), in_=pin(tmp[:, 0:N]), func=AF.Identity,
            scale=(C.lng if final else C.lnga)[:, idx, kt:kt + 1], bias=(C.lnb if final else C.lnba)[:, idx, kt:kt + 1]),
            reads=[tmpB, C.lnB], writes=[C.XB[kt]])
        S.op("act", lambda a, kt=kt, tmp=tmp: a.activation(
            out=pv(C.xb[:, kt, 0:N]), in_=pin(tmp[:, 0:N]), func=AF.Identity,
            scale=C.lng[:, idx, kt:kt + 1], bias=C.lnb[:, idx, kt:kt + 1]),
            reads=[tmpB, C.lnB], writes=[C.xbB[kt]])


def host_consts():
    mask = np.zeros((128, 128), np.float32)
    for s_ in range(8):
        for t_ in range(8):
            if t_ >= s_:
                mask[s_ * 16:(s_ + 1) * 16, t_ * 16:(t_ + 1) * 16] = 1.0
    ident64 = np.zeros((128, 64), np.float32)
    ident64[np.arange(128), np.arange(128) % 64] = 1.0
    return {"ident": np.eye(128, dtype=np.float32), "ident64": ident64, "mask_ts": mask}


WEIGHT_KEYS = ["ffn1_w_in", "ffn1_w_out", "ffn2_w_in", "ffn2_w_out", "ln_g", "ln_b", "ple_w_proj", "ple_w_gate",
               "s5_a_re", "s5_a_im", "s5_log_dt", "s5_b_re", "s5_b_im", "s5_c_re", "s5_c_im", "s5_d", "s5_w_glu",
               "attn_w_qkv", "attn_w_o", "attn_rel_bias"]


def make_core_inputs(inputs, c, n_pre, n_own, own_start, consts):
    f32 = np.float32
    xp = np.asarray(inputs["x_prompt"])[0]
    pp = np.asarray(inputs["p_prompt"])[:, 0]
    lo = own_start - NT
    hi = own_start + (n_own - 1) * NT
    x_own = np.zeros((n_own * NT, D), f32)
    p_own = np.zeros((2, n_own * NT, 256), f32)
    a = max(lo, 0)
    x_own[a - lo:] = xp[a:hi]
    p_own[:, a - lo:] = pp[:, a:hi]
    m = dict(consts)
    m["x_full"] = np.ascontiguousarray(xp[0:max(n_pre, 1) * NT])
    m["x_own"] = x_own
    m["p_own"] = p_own
    m["x_smp"] = np.ascontiguousarray(np.asarray(inputs["x_sample"])[2 * c:2 * c + 2]).reshape(NS, D)
    m["p_smp"] = np.ascontiguousarray(np.asarray(inputs["p_sample"])[:, 2 * c:2 * c + 2]).reshape(2, NS, 256)
    m["st_re"] = np.ascontiguousarray(np.asarray(inputs["state_s5_re"])[2 * c:2 * c + 2])
    m["st_im"] = np.ascontiguousarray(np.asarray(inputs["state_s5_im"])[2 * c:2 * c + 2])
    m["cache_k"] = np.ascontiguousarray(np.asarray(inputs["cache_k"])[2 * c:2 * c + 2]).reshape(2, 512, D)
    m["cache_v"] = np.ascontiguousarray(np.asarray(inputs["cache_v"])[2 * c:2 * c + 2]).reshape(2, 512, D)
    hsel = np.zeros((128, 8), f32)
    if own_start - NT > 0:
        hsel[:, (own_start - NT + NT) // (4 * NT)] = 1.0
    m["hsel"] = hsel
    m["valid"] = np.full((128, 1), 1.0 if own_start > 0 else 0.0, f32)
    for k in WEIGHT_KEYS:
        m[k] = np.asarray(inputs[k])
    return m


_NC_CACHE = {}


def kernel(**inputs):
    n_pre, n_own = N_PRE, N_OWN
    key = (n_pre, n_own)
    if key not in _NC_CACHE:
        _NC_CACHE[key] = build(dict(n_pre=n_pre, n_own=n_own))
    nc, C = _NC_CACHE[key]
    consts = host_consts()
    in_maps = [make_core_inputs(inputs, c, n_pre, n_own, 2048 * c, consts) for c in range(NCORES)]
    res = run_bass_kernel_spmd(nc, in_maps, core_ids=list(range(NCORES)))
    R = res.results
    f32 = np.float32
    y_p = np.concatenate([np.asarray(R[c]["y_p"], f32) for c in range(NCORES)], axis=0)[None]
    y_s = np.concatenate([np.asarray(R[c]["y_s"], f32).reshape(2, 16, D) for c in range(NCORES)], axis=0)
    last = NCORES - 1
    s5_re_p = np.asarray(R[last]["s5_re_p"], f32)[None]
    s5_im_p = np.asarray(R[last]["s5_im_p"], f32)[None]
    k_p = np.asarray(R[last]["k_p"], f32).reshape(1, NT, 16, 128)
    v_p = np.asarray(R[last]["v_p"], f32).reshape(1, NT, 16, 128)
    s5_re_s = np.concatenate([np.asarray(R[c]["s5_re_s"], f32) for c in range(NCORES)], axis=0)
    s5_im_s = np.concatenate([np.asarray(R[c]["s5_im_s"], f32) for c in range(NCORES)], axis=0)
    k_s = np.concatenate([np.asarray(R[c]["k_s"], f32).reshape(2, 16, 16, 128) for c in range(NCORES)], axis=0)
    v_s = np.concatenate([np.asarray(R[c]["v_s"], f32).reshape(2, 16, 16, 128) for c in range(NCORES)], axis=0)
    return (y_p, y_s, s5_re_p, s5_im_p, k_p, v_p, s5_re_s, s5_im_s, k_s, v_s)


GELU_C = 1.5957691216057308


def s5_phase(C, Ud_pre, Ud_own, Ud_smp, Zd_own, Zd_smp, outs):
    from contextlib import ExitStack
    nc, S, I = C.nc, C.S, C.I
    ps = C.ps
    n_pre, n_own = C.n_pre, C.n_own
    stk = ExitStack()

    def sb(name, shape, dt):
        return stk.enter_context(nc.sbuf_tensor("s5_" + name, shape, dt))

    WTs = sb("WTs", [128, 128, 2, 64], BF16)
    VP = sb("VP", [128, 2, 64, 128], BF16)
    Toep = sb("Toep", [128, 128, 128], BF16)
    A12 = sb("A12", [128, 2, 2, 64], F32)
    Hsel = sb("Hsel", [128, 2, 64], F32)
    Hcar = sb("Hcar", [128, 2, 64], F32)
    hselv = sb("hselv", [128, 8], F32)
    validv = sb("validv", [128, 1], F32)
    identb = sb("identb", [128, 64], BF16)
    mask4 = sb("mask4", [128, 4, 128], F32)
    dcol = sb("dcol", [128, 128], F32)
    tabB = Buf("tab")
    cB = Buf("s5consts", multi=True)

    S.dma("sp", out=hselv[:], in_=I["hsel"][:, :], writes=[cB])
    S.dma("sp", out=validv[:], in_=I["valid"][:, :], writes=[cB])
    S.dma("pool", out=identb[:], in_=I["ident64"][:, :], writes=[cB])
    for i in range(4):
        S.dma("sp", out=mask4[:, i, :], in_=I["mask_ts"][:, :], writes=[cB])
    with nc.allow_non_contiguous_dma(reason="tiny s5 param layout"):
        for s in range(8):
            S.dma("sp", out=dcol[16 * s:16 * s + 16, :], in_=I["s5_d"].rearrange("g c -> c g"), writes=[cB])

    with ExitStack() as bstk:
        def tb(name, shape, dt=F32):
            return bstk.enter_context(nc.sbuf_tensor("s5b_" + name, shape, dt))
        ar = tb("ar", [128, 64]); ai = tb("ai", [128, 64]); ldt = tb("ldt", [128, 64])
        br = tb("br", [128, 64, 16]); bi = tb("bi", [128, 64, 16])
        cr = tb("cr", [128, 64, 16]); ci = tb("ci", [128, 64, 16])
        Bcr = tb("Bcr", [128, 64, 16]); Bci = tb("Bci", [128, 64, 16])
        t1 = tb("t1", [128, 64, 16]); t2 = tb("t2", [128, 64, 16])
        PW = tb("PW", [128, 9, 2, 64]); NG = tb("NG", [128, 9, 2, 64])
        sc = [tb(f"sc{i}", [128, 64]) for i in range(8)]
        WS = tb("WS", [128, 2, 64, 128], BF16)
        bB = Buf("s5build", multi=True)
        with nc.allow_non_contiguous_dma(reason="tiny s5 param layout"):
            for h in range(2):
                hs = slice(h * 64, (h + 1) * 64)
                S.dma("sp", out=ar[hs, :], in_=I["s5_a_re"][hs, :].rearrange("g p -> p g"), writes=[bB])
                S.dma("sp", out=ai[hs, :], in_=I["s5_a_im"][hs, :].rearrange("g p -> p g"), writes=[bB])
                S.dma("sp", out=ldt[hs, :], in_=bass.AP(I["s5_log_dt"].tensor, h * 64, [[0, 64], [1, 64]]), writes=[bB])
                S.dma("sp", out=br[hs, :, :], in_=I["s5_b_re"][hs].rearrange("g p c -> p g c"), writes=[bB])
                S.dma("sp", out=bi[hs, :, :], in_=I["s5_b_im"][hs].rearrange("g p c -> p g c"), writes=[bB])
                for c in range(16):
                    S.dma("sp", out=cr[hs, :, c], in_=I["s5_c_re"][hs, c, :].rearrange("g p -> p g"), writes=[bB])
                    S.dma("sp", out=ci[hs, :, c], in_=I["s5_c_im"][hs, c, :].rearrange("g p -> p g"), writes=[bB])

        if C.cfg.get("s5_stop", 99) <= 1:
            S.barrier(); return
        def V(fn):
            S.op("dve", fn, reads=[bB, cB], writes=[bB])

        def A(fn):
            S.op("act", fn, reads=[bB, cB], writes=[bB])
        dt_, ardt, th, mag, sn, cs, den, nr = sc
        A(lambda a: a.activation(out=dt_[:], in_=ldt[:], func=AF.Exp))
        V(lambda v: v.tensor_tensor(out=ardt[:], in0=ar[:], in1=dt_[:], op=ALU.mult))
        V(lambda v: v.tensor_tensor(out=th[:], in0=ai[:], in1=dt_[:], op=ALU.mult))
        A(lambda a: a.activation(out=mag[:], in_=ardt[:], func=AF.Exp))
        MAGIC = 12582912.0

        def range_reduce(dst, src, shift):
            V(lambda v: v.tensor_scalar(out=den[:], in0=src, scalar1=shift, scalar2=1.0 / (2 * math.pi), op0=ALU.add, op1=ALU.mult))
            V(lambda v: v.tensor_scalar(out=den[:], in0=den[:], scalar1=MAGIC, scalar2=None, op0=ALU.add))
            V(lambda v: v.tensor_scalar(out=den[:], in0=den[:], scalar1=-MAGIC, scalar2=None, op0=ALU.add))
            V(lambda v: v.scalar_tensor_tensor(out=dst, in0=den[:], scalar=-2 * math.pi, in1=src, op0=ALU.mult, op1=ALU.add))
            if shift != 0.0:
                V(lambda v: v.tensor_scalar(out=dst, in0=dst, scalar1=shift, scalar2=None, op0=ALU.add))
        range_reduce(sn[:], th[:], 0.0)
        A(lambda a: a.activation(out=sn[:], in_=sn[:], func=AF.Sin))
        range_reduce(cs[:], th[:], 0.5 * math.pi)
        A(lambda a: a.activation(out=cs[:], in_=cs[:], func=AF.Sin))
        lr = PW[:, 1, 0, :]
        li = PW[:, 1, 1, :]
        V(lambda v: v.memset(PW[:, 0, 0, :], 1.0))
        V(lambda v: v.memset(PW[:, 0, 1, :], 0.0))
        V(lambda v: v.tensor_tensor(out=lr, in0=mag[:], in1=cs[:], op=ALU.mult))
        V(lambda v: v.tensor_tensor(out=li, in0=mag[:], in1=sn[:], op=ALU.mult))
        V(lambda v: v.tensor_tensor(out=den[:], in0=ar[:], in1=ar[:], op=ALU.mult))
        V(lambda v: v.tensor_tensor(out=dt_[:], in0=ai[:], in1=ai[:], op=ALU.mult))
        V(lambda v: v.tensor_tensor(out=den[:], in0=den[:], in1=dt_[:], op=ALU.add))
        V(lambda v: v.reciprocal(out=den[:], in_=den[:]))
        V(lambda v: v.tensor_scalar(out=nr[:], in0=lr, scalar1=-1.0, scalar2=None, op0=ALU.add))
        cfr, cfi = ardt, th
        V(lambda v: v.tensor_tensor(out=cfr[:], in0=nr[:], in1=ar[:], op=ALU.mult))
        V(lambda v: v.tensor_tensor(out=dt_[:], in0=li, in1=ai[:], op=ALU.mult))
        V(lambda v: v.tensor_tensor(out=cfr[:], in0=cfr[:], in1=dt_[:], op=ALU.add))
        V(lambda v: v.tensor_tensor(out=cfr[:], in0=cfr[:], in1=den[:], op=ALU.mult))
        V(lambda v: v.tensor_tensor(out=cfi[:], in0=li, in1=ar[:], op=ALU.mult))
        V(lambda v: v.tensor_tensor(out=dt_[:], in0=nr[:], in1=ai[:], op=ALU.mult))
        V(lambda v: v.tensor_tensor(out=cfi[:], in0=cfi[:], in1=dt_[:], op=ALU.subtract))
        V(lambda v: v.tensor_tensor(out=cfi[:], in0=cfi[:], in1=den[:], op=ALU.mult))

        def bc(ap2):
            return ap2.unsqueeze(2).broadcast_to([128, 64, 16])

        def cmul(out_r, out_i, xr, xi, yr, yi, neg_i=False):
            V(lambda v: v.tensor_tensor(out=t1[:], in0=xr, in1=yr, op=ALU.mult))
            V(lambda v: v.tensor_tensor(out=t2[:], in0=xi, in1=yi, op=ALU.mult))
            V(lambda v: v.tensor_tensor(out=out_r, in0=t1[:], in1=t2[:], op=ALU.subtract))
            V(lambda v: v.tensor_tensor(out=t1[:], in0=xr, in1=yi, op=ALU.mult))
            V(lambda v: v.tensor_tensor(out=t2[:], in0=xi, in1=yr, op=ALU.mult))
            if neg_i:
                V(lambda v: v.scalar_tensor_tensor(out=out_i, in0=t1[:], scalar=-1.0, in1=t2[:], op0=ALU.mult, op1=ALU.subtract))
            else:
                V(lambda v: v.tensor_tensor(out=out_i, in0=t1[:], in1=t2[:], op=ALU.add))
        cmul(Bcr[:], Bci[:], bc(cfr[:]), bc(cfi[:]), br[:], bi[:])
        for k in range(1, 8):
            pr, pi_ = PW[:, k, 0, :], PW[:, k, 1, :]
            qr, qi = PW[:, k + 1, 0, :], PW[:, k + 1, 1, :]
            V(lambda v, pr=pr: v.tensor_tensor(out=sn[:], in0=pr, in1=lr, op=ALU.mult))
            V(lambda v, pi_=pi_: v.tensor_tensor(out=cs[:], in0=pi_, in1=li, op=ALU.mult))
            V(lambda v, qr=qr: v.tensor_tensor(out=qr, in0=sn[:], in1=cs[:], op=ALU.subtract))
            V(lambda v, pr=pr: v.tensor_tensor(out=sn[:], in0=pr, in1=li, op=ALU.mult))
            V(lambda v, pi_=pi_: v.tensor_tensor(out=cs[:], in0=pi_, in1=lr, op=ALU.mult))
            V(lambda v, qi=qi: v.tensor_tensor(out=qi, in0=sn[:], in1=cs[:], op=ALU.add))
        for k in range(1, 9):
            pr, pi_ = PW[:, k, 0, :], PW[:, k, 1, :]
            V(lambda v, pr=pr: v.tensor_tensor(out=sn[:], in0=pr, in1=pr, op=ALU.mult))
            V(lambda v, pi_=pi_: v.tensor_tensor(out=cs[:], in0=pi_, in1=pi_, op=ALU.mult))
            V(lambda v: v.tensor_tensor(out=sn[:], in0=sn[:], in1=cs[:], op=ALU.add))
            V(lambda v: v.reciprocal(out=sn[:], in_=sn[:]))
            V(lambda v, pr=pr, k=k: v.tensor_tensor(out=NG[:, k, 0, :], in0=pr, in1=sn[:], op=ALU.mult))
            V(lambda v, pi_=pi_, k=k: v.scalar_tensor_tensor(out=NG[:, k, 1, :], in0=pi_, scalar=-1.0, in1=sn[:], op0=ALU.mult, op1=ALU.mult))
        V(lambda v: v.tensor_copy(out=A12[:, 0, 0, :], in_=PW[:, 8, 0, :]))
        V(lambda v: v.tensor_copy(out=A12[:, 0, 1, :], in_=PW[:, 8, 0, :]))
        V(lambda v: v.tensor_scalar(out=A12[:, 1, 0, :], in0=PW[:, 8, 1, :], scalar1=-1.0, scalar2=None, op0=ALU.mult))
        V(lambda v: v.tensor_copy(out=A12[:, 1, 1, :], in_=PW[:, 8, 1, :]))
        VPv = VP[:].rearrange("q r g (t c) -> q r g t c", c=16)
        for t in range(8):
            cmul(VPv[:, 0, :, t, :], VPv[:, 1, :, t, :], cr[:], ci[:], bc(PW[:, t + 1, 0, :]), bc(PW[:, t + 1, 1, :]), neg_i=True)
        WSv = WS[:].rearrange("q r g (s c) -> q r g s c", c=16)
        for s in range(8):
            cmul(WSv[:, 0, :, s, :], WSv[:, 1, :, s, :], Bcr[:], Bci[:], bc(PW[:, 7 - s, 0, :]), bc(PW[:, 7 - s, 1, :]))
        if C.cfg.get("s5_stop", 99) <= 2:
            S.barrier(); return
        for g0 in range(0, 128, 4):
            bank = next_ps(C)

            def trs(pe, g0=g0, bank=bank):
                ins = None
                for gi in range(4):
                    g = g0 + gi
                    h, gp = g // 64, g % 64
                    for ri in range(2):
                        ins = pe.matmul(ps[:, bank, (gi * 2 + ri) * 64:(gi * 2 + ri + 1) * 64],
                                        WS[h * 64:(h + 1) * 64, ri, gp, :], identb[h * 64:(h + 1) * 64, :],
                                        start=True, stop=True)
                return ins
            S.op("pe", trs, reads=[bB, cB], writes=[C.PB[bank]])
            S.op("act", lambda a, g0=g0, bank=bank: a.activation(
                out=WTs[:, g0:g0 + 4, :, :], in_=ps[:, bank, :].rearrange("p (g r c) -> p g r c", g=4, r=2), func=AF.Copy),
                writes=[C.PB[bank], tabB])
        if C.cfg.get("s5_stop", 99) <= 3:
            S.barrier(); return
        for s in range(8):
            cmul(WSv[:, 0, :, s, :], WSv[:, 1, :, s, :], Bcr[:], Bci[:], bc(NG[:, s + 1, 0, :]), bc(NG[:, s + 1, 1, :]))
        for g0 in range(0, 128, 4):
            bank = next_ps(C)

            def gm(pe, g0=g0, bank=bank):
                ins = None
                for gi in range(4):
                    g = g0 + gi
                    h, gp = g // 64, g % 64
                    hs = slice(h * 64, (h + 1) * 64)
                    for ri in range(2):
                        ins = pe.matmul(ps[:, bank, gi * 128:(gi + 1) * 128], WS[hs, ri, gp, :], VP[hs, ri, gp, :],
                                        start=(ri == 0), stop=(ri == 1))
                return ins
            S.op("pe", gm, reads=[bB, cB], writes=[C.PB[bank]])
            tmpt = t1[:].rearrange("p a b -> p (a b)")[:, 0:512]
            S.op("dve", lambda v, bank=bank: v.tensor_tensor(out=tmpt, in0=ps[:, bank, :],
                                                              in1=mask4[:].rearrange("p a b -> p (a b)"), op=ALU.mult),
                 reads=[cB], writes=[C.PB[bank], bB])
            for gi in range(4):
                S.op("dve", lambda v, g=g0 + gi, gi=gi: v.scalar_tensor_tensor(
                    out=Toep[:, g, :], in0=C.ident[:, :], scalar=dcol[:, g:g + 1], in1=tmpt[:, gi * 128:(gi + 1) * 128],
                    op0=ALU.mult, op1=ALU.add), reads=[bB, cB, C.identB], writes=[tabB])
        S.barrier()
    if C.cfg.get("s5_stop", 99) <= 4:
        S.barrier(); return
    JB = 32
    U = sb("U", [128, 128, JB], BF16)
    UB = Buf("U", multi=True)
    Sx = sb("Sx", [128, 2, 64, JB], F32)
    SxB = Buf("Sx")
    Hb = sb("Hb", [128, 2, 64, JB], BF16)
    HbB = Buf("Hb")
    zst = sb("zst", [128, 128, JB], BF16)
    zstB = Buf("zst")
    zsts = sb("zsts", [128, 128, 4], BF16)
    Us = sb("Us", [128, 128, 4], BF16)
    ta = sb("ta", [128, 2, 64], F32)
    tb_ = sb("tb", [128, 2, 64], F32)
    gtmp = [sb(f"gt{i}", [128, 512], F32) for i in range(2)]
    gtmpB = [Buf(f"gt{i}") for i in range(2)]
    HB = Buf("H")
    S.op("dve", lambda v: v.memset(Hcar[:], 0.0), writes=[HB])
    S.op("dve", lambda v: v.memset(Hsel[:], 0.0), writes=[HB])

    def load_U(ud, j0, jn, Jtot):
        Ut = U if jn == JB else Us
        for s in range(8):
            S.dma("sp", out=Ut[16 * s:16 * s + 16, :, 0:jn],
                  in_=ud[s, :, j0:j0 + jn].rearrange("(g c) j -> c g j", c=16), writes=[UB])

    def compute_S(jn):
        Ut = U if jn == JB else Us
        for gb in range(8):
            bank = next_ps(C)
            pv = ps[:, bank, 0:2 * 8 * jn].rearrange("p (r g j) -> p r g j", r=2, g=8)

            def mm(pe, gb=gb, pv=pv):
                ins = None
                for h in range(2):
                    for gi in range(8):
                        gp = gb * 8 + gi
                        g = h * 64 + gp
                        for ri in range(2):
                            ins = pe.matmul(pv[h * 64:(h + 1) * 64, ri, gi, :], WTs[:, g, ri, :], Ut[:, g, 0:jn],
                                            start=True, stop=True, tile_position=(0, h * 64))
                return ins
            S.op("pe", mm, reads=[tabB, UB], writes=[C.PB[bank]])
            S.op("act", lambda a, gb=gb, pv=pv: a.activation(out=Sx[:, :, gb * 8:(gb + 1) * 8, 0:jn], in_=pv, func=AF.Copy),
                 writes=[C.PB[bank], SxB])

    def step(hprev, sx_j):
        S.op("dve", lambda v: v.tensor_tensor(out=ta[:], in0=A12[:, 0, :, :], in1=hprev, op=ALU.mult), reads=[HB, SxB, tabB], writes=[HB])
        S.op("dve", lambda v: v.tensor_tensor(out=tb_[:], in0=A12[:, 1, :, :], in1=hprev[:, ::-1, :], op=ALU.mult), reads=[HB, SxB, tabB], writes=[HB])
        S.op("dve", lambda v: v.tensor_tensor(out=ta[:], in0=ta[:], in1=tb_[:], op=ALU.add), reads=[HB], writes=[HB])
        S.op("dve", lambda v: v.tensor_tensor(out=sx_j, in0=ta[:], in1=sx_j, op=ALU.add), reads=[HB], writes=[SxB])

    def scan(jn, want_hb):
        for j in range(jn):
            hprev = Hcar[:] if j == 0 else Sx[:, :, :, j - 1]
            if want_hb:
                S.op("act", lambda a, hprev=hprev, j=j: a.activation(out=Hb[:, :, :, j], in_=hprev, func=AF.Copy),
                     reads=[HB, SxB], writes=[HbB])
            step(hprev, Sx[:, :, :, j])
        S.op("dve", lambda v: v.tensor_copy(out=Hcar[:], in_=Sx[:, :, :, jn - 1]), reads=[SxB], writes=[HB])

    def compute_Y(jn, zd, j0):
        zt = zst if jn == JB else zsts
        for gb in range(8):
            bank = next_ps(C)
            pv = ps[:, bank, 0:16 * jn].rearrange("p (g j) -> p g j", g=16)

            def mm(pe, gb=gb, pv=pv):
                ins = None
                for gi in range(16):
                    g = gb * 16 + gi
                    h, gp = g // 64, g % 64
                    hs = slice(h * 64, (h + 1) * 64)
                    pe.matmul(pv[:, gi, :], Toep[:, g, :], (U if jn == JB else Us)[:, g, 0:jn], start=True, stop=False)
                    pe.matmul(pv[:, gi, :], VP[hs, 0, gp, :], Hb[hs, 0, gp, 0:jn], start=False, stop=False)
                    ins = pe.matmul(pv[:, gi, :], VP[hs, 1, gp, :], Hb[hs, 1, gp, 0:jn], start=False, stop=True)
                return ins
            S.op("pe", mm, reads=[tabB, UB, HbB], writes=[C.PB[bank]])
            if C.cfg.get("y_stop", 9) <= 1 and jn == 4:
                continue
            n = 16 * jn
            y = ps[:, bank, 0:n]
            g1, g1B = gtmp[0], gtmpB[0]
            g2, g2B = gtmp[1], gtmpB[1]
            S.op("act", lambda a, y=y: a.activation(out=g1[:, 0:n], in_=y, func=AF.Square), writes=[C.PB[bank], g1B])
            S.op("dve", lambda v: v.tensor_scalar(out=g1[:, 0:n], in0=g1[:, 0:n], scalar1=0.044715, scalar2=1.0,
                                                  op0=ALU.mult, op1=ALU.add), writes=[g1B])
            S.op("dve", lambda v, y=y: v.tensor_tensor(out=g1[:, 0:n], in0=g1[:, 0:n], in1=y, op=ALU.mult),
                 writes=[C.PB[bank], g1B])
            S.op("act", lambda a: a.activation(out=g2[:, 0:n], in_=g1[:, 0:n], func=AF.Sigmoid, scale=GELU_C),
                 reads=[g1B], writes=[g2B])
            S.op("dve", lambda v, y=y, gb=gb: v.tensor_tensor(
                out=zt[:, gb * 16:(gb + 1) * 16, 0:jn], in0=g2[:, 0:n].rearrange("p (g j) -> p g j", g=16),
                in1=y.rearrange("p (g j) -> p g j", g=16), op=ALU.mult),
                reads=[g2B], writes=[C.PB[bank], zstB])
        for t in range(8 if not (C.cfg.get("y_stop", 9) <= 2 and jn == 4) else 0):
            S.dma("sp", out=zd[t, :, j0:j0 + jn].rearrange("(g c) j -> c g j", c=16),
                  in_=zt[16 * t:16 * t + 16, :, 0:jn], reads=[zstB], writes=[C.ZdB])

    for t in range(n_pre):
        for hj in range(2):
            load_U(Ud_pre[t], hj * JB, JB, 64)
            compute_S(JB)
            scan(JB, False)
        if (t + 2) % 4 == 0:
            c = (t + 2) // 4
            S.op("dve", lambda v, c=c: v.scalar_tensor_tensor(out=Hsel[:], in0=Hcar[:], scalar=hselv[:, c:c + 1], in1=Hsel[:],
                                                              op0=ALU.mult, op1=ALU.add), reads=[HB, cB], writes=[HB])
    if C.cfg.get("s5_stop", 99) <= 5:
        S.barrier(); return
    S.op("dve", lambda v: v.tensor_copy(out=Hcar[:], in_=Hsel[:]), reads=[HB], writes=[HB])
    for t in range(n_own):
        for hj in range(2):
            load_U(Ud_own[t], hj * JB, JB, 64)
            compute_S(JB)
            scan(JB, True)
            compute_Y(JB, Zd_own[t], hj * JB)
        if t == 0:
            S.op("dve", lambda v: v.tensor_scalar(out=Hcar[:], in0=Hcar[:], scalar1=validv[:, 0:1], scalar2=None, op0=ALU.mult),
                 reads=[HB, cB], writes=[HB])
    if C.cfg.get("s5_stop", 99) <= 6:
        S.barrier(); return
    with nc.allow_non_contiguous_dma(reason="state layout"):
        for h in range(2):
            hs = slice(h * 64, (h + 1) * 64)
            S.dma("sp", out=outs["s5_re_p"][hs, :].rearrange("g p -> p g"), in_=Hcar[hs, 0, :], reads=[HB], writes=[C.outB])
            S.dma("sp", out=outs["s5_im_p"][hs, :].rearrange("g p -> p g"), in_=Hcar[hs, 1, :], reads=[HB], writes=[C.outB])
    if C.cfg.get("s5_stop", 99) <= 7:
        S.barrier(); return
    if Ud_smp is not None:
        load_U(Ud_smp, 0, 4, 4)
        compute_S(4)
        for q in range(2):
            with nc.allow_non_contiguous_dma(reason="state layout"):
                for h in range(2):
                    hs = slice(h * 64, (h + 1) * 64)
                    S.dma("sp", out=Hcar[hs, 0, :], in_=I["st_re"][q, hs, :].rearrange("g p -> p g"), writes=[HB])
                    S.dma("sp", out=Hcar[hs, 1, :], in_=I["st_im"][q, hs, :].rearrange("g p -> p g"), writes=[HB])
            for k in range(2):
                j = 2 * q + k
                hprev = Hcar[:] if k == 0 else Sx[:, :, :, j - 1]
                S.op("act", lambda a, hprev=hprev, j=j: a.activation(out=Hb[:, :, :, j], in_=hprev, func=AF.Copy),
                     reads=[HB, SxB], writes=[HbB])
                step(hprev, Sx[:, :, :, j])
            with nc.allow_non_contiguous_dma(reason="state layout"):
                for h in range(2):
                    hs = slice(h * 64, (h + 1) * 64)
                    S.dma("sp", out=outs["s5_re_s"][q, hs, :].rearrange("g p -> p g"), in_=Sx[hs, 0, :, 2 * q + 1],
                          reads=[SxB], writes=[C.outB])
                    S.dma("sp", out=outs["s5_im_s"][q, hs, :].rearrange("g p -> p g"), in_=Sx[hs, 1, :, 2 * q + 1],
                          reads=[SxB], writes=[C.outB])
        if C.cfg.get("s5_stop", 99) <= 8:
            S.barrier(); return
        compute_Y(4, Zd_smp, 0)
    S.barrier()
    stk.close()


def load_tile_s(C, xd, zd, N):
    S = C.S
    for kt in range(KT):
        S.dma("sp", out=C.X[:, kt, 0:N].rearrange("p (s j) -> p s j", s=8),
              in_=xd[:, kt * 128:(kt + 1) * 128, :].rearrange("s p j -> p s j"), reads=[C.X1B], writes=[C.XB[kt]])
        S.dma("sp", out=C.xb[:, kt, 0:N].rearrange("p (s j) -> p s j", s=8),
              in_=zd[:, kt * 128:(kt + 1) * 128, :].rearrange("s p j -> p s j"), reads=[C.ZdB], writes=[C.xbB[kt]])


def gated(C, N, wa_fn, wb_fn, rhs_a, rhs_a_bufs, kta):
    S = C.S
    ps = C.ps
    for mp in range(KT // 2):
        wa, waB = wa_fn(mp)
        wb, wbB = wb_fn(mp)
        for j in range(2):
            m = mp * 2 + j
            ba = next_ps(C)
            bb = next_ps(C)
            S.op("pe", lambda pe, ba=ba, j=j, wa=wa: _mm_group(
                pe, ps[:, ba, 0:N], [(wa[:, kt, j * 128:(j + 1) * 128], rhs_a[:, kt, 0:N]) for kt in range(kta)]),
                reads=[waB] + rhs_a_bufs, writes=[C.PB[ba]])
            S.op("pe", lambda pe, bb=bb, j=j, wb=wb: _mm_group(
                pe, ps[:, bb, 0:N], [(wb[:, kt, j * 128:(j + 1) * 128], C.xb[:, kt, 0:N]) for kt in range(KT)]),
                reads=[wbB] + C.xbB, writes=[C.PB[bb]])
            tmp, tmpB = next_tmp(C)
            S.op("act", lambda a, bb=bb, tmp=tmp: a.activation(out=tmp[:, 0:N], in_=ps[:, bb, 0:N], func=AF.Sigmoid),
                 writes=[C.PB[bb], tmpB])
            S.op("dve", lambda v, ba=ba, tmp=tmp: v.tensor_tensor(out=tmp[:, 0:N], in0=ps[:, ba, 0:N], in1=tmp[:, 0:N], op=ALU.mult),
                 writes=[C.PB[ba], tmpB])
            S.op("dve", lambda v, m=m, tmp=tmp: v.tensor_tensor(out=C.X[:, m, 0:N], in0=C.X[:, m, 0:N], in1=tmp[:, 0:N], op=ALU.add),
                 reads=[tmpB], writes=[C.XB[m]])


def glu_mix(C, N):
    gated(C, N,
          lambda mp: load_wb(C, "glu", mp),
          lambda mp: load_wb(C, "glu", 8 + mp),
          C.xb, C.xbB, KT)


def load_pT(C, psrc, N, order):
    S = C.S
    J = N // 8
    nb = (N + 127) // 128
    for b in range(nb):
        cols = min(128, N - b * 128)
        tok, tokB = C.tok[C.toki], C.tokB[C.toki]
        C.toki = (C.toki + 1) % 2
        if order == "nat":
            S.dma("sp", out=tok[0:cols, 0:256], in_=psrc[b * 128:b * 128 + cols, :], writes=[tokB])
        else:
            prs = psrc.rearrange("(j s) f -> s j f", s=8)
            if N >= 128:
                for ss in range(2):
                    S.dma("sp", out=tok[ss * 64:(ss + 1) * 64, 0:256], in_=prs[2 * b + ss, :, :], writes=[tokB])
            else:
                for s in range(8):
                    S.dma("sp", out=tok[s * J:(s + 1) * J, 0:256], in_=prs[s, :, :], writes=[tokB])
        bank = next_ps(C)

        def tr(pe, bank=bank, tok=tok, cols=cols):
            ins = None
            for k in range(2):
                ins = pe.matmul(C.ps[:, bank, k * 128:k * 128 + cols], tok[0:cols, k * 128:(k + 1) * 128],
                                C.ident[0:cols, 0:cols], start=True, stop=True)
            return ins
        S.op("pe", tr, reads=[tokB, C.identB], writes=[C.PB[bank]])
        S.op("act", lambda a, bank=bank, b=b, cols=cols: a.activation(
            out=C.pT[:, 0:2, b * 128:b * 128 + cols],
            in_=C.ps[:, bank, 0:256].rearrange("p (k c) -> p k c", k=2)[:, :, 0:cols], func=AF.Copy),
            writes=[C.PB[bank], C.pTB])


def ple(C, N, w_proj, gname):
    S = C.S
    S.dma("pool", out=C.Wp[:], in_=w_proj.rearrange("(k p) c -> p k c", p=128), writes=[C.WpB])
    gated(C, N,
          lambda mp: (C.Wp[:, :, mp * 256:(mp + 1) * 256], C.WpB),
          lambda mp: load_wb(C, gname, mp),
          C.pT, [C.pTB], 2)


def linear_residual(C, N, wname):
    S = C.S
    ps = C.ps
    for mp in range(KT // 2):
        ww, wB = load_wb(C, wname, mp)
        for j in range(2):
            m = mp * 2 + j
            bo = next_ps(C)
            S.op("pe", lambda pe, bo=bo, j=j, ww=ww: _mm_group(
                pe, ps[:, bo, 0:N], [(ww[:, kt, j * 128:(j + 1) * 128], C.xb[:, kt, 0:N]) for kt in range(KT)]),
                reads=[wB] + C.xbB, writes=[C.PB[bo]])
            S.op("dve", lambda v, bo=bo, m=m: v.tensor_tensor(out=C.X[:, m, 0:N], in0=ps[:, bo, 0:N], in1=C.X[:, m, 0:N], op=ALU.add),
                 writes=[C.PB[bo], C.XB[m]])


def qkv_project(C, N, Qd, Kd, Vd, kout, vout):
    S = C.S
    ps = C.ps
    for mp in range(16):
        ww, wB = load_wb(C, "qkv", mp)
        for j in range(2):
            m = mp * 2 + j
            bo = next_ps(C)
            S.op("pe", lambda pe, bo=bo, j=j, ww=ww: _mm_group(
                pe, ps[:, bo, 0:N], [(ww[:, kt, j * 128:(j + 1) * 128], C.xb[:, kt, 0:N]) for kt in range(KT)]),
                reads=[wB] + C.xbB, writes=[C.PB[bo]])
            S.op("act", lambda a, bo=bo, m=m: a.activation(out=C.gT[:, m, 0:N], in_=ps[:, bo, 0:N], func=AF.Copy),
                 writes=[C.PB[bo], C.gTB[m]])
    if Qd is not None:
        S.dma("sp", out=Qd.rearrange("(k p) n -> p k n", p=128), in_=C.gT[:, 0:16, 0:N], reads=C.gTB[0:16], writes=[C.QdB])
    S.dma("sp", out=Kd.rearrange("(k p) n -> p k n", p=128), in_=C.gT[:, 16:32, 0:N], reads=C.gTB[16:32], writes=[C.KdB])
    nb = (N + 127) // 128
    jobs = [("v", 16, Vd, vout)]
    if kout is not None:
        jobs.append(("k", 8, None, kout))
    for name, boff, vd, od in jobs:
        for cb in range(8):
            ww, wB = load_wb(C, "qkv", boff + cb)
            for tb in range(nb):
                rows = min(128, N - tb * 128)
                bo = next_ps(C)
                S.op("pe", lambda pe, bo=bo, tb=tb, rows=rows, ww=ww: _mm_group(
                    pe, ps[0:rows, bo, 0:256], [(C.xb[:, kt, tb * 128:tb * 128 + rows], ww[:, kt, :]) for kt in range(KT)]),
                    reads=[wB] + C.xbB, writes=[C.PB[bo]])
                if vd is not None:
                    S.op("dve", lambda v, bo=bo, tb=tb, rows=rows, cb=cb: v.tensor_copy(
                        out=C.gT[0:rows, tb * 4 + cb // 2, (cb % 2) * 256:(cb % 2) * 256 + 256], in_=ps[0:rows, bo, 0:256]),
                        writes=[C.PB[bo], C.gTB[tb * 4 + cb // 2]])
                if od is not None:
                    tmp, tmpB = next_tmp(C)
                    S.op("act", lambda a, bo=bo, rows=rows, tmp=tmp: a.activation(out=tmp[0:rows, 0:256], in_=ps[0:rows, bo, 0:256], func=AF.Copy),
                         writes=[C.PB[bo], tmpB])
                    S.dma("sp", out=od[tb * 128:tb * 128 + rows, cb * 256:(cb + 1) * 256], in_=tmp[0:rows, 0:256],
                          reads=[tmpB], writes=[C.outB])
        if vd is not None:
            for tb in range(nb):
                rows = min(128, N - tb * 128)
                S.dma("sp", out=vd[tb * 128:tb * 128 + rows, :].rearrange("p (q c) -> p q c", q=4),
                      in_=C.gT[0:rows, tb * 4:tb * 4 + 4, :], reads=C.gTB[tb * 4:tb * 4 + 4], writes=[C.VdB])


def phase3(C, kind, t, N, outs):
    S, I = C.S, C.I
    if kind == "own":
        xd, zd = C.X1_own[t], C.Zd_own[t]
        p0 = I["p_own"][0, t * NT:(t + 1) * NT, :]
        x2d, Qd, Kd, Vd = C.X2_own[t], (C.Qd_own[t - 1] if t >= 1 else None), C.Kd_own[t], C.Vd_own[t]
        last = (t == C.n_own - 1)
        kout = outs["k_p"] if last else None
        vout = outs["v_p"] if last else None
    else:
        xd, zd = C.X1_smp, C.Zd_smp
        p0 = I["p_smp"][0, :, :]
        x2d, Qd, Kd, Vd = C.X2_smp, C.Qd_smp, C.Kd_smp, C.Vd_smp
        kout, vout = outs["k_s"], outs["v_s"]
    load_tile_s(C, xd, zd, N)
    glu_mix(C, N)
    layer_norm(C, N, 1)
    ffn(C, N, "f2i0", "f2o0")
    layer_norm(C, N, 2)
    load_pT(C, p0, N, "sj")
    ple(C, N, I["ple_w_proj"][0], "pg0")
    layer_norm(C, N, 3)
    ffn(C, N, "f1i1", "f1o1")
    layer_norm(C, N, 4, perm="int")
    if kind == "smp" or t >= 1:
        S.dma("sp", out=x2d.rearrange("(k p) n -> p k n", p=128), in_=C.X[:, :, 0:N], reads=C.XB, writes=[C.X2B])
    qkv_project(C, N, Qd, Kd, Vd, kout, vout)


ATT_SCALE = 128 ** -0.5


def attention_phase(C, outs):
    from contextlib import ExitStack
    nc, S, I = C.nc, C.S, C.I
    ps = C.ps
    stk = ExitStack()

    def sb(name, shape, dt):
        return stk.enter_context(nc.sbuf_tensor("at_" + name, shape, dt))
    QT = sb("QT", [128, 16, 512], BF16); QB = Buf("QT")
    KTt = sb("KTt", [128, 16, 1024], BF16); KB = Buf("KT", multi=True)
    Vt = sb("Vt", [128, 8, 2048], BF16); VB = Buf("Vt", multi=True)
    oT = sb("oT", [128, 16, 512], BF16); oB = Buf("oT")
    kc = sb("kc", [128, 4, 2048], BF16); kcB = Buf("kc")
    Ksm = sb("Ksm", [128, 16, NS], BF16); KsmB = Buf("Ksm")
    biasT = sb("biasT", [64, 16, 192], F32)
    brev = sb("brev", [64, 16, 192], F32)
    t256 = sb("t256", [64, 16], F32)
    tab = sb("tab", [16, 257], F32)
    tabx = sb("tabx", [16, 255], F32)
    negv = sb("negv", [128, 1], F32)
    idb = sb("idb", [128, 128], BF16)
    cB = Buf("attconst")
    scs = [sb(f"sc{i}", [64, 576], F32) for i in range(2)]
    scB = [Buf(f"sc{i}") for i in range(2)]
    Pes = {0: [sb(f"pe0{i}", [64, 640], BF16) for i in range(2)], 1: [sb(f"pe1{i}", [64, 640], BF16) for i in range(2)],
           2: [sb("pes", [64, 640], BF16)]}
    PeB = {0: [Buf(), Buf()], 1: [Buf(), Buf()], 2: [Buf()]}
    PTs = [sb(f"PT{i}", [128, 5, 64], BF16) for i in range(2)]
    PTB = [Buf(), Buf()]
    stt = [sb(f"stt{i}", [64, 4], F32) for i in range(2)]
    sttB = [Buf(), Buf()]

    S.dma("pool", out=idb[:], in_=I["ident"][:, :], writes=[cB])
    S.dma("sp", out=negv[:], in_=I["valid"][:, :], writes=[cB])
    S.op("dve", lambda v: v.tensor_scalar(out=negv[:], in0=negv[:], scalar1=-1.0, scalar2=1e30, op0=ALU.add, op1=ALU.mult),
         reads=[cB], writes=[cB])
    for par in Pes:
        for pb, pbB in zip(Pes[par], PeB[par]):
            S.op("dve", lambda v, pb=pb: v.memset(pb[:], 0.0), writes=[pbB])
    Ed = nc.dram_tensor("att_Ed", [16, 255], F32, kind="Internal").ap()
    EdB = Buf("Ed")
    S.dma("sp", out=tab[:], in_=I["attn_rel_bias"][:, :], writes=[cB])
    S.op("dve", lambda v: v.tensor_copy(out=tabx[:, 0:192], in_=tab[:, 65:257]), reads=[cB], writes=[cB])
    S.op("dve", lambda v: v.tensor_copy(out=tabx[:, 192:255], in_=tab[:, 256:257].broadcast_to([16, 63])), reads=[cB], writes=[cB])
    S.dma("sp", out=Ed[:, :], in_=tabx[:], reads=[cB], writes=[EdB])
    S.dma("sp", out=brev[:], in_=bass.AP(Ed.tensor, 0, [[1, 64], [255, 16], [1, 192]]), reads=[EdB], writes=[cB])
    S.op("dve", lambda v: v.tensor_tensor(out=biasT[:], in0=brev[:, :, ::-1], in1=brev[:, :, 191:192].broadcast_to([64, 16, 192]),
                                          op=ALU.subtract), reads=[cB], writes=[cB])
    cnt = [0]

    def attend(nq, qcols, kcol0, ncols, vb0, off, par, mask_cols, out_cols, last_rows):
        nblk = (off + ncols + 127) // 128
        for h in range(16):
            i = cnt[0] % 2
            cnt[0] += 1
            sc, scb = scs[i], scB[i]
            pe_l = Pes[par]
            pb, pbB = pe_l[i % len(pe_l)], PeB[par][i % len(pe_l)]
            PT, ptB = PTs[i], PTB[i]
            st, stB = stt[i], sttB[i]
            bA = next_ps(C)
            bB_ = next_ps(C)
            n1 = min(512, ncols)
            n2 = ncols - n1
            S.op("pe", lambda pe, bA=bA, h=h: pe.matmul(ps[0:nq, bA, 0:n1], QT[:, h, qcols], KTt[:, h, kcol0:kcol0 + n1],
                                                       start=True, stop=True), reads=[QB, KB], writes=[C.PB[bA]])
            S.op("pe", lambda pe, bB_=bB_, h=h: pe.matmul(ps[0:nq, bB_, 0:n2], QT[:, h, qcols], KTt[:, h, kcol0 + n1:kcol0 + ncols],
                                                         start=True, stop=True), reads=[QB, KB], writes=[C.PB[bB_]])
            S.op("dve", lambda v, bA=bA, sc=sc: v.tensor_scalar(out=sc[0:nq, 0:384], in0=ps[0:nq, bA, 0:384], scalar1=ATT_SCALE,
                                                                scalar2=None, op0=ALU.mult), writes=[C.PB[bA], scb])
            S.op("dve", lambda v, bA=bA, sc=sc, h=h: v.scalar_tensor_tensor(
                out=sc[0:nq, 384:512], in0=ps[0:nq, bA, 384:512], scalar=ATT_SCALE, in1=biasT[0:nq, h, 0:128],
                op0=ALU.mult, op1=ALU.add), reads=[cB], writes=[C.PB[bA], scb])
            S.op("dve", lambda v, bB_=bB_, sc=sc, h=h: v.scalar_tensor_tensor(
                out=sc[0:nq, 512:512 + n2], in0=ps[0:nq, bB_, 0:n2], scalar=ATT_SCALE, in1=biasT[0:nq, h, 128:128 + n2],
                op0=ALU.mult, op1=ALU.add), reads=[cB], writes=[C.PB[bB_], scb])
            if mask_cols > 0:
                S.op("dve", lambda v, sc=sc: v.tensor_scalar(out=sc[0:nq, 0:mask_cols], in0=sc[0:nq, 0:mask_cols],
                                                             scalar1=negv[0:nq, 0:1], scalar2=None, op0=ALU.add),
                     reads=[cB], writes=[scb])
            S.op("dve", lambda v, sc=sc, st=st: v.reduce_max(out=st[0:nq, 0:1], in_=sc[0:nq, 0:ncols], axis=AX.X),
                 reads=[scb], writes=[stB])
            S.op("dve", lambda v, st=st: v.tensor_scalar(out=st[0:nq, 1:2], in0=st[0:nq, 0:1], scalar1=-1.0, scalar2=None, op0=ALU.mult),
                 writes=[stB])
            S.op("dve", lambda v, st=st: v.memset(st[0:nq, 2:3], 0.0), writes=[stB])
            S.op("act", lambda a, sc=sc, st=st, pb=pb: a.activation(out=pb[0:nq, off:off + ncols], in_=sc[0:nq, 0:ncols], func=AF.Exp,
                                                                    bias=st[0:nq, 1:2], accum_out=st[0:nq, 2:3]),
                 reads=[scb], writes=[stB, pbB])
            S.op("dve", lambda v, st=st: v.reciprocal(out=st[0:nq, 3:4], in_=st[0:nq, 2:3]), writes=[stB])
            S.op("dve", lambda v, st=st, pb=pb: v.tensor_scalar(out=pb[0:nq, off:off + ncols], in0=pb[0:nq, off:off + ncols],
                                                                scalar1=st[0:nq, 3:4], scalar2=None, op0=ALU.mult),
                 reads=[stB], writes=[pbB])
            bT = next_ps(C)

            def trp(pe, bT=bT, pb=pb):
                ins = None
                for kb in range(nblk):
                    ins = pe.matmul(ps[:, bT, kb * 64:kb * 64 + nq], pb[0:nq, kb * 128:(kb + 1) * 128], idb[0:nq, 0:nq],
                                    start=True, stop=True)
                return ins
            S.op("pe", trp, reads=[pbB, cB], writes=[C.PB[bT]])
            S.op("act", lambda a, bT=bT, PT=PT: a.activation(
                out=PT[:, 0:nblk, 0:nq], in_=ps[:, bT, 0:nblk * 64].rearrange("p (k c) -> p k c", c=64)[:, :, 0:nq], func=AF.Copy),
                writes=[C.PB[bT], ptB])
            bO = next_ps(C)

            def pv(pe, bO=bO, PT=PT, h=h):
                ins = None
                for kb in range(nblk):
                    kr = 128 if kb < nblk - 1 else last_rows
                    ins = pe.matmul(ps[:, bO, 0:nq], Vt[0:kr, vb0 + kb, h * 128:(h + 1) * 128], PT[0:kr, kb, 0:nq],
                                    start=(kb == 0), stop=(kb == nblk - 1))
                return ins
            S.op("pe", pv, reads=[ptB, VB], writes=[C.PB[bO]])
            S.op("act", lambda a, bO=bO, h=h: a.activation(out=oT[:, h, out_cols], in_=ps[:, bO, 0:nq], func=AF.Copy),
                 writes=[C.PB[bO], oB])

    for t in range(1, C.n_own):
        S.dma("sp", out=QT[:], in_=C.Qd_own[t - 1].rearrange("(k p) n -> p k n", p=128), reads=[C.QdB], writes=[QB])
        for u in range(2):
            S.dma("sp", out=KTt[:, :, u * 512:(u + 1) * 512], in_=C.Kd_own[t - 1 + u].rearrange("(k p) n -> p k n", p=128),
                  reads=[C.KdB], writes=[KB])
            S.dma("sp", out=Vt[:, u * 4:(u + 1) * 4, :], in_=C.Vd_own[t - 1 + u].rearrange("(b p) f -> p b f", p=128),
                  reads=[C.VdB], writes=[VB])
        for c in range(8):
            mask_cols = (512 - 64 * c) if t == 1 else 0
            attend(64, slice(64 * c, 64 * c + 64), 64 * c, 576, c // 2, 64 * (c % 2), c % 2, mask_cols,
                   slice(64 * c, 64 * c + 64), 128)
        S.dma("sp", out=C.Od_own[t - 1].rearrange("(k p) n -> p k n", p=128), in_=oT[:], reads=[oB], writes=[C.OdB])
    if C.cfg.get("sample", True):
        S.dma("sp", out=QT[:, :, 0:NS], in_=C.Qd_smp.rearrange("(k p) n -> p k n", p=128), reads=[C.QdB], writes=[QB])
        for q in range(2):
            S.dma("pool", out=kc[:], in_=I["cache_k"][q].rearrange("(b p) f -> p b f", p=128), writes=[kcB])
            S.dma("pool", out=Vt[:, 0:4, :], in_=I["cache_v"][q].rearrange("(b p) f -> p b f", p=128), writes=[VB])
            S.dma("sp", out=Vt[0:16, 4, :], in_=C.Vd_smp[q * 16:(q + 1) * 16, :], reads=[C.VdB], writes=[VB])
            if q == 0:
                S.dma("sp", out=Ksm[:], in_=C.Kd_smp.rearrange("(k p) n -> p k n", p=128), reads=[C.KdB], writes=[KsmB])
            S.op("act", lambda a, q=q: a.activation(out=KTt[:, :, 512:528], in_=Ksm[:, :, q * 16:(q + 1) * 16], func=AF.Copy),
                 reads=[KsmB], writes=[KB])
            for h in range(16):
                bank = next_ps(C)

                def trk(pe, bank=bank, h=h):
                    ins = None
                    for b in range(4):
                        ins = pe.matmul(ps[:, bank, b * 128:(b + 1) * 128], kc[:, b, h * 128:(h + 1) * 128], idb[:, :],
                                        start=True, stop=True)
                    return ins
                S.op("pe", trk, reads=[kcB, cB], writes=[C.PB[bank]])
                S.op("act", lambda a, bank=bank, h=h: a.activation(out=KTt[:, h, 0:512], in_=ps[:, bank, :], func=AF.Copy),
                     writes=[C.PB[bank], KB])
            attend(16, slice(q * 16, q * 16 + 16), 0, 528, 0, 0, 2, 0, slice(q * 16, q * 16 + 16), 16)
        S.dma("sp", out=C.Od_smp.rearrange("(k p) n -> p k n", p=128), in_=oT[:, :, 0:NS], reads=[oB], writes=[C.OdB])
    S.barrier()
    stk.close()


def phase4b(C, kind, t, N, outs):
    S, I = C.S, C.I
    ps = C.ps
    if kind == "own":
        x2d, od = C.X2_own[t], C.Od_own[t - 1]
        p1 = I["p_own"][1, t * NT:(t + 1) * NT, :]
        yout = outs["y_p"][(t - 1) * NT:t * NT, :]
    else:
        x2d, od = C.X2_smp, C.Od_smp
        p1 = I["p_smp"][1, :, :]
        yout = outs["y_s"]
    S.dma("sp", out=C.X[:, :, 0:N], in_=x2d.rearrange("(k p) n -> p k n", p=128), reads=[C.X2B], writes=C.XB)
    S.dma("sp", out=C.xb[:, :, 0:N], in_=od.rearrange("(k p) n -> p k n", p=128), reads=[C.OdB], writes=C.xbB)
    linear_residual(C, N, "wo")
    layer_norm(C, N, 5)
    ffn(C, N, "f2i1", "f2o1")
    layer_norm(C, N, 6)
    load_pT(C, p1, N, "nat")
    ple(C, N, I["ple_w_proj"][1], "pg1")
    layer_norm(C, N, 7, final=True)
    nb = (N + 127) // 128
    for b in range(nb):
        rows = min(128, N - b * 128)
        tok, tokB = C.tok[C.toki], C.tokB[C.toki]
        C.toki = (C.toki + 1) % 2
        for k0 in range(0, KT, 4):
            bank = next_ps(C)

            def tr(pe, bank=bank, k0=k0, b=b, rows=rows):
                ins = None
                for j in range(4):
                    ins = pe.matmul(ps[0:rows, bank, j * 128:(j + 1) * 128], C.X[:, k0 + j, b * 128:b * 128 + rows], C.ident[:, :],
                                    start=True, stop=True)
                return ins
            S.op("pe", tr, reads=C.XB[k0:k0 + 4] + [C.identB], writes=[C.PB[bank]])
            S.op("act", lambda a, bank=bank, k0=k0, rows=rows, tok=tok: a.activation(
                out=tok[0:rows, k0 * 128:(k0 + 4) * 128], in_=ps[0:rows, bank, :], func=AF.Copy),
                writes=[C.PB[bank], tokB])
        S.dma("sp", out=yout[b * 128:b * 128 + rows, :], in_=tok[0:rows, :], reads=[tokB], writes=[C.outB])
```

```python
import math
import numpy as np
import concourse.bass as bass
import concourse.mybir as mybir
from concourse.bass_utils import run_bass_kernel_spmd

F32 = mybir.dt.float32
BF16 = mybir.dt.bfloat16
AF = mybir.ActivationFunctionType
ALU = mybir.AluOpType
AX = mybir.AxisListType

D = 2048
KT = 16
FF = 5632
FT = 44
NT = 512
NCORES = 8
ALPHA = float((2 * 2) ** 0.25)
LN_EPS = 1e-5
N_PRE = 27
N_OWN = 5
NS = 32
RING = 16


class Buf:
    __slots__ = ("name", "w", "r", "wl", "multi")

    def __init__(self, name="", multi=False):
        self.name = name
        self.w = None
        self.r = {}
        self.wl = {}
        self.multi = multi


class Sched:
    ENG = ("pe", "act", "dve", "pool", "sp")

    def __init__(self, nc):
        self.nc = nc
        self.h = {"pe": nc.tensor, "act": nc.scalar, "dve": nc.vector, "pool": nc.gpsimd, "sp": nc.sync}
        self.nsem = 0
        self.psem = {}
        self.pekeys = set()
        for e in self.ENG:
            self._newsem(e)
        self.seen = {e: {} for e in self.ENG}
        self.ring = {q: [self._alloc() for _ in range(RING)] for q in ("sp", "pool")}
        self.ridx = {"sp": 0, "pool": 0}
        self.nins = {e: 0 for e in self.ENG}

    def _alloc(self):
        self.nsem += 1
        return [self.nsem, self.nc.alloc_semaphore(f"sm{self.nsem}"), 0]

    def _newsem(self, e):
        k = self._alloc()
        self.psem[e] = k
        if e == "pe":
            self.pekeys.add(k[0])

    def _wait(self, e, key, sem, val):
        if self.seen[e].get(key, 0) >= val:
            return
        self.h[e].wait_ge(sem, val)
        self.nins[e] += 1
        self.seen[e][key] = val

    def _deps(self, e, reads, writes, accum=False):
        need = {}

        def add(t):
            if t[0] not in need or need[t[0]][2] < t[2]:
                need[t[0]] = t
        for b in reads:
            if b.w is not None:
                add(b.w)
            for t in b.wl.values():
                add(t)
        for b in writes:
            if b.w is not None:
                add(b.w)
            if not accum:
                for t in b.wl.values():
                    add(t)
            for t in b.r.values():
                add(t)
        for key, t in need.items():
            if e == "pe" and key in self.pekeys:
                continue
            self._wait(e, t[0], t[1], t[2])

    def op(self, e, fn, reads=(), writes=()):
        self._deps(e, reads, writes)
        ins = fn(self.h[e])
        k = self.psem[e]
        k[2] += 1
        ins.then_inc(k[1], 1)
        self.nins[e] += 1
        tok = (k[0], k[1], k[2])
        for b in writes:
            b.w = tok
            b.r = {}
            b.wl = {}
        for b in reads:
            b.r[e] = tok
        if k[2] >= 30000:
            self._newsem(e)
        return tok

    def dma(self, q, out, in_, reads=(), writes=(), accum=False, **kw):
        if writes and all(b.multi for b in writes):
            accum = True
        self._deps(q, reads, writes, accum)
        i = self.ridx[q]
        self.ridx[q] = (i + 1) % RING
        slot = self.ring[q][i]
        if slot[2] > 0:
            self._wait(q, slot[0], slot[1], slot[2])
        if slot[2] >= 30000:
            slot = self._alloc()
            self.ring[q][i] = slot
        ins = self.h[q].dma_start(out=out, in_=in_, **kw)
        slot[2] += 16
        ins.then_inc(slot[1], 16)
        self.nins[q] += 1
        tok = (slot[0], slot[1], slot[2])
        for b in writes:
            if accum:
                b.wl[slot[0]] = tok
            else:
                b.w = tok
                b.r = {}
                b.wl = {}
        for b in reads:
            b.r[("d", slot[0])] = tok
        return tok

    def barrier(self):
        toks = []
        for e in self.ENG:
            k = self.psem[e]
            if k[2] > 0:
                toks.append((k[0], k[1], k[2]))
        for q in ("sp", "pool"):
            for slot in self.ring[q]:
                if slot[2] > 0:
                    toks.append((slot[0], slot[1], slot[2]))
        for e in self.ENG:
            for t in toks:
                if t[0] == self.psem[e][0]:
                    continue
                self._wait(e, t[0], t[1], t[2])


class Ctx:
    pass


def _mm_group(pe, out, pairs):
    n = len(pairs)
    ins = None
    for i, (l, r) in enumerate(pairs):
        ins = pe.matmul(out, l, r, start=(i == 0), stop=(i == n - 1))
    return ins


def build(cfg):
    nc = bass.Bass("TRN2", target_bir_lowering=False)
    S = Sched(nc)
    C = Ctx()
    C.nc, C.S, C.cfg = nc, S, cfg
    dbg = cfg.get("debug", False)

    def din(name, shape, dt=F32):
        return nc.dram_tensor(name, list(shape), dt, kind="ExternalInput").ap()

    def dout(name, shape, dt=F32):
        return nc.dram_tensor(name, list(shape), dt, kind="ExternalOutput").ap()

    def dscr(name, shape, dt=F32, out=False):
        return nc.dram_tensor(name, list(shape), dt, kind=("ExternalOutput" if (out and dbg) else "Internal")).ap()

    n_pre = cfg.get("n_pre", N_PRE)
    n_own = cfg.get("n_own", N_OWN)
    C.n_pre, C.n_own = n_pre, n_own

    I = {}
    I["x_full"] = din("x_full", [max(n_pre, 1) * NT, D])
    I["x_own"] = din("x_own", [n_own * NT, D])
    I["x_smp"] = din("x_smp", [NS, D])
    I["ident"] = din("ident", [128, 128])
    I["ffn1_w_in"] = din("ffn1_w_in", [2, D, 2 * FF])
    I["ffn1_w_out"] = din("ffn1_w_out", [2, FF, D])
    I["ln_g"] = din("ln_g", [2, 4, D])
    I["ln_b"] = din("ln_b", [2, 4, D])
    C.I = I

    def sb(name, shape, dt):
        return nc.alloc_sbuf_tensor("sb_" + name, shape, dt)
    C.sb = sb
    C.ident = sb("ident", [128, 128], F32)
    C.identB = Buf("ident")
    C.ones_bf = sb("ones_bf", [128, 128], BF16)
    C.onesB = Buf("ones")
    C.lng = sb("lng", [128, 8, KT], F32)
    C.lnb = sb("lnb", [128, 8, KT], F32)
    C.lnga = sb("lnga", [128, 8, KT], F32)
    C.lnba = sb("lnba", [128, 8, KT], F32)
    C.lnB = Buf("ln")
    C.ps = nc.alloc_psum_tensor("ps", [128, 8, 512], F32)
    C.PB = [Buf(f"ps{i}") for i in range(8)]

    S.dma("sp", out=C.ident[:], in_=I["ident"][:, :], writes=[C.identB])
    S.op("dve", lambda v: v.memset(C.ones_bf[:], 1.0), writes=[C.onesB])
    with nc.allow_non_contiguous_dma(reason="tiny ln param transposes"):
        S.dma("sp", out=C.lng[:], in_=I["ln_g"].rearrange("l s (kt p) -> p (l s) kt", p=128), writes=[C.lnB])
        S.dma("sp", out=C.lnb[:], in_=I["ln_b"].rearrange("l s (kt p) -> p (l s) kt", p=128), writes=[C.lnB])
    S.op("dve", lambda v: v.tensor_scalar(out=C.lnga[:], in0=C.lng[:], scalar1=ALPHA, scalar2=None, op0=ALU.mult),
         reads=[C.lnB], writes=[C.lnB])
    S.op("dve", lambda v: v.tensor_scalar(out=C.lnba[:], in0=C.lnb[:], scalar1=ALPHA, scalar2=None, op0=ALU.mult),
         reads=[C.lnB], writes=[C.lnB])

    C.psi = 0
    C.wsi = 0
    C.tmpi = 0
    C.toki = 0

    C.Ud_pre = dscr("Ud_pre", [max(n_pre, 1), 8, D, 64], BF16, out=True)
    C.Ud_own = dscr("Ud_own", [n_own, 8, D, 64], BF16, out=True)
    C.Ud_smp = dscr("Ud_smp", [8, D, 4], BF16, out=True)
    C.X1_own = dscr("X1_own", [n_own, 8, D, 64], F32, out=True)
    C.X1_smp = dscr("X1_smp", [8, D, 4], F32, out=True)
    C.UdB = Buf("Ud", multi=True)
    C.X1B = Buf("X1d", multi=True)

    C.ZdB = Buf("Zd", multi=True)
    C.outB = Buf("outs", multi=True)
    C.QdB, C.KdB, C.VdB, C.X2B, C.OdB = (Buf("Qd", multi=True), Buf("Kd", multi=True), Buf("Vd", multi=True),
                                            Buf("X2", multi=True), Buf("Od", multi=True))
    for nm, shp in [("s5_a_re", [128, 64]), ("s5_a_im", [128, 64]), ("s5_log_dt", [128]), ("s5_b_re", [128, 64, 16]),
                    ("s5_b_im", [128, 64, 16]), ("s5_c_re", [128, 16, 64]), ("s5_c_im", [128, 16, 64]), ("s5_d", [128, 16]),
                    ("hsel", [128, 8]), ("valid", [128, 1]), ("ident64", [128, 64]), ("mask_ts", [128, 128]),
                    ("st_re", [2, 128, 64]), ("st_im", [2, 128, 64]),
                    ("p_own", [2, n_own * NT, 256]), ("p_smp", [2, NS, 256]),
                    ("cache_k", [2, 512, D]), ("cache_v", [2, 512, D]),
                    ("ffn2_w_in", [2, D, 2 * FF]), ("ffn2_w_out", [2, FF, D]),
                    ("ple_w_proj", [2, 256, D]), ("ple_w_gate", [2, D, D]),
                    ("s5_w_glu", [D, 2 * D]), ("attn_w_qkv", [D, 3 * D]), ("attn_w_o", [D, D]), ("attn_rel_bias", [16, 257])]:
        I[nm] = din(nm, shp)
    O = {}
    O["y_p"] = dout("y_p", [(n_own - 1) * NT, D]); O["y_s"] = dout("y_s", [NS, D])
    O["s5_re_p"] = dout("s5_re_p", [128, 64]); O["s5_im_p"] = dout("s5_im_p", [128, 64])
    O["k_p"] = dout("k_p", [NT, D]); O["v_p"] = dout("v_p", [NT, D])
    O["s5_re_s"] = dout("s5_re_s", [2, 128, 64]); O["s5_im_s"] = dout("s5_im_s", [2, 128, 64])
    O["k_s"] = dout("k_s", [NS, D]); O["v_s"] = dout("v_s", [NS, D])
    C.O = O
    C.Zd_own = dscr("Zd_own", [n_own, 8, D, 64], BF16, out=True)
    C.Zd_smp = dscr("Zd_smp", [8, D, 4], BF16, out=True)
    C.X2_own = dscr("X2_own", [n_own, D, NT], F32, out=True)
    C.X2_smp = dscr("X2_smp", [D, NS], F32, out=True)
    C.Qd_own = dscr("Qd_own", [n_own - 1, D, NT], BF16, out=True)
    C.Qd_smp = dscr("Qd_smp", [D, NS], BF16, out=True)
    C.Kd_own = dscr("Kd_own", [n_own, D, NT], BF16, out=True)
    C.Kd_smp = dscr("Kd_smp", [D, NS], BF16, out=True)
    C.Vd_own = dscr("Vd_own", [n_own, NT, D], BF16, out=True)
    C.Vd_smp = dscr("Vd_smp", [NS, D], BF16, out=True)
    C.Od_own = dscr("Od_own", [n_own - 1, D, NT], BF16, out=True)
    C.Od_smp = dscr("Od_smp", [D, NS], BF16, out=True)
    from contextlib import ExitStack
    mode = cfg.get("mode", "full")
    if mode == "s5test":
        Ud_pre = din("Ud_pre_in", [max(n_pre, 1), 8, D, 64], BF16)
        Ud_own = din("Ud_own_in", [n_own, 8, D, 64], BF16)
        Ud_smp = din("Ud_smp_in", [8, D, 4], BF16)
        s5_phase(C, Ud_pre, Ud_own, Ud_smp, C.Zd_own, C.Zd_smp, O)
        C.nins = dict(S.nins)
        return nc, C
    smp = cfg.get("sample", True)
    stk = ExitStack()
    alloc_rowlocal(C, stk)
    setup_weights(C)
    emit_conv(C, C.conv_upfront)
    tiles = []
    for t in range(n_pre):
        tiles.append(("pre", t, NT))
    for t in range(n_own):
        tiles.append(("own", t, NT))
    if smp:
        tiles.append(("smp", 0, NS))
    for kind, t, N in tiles:
        if kind == "pre":
            src = I["x_full"][t * NT:(t + 1) * NT, :]
        elif kind == "own":
            src = I["x_own"][t * NT:(t + 1) * NT, :]
        else:
            src = I["x_smp"][:, :]
        load_xT(C, src, N)
        if not hasattr(C, "s5p"):
            s5_param_relayout(C)
        ffn(C, N, "f1i0", "f1o0")
        emit_conv(C, 8)
        layer_norm(C, N, 0, perm="deint")
        if kind == "pre":
            ud = C.Ud_pre[t]
        elif kind == "own":
            ud = C.Ud_own[t]
        else:
            ud = C.Ud_smp
        for kt in range(KT):
            S.dma("sp", out=ud[:, kt * 128:(kt + 1) * 128, :].rearrange("s p j -> p s j"),
                  in_=C.xb[:, kt, 0:N].rearrange("p (s j) -> p s j", s=8),
                  reads=[C.xbB[kt]], writes=[C.UdB])
        if kind != "pre":
            xd = C.X1_own[t] if kind == "own" else C.X1_smp
            for kt in range(KT):
                S.dma("sp", out=xd[:, kt * 128:(kt + 1) * 128, :].rearrange("s p j -> p s j"),
                      in_=C.X[:, kt, 0:N].rearrange("p (s j) -> p s j", s=8),
                      reads=[C.XB[kt]], writes=[C.X1B])
    emit_conv(C, 10 ** 6)
    S.barrier()
    stk.close()
    if mode == "p01":
        C.nins = dict(S.nins)
        return nc, C
    s5_phase(C, C.Ud_pre, C.Ud_own, C.Ud_smp if smp else None, C.Zd_own, C.Zd_smp, O)
    stk = ExitStack()
    alloc_rowlocal(C, stk)
    for t in range(n_own):
        phase3(C, "own", t, NT, O)
    if smp:
        phase3(C, "smp", 0, NS, O)
    S.barrier()
    stk.close()
    if mode == "p3":
        C.nins = dict(S.nins)
        return nc, C
    attention_phase(C, O)
    stk = ExitStack()
    alloc_rowlocal(C, stk)
    for t in range(1, n_own):
        phase4b(C, "own", t, NT, O)
    if smp:
        phase4b(C, "smp", 0, NS, O)
    S.barrier()
    stk.close()
    C.nins = dict(S.nins)
    return nc, C


def alloc_rowlocal(C, stk):
    nc = C.nc
    C.rlgen = getattr(C, "rlgen", 0) + 1
    gen = C.rlgen

    def sb(name, shape, dt):
        return stk.enter_context(nc.sbuf_tensor(f"rl{gen}_" + name, shape, dt))
    C.X = sb("X", [128, KT, NT], F32)
    C.XB = [Buf(f"X{k}") for k in range(KT)]
    C.xb = sb("xb", [128, KT, NT], BF16)
    C.xbB = [Buf(f"xb{k}") for k in range(KT)]
    C.gT = sb("gT", [128, FT, NT], BF16)
    C.gTB = [Buf(f"gT{k}") for k in range(FT)]
    C.tmp = [sb(f"tmp{i}", [128, NT], F32) for i in range(3)]
    C.tmpB = [Buf(f"tmp{i}") for i in range(3)]
    C.st = sb("st", [128, 6, NT], F32)
    C.stB = Buf("st")
    C.tok = [sb(f"tok{i}", [128, D], F32) for i in range(2)]
    C.tokB = [Buf(f"tok{i}", multi=True) for i in range(2)]
    C.pT = sb("pT", [128, 2, NT], BF16)
    C.pTB = Buf("pT")
    C.Wp = sb("Wp", [128, 2, D], BF16)
    C.WpB = Buf("Wp")
    NSLOT = 4
    C.wsl = [sb(f"wsl{i}", [128, 6144], BF16) for i in range(NSLOT)]
    C.wslB = [Buf(f"wsl{i}") for i in range(NSLOT)]


def next_tmp(C):
    i = C.tmpi
    C.tmpi = (i + 1) % len(C.tmp)
    return C.tmp[i], C.tmpB[i]


def next_ps(C, n=1):
    i = C.psi
    C.psi = (i + 1) % 8
    return i


def setup_weights(C):
    nc, I = C.nc, C.I
    C.W = {}
    C.convq = []

    def reg(name, ap2d, ktn, cw):
        nblk = ap2d.shape[1] // cw
        scr = nc.dram_tensor("wb_" + name, [nblk, 128, ktn * cw], BF16, kind="Internal").ap()
        C.W[name] = dict(src=ap2d, scr=scr, ktn=ktn, cw=cw, nblk=nblk, buf=Buf("wb_" + name, multi=True))
        for b in range(nblk):
            C.convq.append((name, b))
    reg("f1i0", I["ffn1_w_in"][0], KT, 256)
    reg("f1o0", I["ffn1_w_out"][0], FT, 128)
    C.conv_upfront = len(C.convq)
    reg("glu", I["s5_w_glu"], KT, 256)
    reg("f2i0", I["ffn2_w_in"][0], KT, 256)
    reg("f2o0", I["ffn2_w_out"][0], FT, 128)
    reg("pg0", I["ple_w_gate"][0], KT, 256)
    reg("f1i1", I["ffn1_w_in"][1], KT, 256)
    reg("f1o1", I["ffn1_w_out"][1], FT, 128)
    reg("qkv", I["attn_w_qkv"], KT, 256)
    reg("wo", I["attn_w_o"], KT, 256)
    reg("f2i1", I["ffn2_w_in"][1], KT, 256)
    reg("f2o1", I["ffn2_w_out"][1], FT, 128)
    reg("pg1", I["ple_w_gate"][1], KT, 256)
    C.convi = 0


def emit_conv(C, n):
    S = C.S
    while n > 0 and C.convi < len(C.convq):
        name, b = C.convq[C.convi]
        C.convi += 1
        n -= 1
        w = C.W[name]
        cw = w["cw"]
        S.dma("pool", out=w["scr"][b].rearrange("p (k c) -> p k c", c=cw),
              in_=w["src"][:, b * cw:(b + 1) * cw].rearrange("(k p) c -> p k c", p=128), writes=[w["buf"]])


def load_wb(C, name, blk):
    S = C.S
    w = C.W[name]
    ktn, cw = w["ktn"], w["cw"]
    i = C.wsi
    C.wsi = (i + 1) % len(C.wsl)
    S.dma("pool", out=C.wsl[i][:, 0:ktn * cw], in_=w["scr"][blk], reads=[w["buf"]], writes=[C.wslB[i]])
    return C.wsl[i][:, 0:ktn * cw].rearrange("p (k c) -> p k c", c=cw), C.wslB[i]


def load_w(C, src_ap, ktn, cw):
    S = C.S
    i = C.wsi
    C.wsi = (i + 1) % len(C.wsl)
    view = C.wsl[i][:, 0:ktn * cw].rearrange("p (k c) -> p k c", c=cw)
    S.dma("pool", out=view, in_=src_ap.rearrange("(k p) c -> p k c", p=128), writes=[C.wslB[i]])
    return view, C.wslB[i]


def load_xT(C, src, N):
    S = C.S
    nb = (N + 127) // 128
    for b in range(nb):
        rows = min(128, N - b * 128)
        ti = C.toki
        C.toki = (ti + 1) % 2
        tok, tokB = C.tok[ti], C.tokB[ti]
        S.dma("sp", out=tok[0:rows, :], in_=src[b * 128:b * 128 + rows, :], writes=[tokB])
        for k0 in range(0, KT, 4):
            bank = next_ps(C)

            def tr(pe, bank=bank, k0=k0, tok=tok, rows=rows):
                ins = None
                for j in range(4):
                    ins = pe.matmul(C.ps[:, bank, j * 128:j * 128 + rows],
                                    tok[0:rows, (k0 + j) * 128:(k0 + j + 1) * 128], C.ident[0:rows, 0:rows],
                                    start=True, stop=True)
                return ins
            S.op("pe", tr, reads=[tokB, C.identB], writes=[C.PB[bank]])
            src_ps = C.ps[:, bank, :].rearrange("p (j c) -> p j c", j=4)[:, :, 0:rows]
            S.op("act", lambda a, k0=k0, b=b, rows=rows, src_ps=src_ps: a.activation(
                out=C.X[:, k0:k0 + 4, b * 128:b * 128 + rows], in_=src_ps, func=AF.Copy, scale=ALPHA),
                writes=[C.PB[bank]] + [C.XB[k] for k in range(k0, k0 + 4)])
            S.op("dve", lambda v, k0=k0, b=b, rows=rows, src_ps=src_ps: v.tensor_copy(
                out=C.xb[:, k0:k0 + 4, b * 128:b * 128 + rows], in_=src_ps),
                writes=[C.PB[bank]] + [C.xbB[k] for k in range(k0, k0 + 4)])


def ffn(C, N, wi, wo_):
    S = C.S
    ps = C.ps
    for fp in range(FT // 2):
        wg, wgB = load_wb(C, wi, fp)
        wu, wuB = load_wb(C, wi, FT // 2 + fp)
        for j in range(2):
            f = fp * 2 + j
            bg = next_ps(C)
            bu = next_ps(C)
            S.op("pe", lambda pe, bg=bg, j=j, wg=wg: _mm_group(
                pe, ps[:, bg, 0:N], [(wg[:, kt, j * 128:(j + 1) * 128], C.xb[:, kt, 0:N]) for kt in range(KT)]),
                reads=[wgB] + C.xbB, writes=[C.PB[bg]])
            S.op("pe", lambda pe, bu=bu, j=j, wu=wu: _mm_group(
                pe, ps[:, bu, 0:N], [(wu[:, kt, j * 128:(j + 1) * 128], C.xb[:, kt, 0:N]) for kt in range(KT)]),
                reads=[wuB] + C.xbB, writes=[C.PB[bu]])
            tmp, tmpB = next_tmp(C)
            S.op("act", lambda a, bg=bg, tmp=tmp: a.activation(out=tmp[:, 0:N], in_=ps[:, bg, 0:N], func=AF.Silu),
                 writes=[C.PB[bg], tmpB])
            S.op("dve", lambda v, bu=bu, tmp=tmp, f=f: v.tensor_tensor(
                out=C.gT[:, f, 0:N], in0=ps[:, bu, 0:N], in1=tmp[:, 0:N], op=ALU.mult),
                reads=[tmpB], writes=[C.PB[bu], C.gTB[f]])
    for m in range(KT):
        wo, woB = load_wb(C, wo_, m)
        bo = next_ps(C)
        S.op("pe", lambda pe, bo=bo, wo=wo: _mm_group(
            pe, ps[:, bo, 0:N], [(wo[:, f, :], C.gT[:, f, 0:N]) for f in range(FT)]),
            reads=[woB] + C.gTB, writes=[C.PB[bo]])
        S.op("dve", lambda v, bo=bo, m=m: v.scalar_tensor_tensor(
            out=C.X[:, m, 0:N], in0=ps[:, bo, 0:N], scalar=0.5, in1=C.X[:, m, 0:N], op0=ALU.mult, op1=ALU.add),
            writes=[C.PB[bo], C.XB[m]])


def layer_norm(C, N, idx, perm=None, final=False):
    S = C.S
    ps = C.ps
    rb = C.gT[:, 0:KT, :]
    rsq = C.gT[:, KT:2 * KT, :]
    for kt in range(KT):
        S.op("dve", lambda v, kt=kt: v.tensor_copy(out=rb[:, kt, 0:N], in_=C.X[:, kt, 0:N]),
             reads=[C.XB[kt]], writes=[C.gTB[kt]])
        S.op("act", lambda a, kt=kt: a.activation(out=rsq[:, kt, 0:N], in_=C.X[:, kt, 0:N], func=AF.Square),
             reads=[C.XB[kt]], writes=[C.gTB[KT + kt]])
    b1 = next_ps(C)
    b2 = next_ps(C)
    S.op("pe", lambda pe: _mm_group(pe, ps[:, b1, 0:N], [(C.ones_bf[:], rb[:, kt, 0:N]) for kt in range(KT)]),
         reads=[C.onesB] + C.gTB[0:KT], writes=[C.PB[b1]])
    S.op("pe", lambda pe: _mm_group(pe, ps[:, b2, 0:N], [(C.ones_bf[:], rsq[:, kt, 0:N]) for kt in range(KT)]),
         reads=[C.onesB] + C.gTB[KT:2 * KT], writes=[C.PB[b2]])
    mu = C.st[:, 0, 0:N]
    ex2 = C.st[:, 1, 0:N]
    var = C.st[:, 2, 0:N]
    sd = C.st[:, 3, 0:N]
    rstd = C.st[:, 4, 0:N]
    S.op("act", lambda a: a.activation(out=mu, in_=ps[:, b1, 0:N], func=AF.Copy, scale=1.0 / D),
         writes=[C.PB[b1], C.stB])
    S.op("act", lambda a: a.activation(out=ex2, in_=ps[:, b2, 0:N], func=AF.Copy, scale=1.0 / D),
         writes=[C.PB[b2], C.stB])
    S.op("dve", lambda v: v.tensor_tensor(out=var, in0=mu, in1=mu, op=ALU.mult), reads=[C.stB], writes=[C.stB])
    S.op("dve", lambda v: v.tensor_tensor(out=var, in0=ex2, in1=var, op=ALU.subtract), reads=[C.stB], writes=[C.stB])
    S.op("dve", lambda v: v.tensor_scalar(out=var, in0=var, scalar1=LN_EPS, scalar2=None, op0=ALU.add),
         reads=[C.stB], writes=[C.stB])
    S.op("act", lambda a: a.activation(out=sd, in_=var, func=AF.Sqrt), reads=[C.stB], writes=[C.stB])
    S.op("dve", lambda v: v.reciprocal(out=rstd, in_=sd), reads=[C.stB], writes=[C.stB])

    def pv(ap):
        if perm is None:
            return ap
        if perm == "deint":
            return ap.rearrange("p (s j) -> p j s", s=8)
        return ap.rearrange("p (j s) -> p s j", s=8)

    def pin(ap):
        if perm is None:
            return ap
        if perm == "deint":
            return ap.rearrange("p (j s) -> p j s", s=8)
        return ap.rearrange("p (s j) -> p s j", s=8)

    for kt in range(KT):
        tmp, tmpB = next_tmp(C)
        S.op("dve", lambda v, kt=kt, tmp=tmp: v.tensor_tensor(out=tmp[:, 0:N], in0=C.X[:, kt, 0:N], in1=mu, op=ALU.subtract),
             reads=[C.XB[kt], C.stB], writes=[tmpB])
        S.op("dve", lambda v, tmp=tmp: v.tensor_tensor(out=tmp[:, 0:N], in0=tmp[:, 0:N], in1=rstd, op=ALU.mult),
             reads=[tmpB, C.stB], writes=[tmpB])
        S.op("act", lambda a, kt=kt, tmp=tmp: a.activation(
            out=pv(C.X[:, kt, 0:N]), in_=pin(tmp[:, 0:N]), func=AF.Identity,
            scale=(C.lng if final else C.lnga)[:, idx, kt:kt + 1], bias=(C.lnb if final else C.lnba)[:, idx, kt:kt + 1]),
            reads=[tmpB, C.lnB], writes=[C.XB[kt]])
        S.op("act", lambda a, kt=kt, tmp=tmp: a.activation(
            out=pv(C.xb[:, kt, 0:N]), in_=pin(tmp[:, 0:N]), func=AF.Identity,
            scale=C.lng[:, idx, kt:kt + 1], bias=C.lnb[:, idx, kt:kt + 1]),
            reads=[tmpB, C.lnB], writes=[C.xbB[kt]])


def host_consts():
    mask = np.zeros((128, 128), np.float32)
    for s_ in range(8):
        for t_ in range(8):
            if t_ >= s_:
                mask[s_ * 16:(s_ + 1) * 16, t_ * 16:(t_ + 1) * 16] = 1.0
    ident64 = np.zeros((128, 64), np.float32)
    ident64[np.arange(128), np.arange(128) % 64] = 1.0
    return {"ident": np.eye(128, dtype=np.float32), "ident64": ident64, "mask_ts": mask}


WEIGHT_KEYS = ["ffn1_w_in", "ffn1_w_out", "ffn2_w_in", "ffn2_w_out", "ln_g", "ln_b", "ple_w_proj", "ple_w_gate",
               "s5_a_re", "s5_a_im", "s5_log_dt", "s5_b_re", "s5_b_im", "s5_c_re", "s5_c_im", "s5_d", "s5_w_glu",
               "attn_w_qkv", "attn_w_o", "attn_rel_bias"]


def make_core_inputs(inputs, c, n_pre, n_own, own_start, consts):
    f32 = np.float32
    xp = np.asarray(inputs["x_prompt"])[0]
    pp = np.asarray(inputs["p_prompt"])[:, 0]
    lo = own_start - NT
    hi = own_start + (n_own - 1) * NT
    x_own = np.zeros((n_own * NT, D), f32)
    p_own = np.zeros((2, n_own * NT, 256), f32)
    a = max(lo, 0)
    x_own[a - lo:] = xp[a:hi]
    p_own[:, a - lo:] = pp[:, a:hi]
    m = dict(consts)
    m["x_full"] = np.ascontiguousarray(xp[0:max(n_pre, 1) * NT])
    m["x_own"] = x_own
    m["p_own"] = p_own
    m["x_smp"] = np.ascontiguousarray(np.asarray(inputs["x_sample"])[2 * c:2 * c + 2]).reshape(NS, D)
    m["p_smp"] = np.ascontiguousarray(np.asarray(inputs["p_sample"])[:, 2 * c:2 * c + 2]).reshape(2, NS, 256)
    m["st_re"] = np.ascontiguousarray(np.asarray(inputs["state_s5_re"])[2 * c:2 * c + 2])
    m["st_im"] = np.ascontiguousarray(np.asarray(inputs["state_s5_im"])[2 * c:2 * c + 2])
    m["cache_k"] = np.ascontiguousarray(np.asarray(inputs["cache_k"])[2 * c:2 * c + 2]).reshape(2, 512, D)
    m["cache_v"] = np.ascontiguousarray(np.asarray(inputs["cache_v"])[2 * c:2 * c + 2]).reshape(2, 512, D)
    hsel = np.zeros((128, 8), f32)
    if own_start - NT > 0:
        hsel[:, (own_start - NT + NT) // (4 * NT)] = 1.0
    m["hsel"] = hsel
    m["valid"] = np.full((128, 1), 1.0 if own_start > 0 else 0.0, f32)
    for k in WEIGHT_KEYS:
        m[k] = np.asarray(inputs[k])
    return m


_NC_CACHE = {}


def kernel(**inputs):
    n_pre, n_own = N_PRE, N_OWN
    key = (n_pre, n_own)
    if key not in _NC_CACHE:
        _NC_CACHE[key] = build(dict(n_pre=n_pre, n_own=n_own))
    nc, C = _NC_CACHE[key]
    consts = host_consts()
    in_maps = [make_core_inputs(inputs, c, n_pre, n_own, 2048 * c, consts) for c in range(NCORES)]
    res = run_bass_kernel_spmd(nc, in_maps, core_ids=list(range(NCORES)))
    R = res.results
    f32 = np.float32
    y_p = np.concatenate([np.asarray(R[c]["y_p"], f32) for c in range(NCORES)], axis=0)[None]
    y_s = np.concatenate([np.asarray(R[c]["y_s"], f32).reshape(2, 16, D) for c in range(NCORES)], axis=0)
    last = NCORES - 1
    s5_re_p = np.asarray(R[last]["s5_re_p"], f32)[None]
    s5_im_p = np.asarray(R[last]["s5_im_p"], f32)[None]
    k_p = np.asarray(R[last]["k_p"], f32).reshape(1, NT, 16, 128)
    v_p = np.asarray(R[last]["v_p"], f32).reshape(1, NT, 16, 128)
    s5_re_s = np.concatenate([np.asarray(R[c]["s5_re_s"], f32) for c in range(NCORES)], axis=0)
    s5_im_s = np.concatenate([np.asarray(R[c]["s5_im_s"], f32) for c in range(NCORES)], axis=0)
    k_s = np.concatenate([np.asarray(R[c]["k_s"], f32).reshape(2, 16, 16, 128) for c in range(NCORES)], axis=0)
    v_s = np.concatenate([np.asarray(R[c]["v_s"], f32).reshape(2, 16, 16, 128) for c in range(NCORES)], axis=0)
    return (y_p, y_s, s5_re_p, s5_im_p, k_p, v_p, s5_re_s, s5_im_s, k_s, v_s)


GELU_C = 1.5957691216057308


def s5_param_relayout(C):
    nc, S, I = C.nc, C.S, C.I
    P = {}
    for nm, shp in [("ar", [128, 64]), ("ai", [128, 64]), ("br", [128, 64, 16]), ("bi", [128, 64, 16]),
                    ("cr", [128, 64, 16]), ("ci", [128, 64, 16]), ("dcol", [128, 128])]:
        P[nm] = nc.dram_tensor("s5p_" + nm, shp, F32, kind="Internal").ap()
    C.s5p = P
    C.s5pB = Buf("s5p", multi=True)
    B = C.s5pB
    with nc.allow_non_contiguous_dma(reason="tiny s5 param layout"):
        for h in range(2):
            hs = slice(h * 64, (h + 1) * 64)
            S.dma("sp", out=P["ar"][hs, :], in_=I["s5_a_re"][hs, :].rearrange("g p -> p g"), writes=[B])
            S.dma("sp", out=P["ai"][hs, :], in_=I["s5_a_im"][hs, :].rearrange("g p -> p g"), writes=[B])
            S.dma("sp", out=P["br"][hs, :, :], in_=I["s5_b_re"][hs].rearrange("g p c -> p g c"), writes=[B])
            S.dma("sp", out=P["bi"][hs, :, :], in_=I["s5_b_im"][hs].rearrange("g p c -> p g c"), writes=[B])
            for c in range(16):
                S.dma("sp", out=P["cr"][hs, :, c], in_=I["s5_c_re"][hs, c, :].rearrange("g p -> p g"), writes=[B])
                S.dma("sp", out=P["ci"][hs, :, c], in_=I["s5_c_im"][hs, c, :].rearrange("g p -> p g"), writes=[B])
        for s_ in range(8):
            S.dma("sp", out=P["dcol"][16 * s_:16 * s_ + 16, :], in_=I["s5_d"].rearrange("g c -> c g"), writes=[B])


def s5_phase(C, Ud_pre, Ud_own, Ud_smp, Zd_own, Zd_smp, outs):
    from contextlib import ExitStack
    nc, S, I = C.nc, C.S, C.I
    ps = C.ps
    n_pre, n_own = C.n_pre, C.n_own
    stk = ExitStack()

    def sb(name, shape, dt):
        return stk.enter_context(nc.sbuf_tensor("s5_" + name, shape, dt))

    WTs = sb("WTs", [128, 128, 2, 64], BF16)
    VP = sb("VP", [128, 2, 64, 128], BF16)
    Toep = sb("Toep", [128, 128, 128], BF16)
    A12 = sb("A12", [128, 2, 2, 64], F32)
    AT = sb("AT", [128, 6, 2, 2, 64], F32)
    Hsel = sb("Hsel", [128, 2, 64], F32)
    Hcar = sb("Hcar", [128, 2, 64], F32)
    hselv = sb("hselv", [128, 8], F32)
    validv = sb("validv", [128, 1], F32)
    identb = sb("identb", [128, 64], BF16)
    mask4 = sb("mask4", [128, 4, 128], F32)
    dcol = sb("dcol", [128, 128], F32)
    tabB = Buf("tab")
    cB = Buf("s5consts", multi=True)

    S.dma("sp", out=hselv[:], in_=I["hsel"][:, :], writes=[cB])
    S.dma("sp", out=validv[:], in_=I["valid"][:, :], writes=[cB])
    S.dma("pool", out=identb[:], in_=I["ident64"][:, :], writes=[cB])
    for i in range(4):
        S.dma("sp", out=mask4[:, i, :], in_=I["mask_ts"][:, :], writes=[cB])
    if not hasattr(C, "s5p"):
        s5_param_relayout(C)
    P5 = C.s5p
    S.dma("sp", out=dcol[:], in_=P5["dcol"][:, :], reads=[C.s5pB], writes=[cB])

    with ExitStack() as bstk:
        def tb(name, shape, dt=F32):
            return bstk.enter_context(nc.sbuf_tensor("s5b_" + name, shape, dt))
        ar = tb("ar", [128, 64]); ai = tb("ai", [128, 64]); ldt = tb("ldt", [128, 64])
        br = tb("br", [128, 64, 16]); bi = tb("bi", [128, 64, 16])
        cr = tb("cr", [128, 64, 16]); ci = tb("ci", [128, 64, 16])
        Bcr = tb("Bcr", [128, 64, 16]); Bci = tb("Bci", [128, 64, 16])
        t1 = tb("t1", [128, 64, 16]); t2 = tb("t2", [128, 64, 16])
        PW = tb("PW", [128, 9, 2, 64]); NG = tb("NG", [128, 9, 2, 64])
        sc = [tb(f"sc{i}", [128, 64]) for i in range(8)]
        WS = tb("WS", [128, 2, 64, 128], BF16)
        bB = Buf("s5build", multi=True)
        for dst, nm in [(ar, "ar"), (ai, "ai"), (br, "br"), (bi, "bi"), (cr, "cr"), (ci, "ci")]:
            S.dma("sp", out=dst[:], in_=P5[nm], reads=[C.s5pB], writes=[bB])
        for h in range(2):
            hs = slice(h * 64, (h + 1) * 64)
            S.dma("sp", out=ldt[hs, :], in_=bass.AP(I["s5_log_dt"].tensor, h * 64, [[0, 64], [1, 64]]), writes=[bB])
        if C.cfg.get("s5_stop", 99) <= 1:
            S.barrier(); return
        def V(fn):
            S.op("dve", fn, reads=[bB, cB], writes=[bB])

        def A(fn):
            S.op("act", fn, reads=[bB, cB], writes=[bB])
        dt_, ardt, th, mag, sn, cs, den, nr = sc
        A(lambda a: a.activation(out=dt_[:], in_=ldt[:], func=AF.Exp))
        V(lambda v: v.tensor_tensor(out=ardt[:], in0=ar[:], in1=dt_[:], op=ALU.mult))
        V(lambda v: v.tensor_tensor(out=th[:], in0=ai[:], in1=dt_[:], op=ALU.mult))
        A(lambda a: a.activation(out=mag[:], in_=ardt[:], func=AF.Exp))
        MAGIC = 12582912.0

        def range_reduce(dst, src, shift):
            V(lambda v: v.tensor_scalar(out=den[:], in0=src, scalar1=shift, scalar2=1.0 / (2 * math.pi), op0=ALU.add, op1=ALU.mult))
            V(lambda v: v.tensor_scalar(out=den[:], in0=den[:], scalar1=MAGIC, scalar2=None, op0=ALU.add))
            V(lambda v: v.tensor_scalar(out=den[:], in0=den[:], scalar1=-MAGIC, scalar2=None, op0=ALU.add))
            V(lambda v: v.scalar_tensor_tensor(out=dst, in0=den[:], scalar=-2 * math.pi, in1=src, op0=ALU.mult, op1=ALU.add))
            if shift != 0.0:
                V(lambda v: v.tensor_scalar(out=dst, in0=dst, scalar1=shift, scalar2=None, op0=ALU.add))
        range_reduce(sn[:], th[:], 0.0)
        A(lambda a: a.activation(out=sn[:], in_=sn[:], func=AF.Sin))
        range_reduce(cs[:], th[:], 0.5 * math.pi)
        A(lambda a: a.activation(out=cs[:], in_=cs[:], func=AF.Sin))
        lr = PW[:, 1, 0, :]
        li = PW[:, 1, 1, :]
        V(lambda v: v.memset(PW[:, 0, 0, :], 1.0))
        V(lambda v: v.memset(PW[:, 0, 1, :], 0.0))
        V(lambda v: v.tensor_tensor(out=lr, in0=mag[:], in1=cs[:], op=ALU.mult))
        V(lambda v: v.tensor_tensor(out=li, in0=mag[:], in1=sn[:], op=ALU.mult))
        V(lambda v: v.tensor_tensor(out=den[:], in0=ar[:], in1=ar[:], op=ALU.mult))
        V(lambda v: v.tensor_tensor(out=dt_[:], in0=ai[:], in1=ai[:], op=ALU.mult))
        V(lambda v: v.tensor_tensor(out=den[:], in0=den[:], in1=dt_[:], op=ALU.add))
        V(lambda v: v.reciprocal(out=den[:], in_=den[:]))
        V(lambda v: v.tensor_scalar(out=nr[:], in0=lr, scalar1=-1.0, scalar2=None, op0=ALU.add))
        cfr, cfi = ardt, th
        V(lambda v: v.tensor_tensor(out=cfr[:], in0=nr[:], in1=ar[:], op=ALU.mult))
        V(lambda v: v.tensor_tensor(out=dt_[:], in0=li, in1=ai[:], op=ALU.mult))
        V(lambda v: v.tensor_tensor(out=cfr[:], in0=cfr[:], in1=dt_[:], op=ALU.add))
        V(lambda v: v.tensor_tensor(out=cfr[:], in0=cfr[:], in1=den[:], op=ALU.mult))
        V(lambda v: v.tensor_tensor(out=cfi[:], in0=li, in1=ar[:], op=ALU.mult))
        V(lambda v: v.tensor_tensor(out=dt_[:], in0=nr[:], in1=ai[:], op=ALU.mult))
        V(lambda v: v.tensor_tensor(out=cfi[:], in0=cfi[:], in1=dt_[:], op=ALU.subtract))
        V(lambda v: v.tensor_tensor(out=cfi[:], in0=cfi[:], in1=den[:], op=ALU.mult))

        def bc(ap2):
            return ap2.unsqueeze(2).broadcast_to([128, 64, 16])

        def cmul(out_r, out_i, xr, xi, yr, yi, neg_i=False):
            V(lambda v: v.tensor_tensor(out=t1[:], in0=xr, in1=yr, op=ALU.mult))
            V(lambda v: v.tensor_tensor(out=t2[:], in0=xi, in1=yi, op=ALU.mult))
            V(lambda v: v.tensor_tensor(out=out_r, in0=t1[:], in1=t2[:], op=ALU.subtract))
            V(lambda v: v.tensor_tensor(out=t1[:], in0=xr, in1=yi, op=ALU.mult))
            V(lambda v: v.tensor_tensor(out=t2[:], in0=xi, in1=yr, op=ALU.mult))
            if neg_i:
                V(lambda v: v.scalar_tensor_tensor(out=out_i, in0=t1[:], scalar=-1.0, in1=t2[:], op0=ALU.mult, op1=ALU.subtract))
            else:
                V(lambda v: v.tensor_tensor(out=out_i, in0=t1[:], in1=t2[:], op=ALU.add))
        cmul(Bcr[:], Bci[:], bc(cfr[:]), bc(cfi[:]), br[:], bi[:])
        for k in range(1, 8):
            pr, pi_ = PW[:, k, 0, :], PW[:, k, 1, :]
            qr, qi = PW[:, k + 1, 0, :], PW[:, k + 1, 1, :]
            V(lambda v, pr=pr: v.tensor_tensor(out=sn[:], in0=pr, in1=lr, op=ALU.mult))
            V(lambda v, pi_=pi_: v.tensor_tensor(out=cs[:], in0=pi_, in1=li, op=ALU.mult))
            V(lambda v, qr=qr: v.tensor_tensor(out=qr, in0=sn[:], in1=cs[:], op=ALU.subtract))
            V(lambda v, pr=pr: v.tensor_tensor(out=sn[:], in0=pr, in1=li, op=ALU.mult))
            V(lambda v, pi_=pi_: v.tensor_tensor(out=cs[:], in0=pi_, in1=lr, op=ALU.mult))
            V(lambda v, qi=qi: v.tensor_tensor(out=qi, in0=sn[:], in1=cs[:], op=ALU.add))
        for k in range(1, 9):
            pr, pi_ = PW[:, k, 0, :], PW[:, k, 1, :]
            V(lambda v, pr=pr: v.tensor_tensor(out=sn[:], in0=pr, in1=pr, op=ALU.mult))
            V(lambda v, pi_=pi_: v.tensor_tensor(out=cs[:], in0=pi_, in1=pi_, op=ALU.mult))
            V(lambda v: v.tensor_tensor(out=sn[:], in0=sn[:], in1=cs[:], op=ALU.add))
            V(lambda v: v.reciprocal(out=sn[:], in_=sn[:]))
            V(lambda v, pr=pr, k=k: v.tensor_tensor(out=NG[:, k, 0, :], in0=pr, in1=sn[:], op=ALU.mult))
            V(lambda v, pi_=pi_, k=k: v.scalar_tensor_tensor(out=NG[:, k, 1, :], in0=pi_, scalar=-1.0, in1=sn[:], op0=ALU.mult, op1=ALU.mult))
        V(lambda v: v.tensor_copy(out=A12[:, 0, 0, :], in_=PW[:, 8, 0, :]))
        V(lambda v: v.tensor_copy(out=A12[:, 0, 1, :], in_=PW[:, 8, 0, :]))
        V(lambda v: v.tensor_scalar(out=A12[:, 1, 0, :], in0=PW[:, 8, 1, :], scalar1=-1.0, scalar2=None, op0=ALU.mult))
        V(lambda v: v.tensor_copy(out=A12[:, 1, 1, :], in_=PW[:, 8, 1, :]))
        V(lambda v: v.tensor_copy(out=AT[:, 0, :, :, :], in_=A12[:, :, :, :]))
        for l in range(1, 6):
            pr_, pi__ = AT[:, l - 1, 0, 0, :], AT[:, l - 1, 1, 1, :]
            V(lambda v, pr_=pr_: v.tensor_tensor(out=sn[:], in0=pr_, in1=pr_, op=ALU.mult))
            V(lambda v, pi__=pi__: v.tensor_tensor(out=cs[:], in0=pi__, in1=pi__, op=ALU.mult))
            V(lambda v, l=l: v.tensor_tensor(out=AT[:, l, 0, 0, :], in0=sn[:], in1=cs[:], op=ALU.subtract))
            V(lambda v, l=l: v.tensor_copy(out=AT[:, l, 0, 1, :], in_=AT[:, l, 0, 0, :]))
            V(lambda v, pr_=pr_, pi__=pi__: v.tensor_tensor(out=sn[:], in0=pr_, in1=pi__, op=ALU.mult))
            V(lambda v, l=l: v.tensor_scalar(out=AT[:, l, 1, 1, :], in0=sn[:], scalar1=2.0, scalar2=None, op0=ALU.mult))
            V(lambda v, l=l: v.tensor_scalar(out=AT[:, l, 1, 0, :], in0=sn[:], scalar1=-2.0, scalar2=None, op0=ALU.mult))
        VPv = VP[:].rearrange("q r g (t c) -> q r g t c", c=16)
        for t in range(8):
            cmul(VPv[:, 0, :, t, :], VPv[:, 1, :, t, :], cr[:], ci[:], bc(PW[:, t + 1, 0, :]), bc(PW[:, t + 1, 1, :]), neg_i=True)
        WSv = WS[:].rearrange("q r g (s c) -> q r g s c", c=16)
        for s in range(8):
            cmul(WSv[:, 0, :, s, :], WSv[:, 1, :, s, :], Bcr[:], Bci[:], bc(PW[:, 7 - s, 0, :]), bc(PW[:, 7 - s, 1, :]))
        if C.cfg.get("s5_stop", 99) <= 2:
            S.barrier(); return
        for g0 in range(0, 128, 4):
            bank = next_ps(C)

            def trs(pe, g0=g0, bank=bank):
                ins = None
                for gi in range(4):
                    g = g0 + gi
                    h, gp = g // 64, g % 64
                    for ri in range(2):
                        ins = pe.matmul(ps[:, bank, (gi * 2 + ri) * 64:(gi * 2 + ri + 1) * 64],
                                        WS[h * 64:(h + 1) * 64, ri, gp, :], identb[h * 64:(h + 1) * 64, :],
                                        start=True, stop=True)
                return ins
            S.op("pe", trs, reads=[bB, cB], writes=[C.PB[bank]])
            S.op("act", lambda a, g0=g0, bank=bank: a.activation(
                out=WTs[:, g0:g0 + 4, :, :], in_=ps[:, bank, :].rearrange("p (g r c) -> p g r c", g=4, r=2), func=AF.Copy),
                writes=[C.PB[bank], tabB])
        if C.cfg.get("s5_stop", 99) <= 3:
            S.barrier(); return
        for s in range(8):
            cmul(WSv[:, 0, :, s, :], WSv[:, 1, :, s, :], Bcr[:], Bci[:], bc(NG[:, s + 1, 0, :]), bc(NG[:, s + 1, 1, :]))
        for g0 in range(0, 128, 4):
            bank = next_ps(C)

            def gm(pe, g0=g0, bank=bank):
                ins = None
                for gi in range(4):
                    g = g0 + gi
                    h, gp = g // 64, g % 64
                    hs = slice(h * 64, (h + 1) * 64)
                    for ri in range(2):
                        ins = pe.matmul(ps[:, bank, gi * 128:(gi + 1) * 128], WS[hs, ri, gp, :], VP[hs, ri, gp, :],
                                        start=(ri == 0), stop=(ri == 1))
                return ins
            S.op("pe", gm, reads=[bB, cB], writes=[C.PB[bank]])
            tmpt = t1[:].rearrange("p a b -> p (a b)")[:, 0:512]
            S.op("dve", lambda v, bank=bank: v.tensor_tensor(out=tmpt, in0=ps[:, bank, :],
                                                              in1=mask4[:].rearrange("p a b -> p (a b)"), op=ALU.mult),
                 reads=[cB], writes=[C.PB[bank], bB])
            for gi in range(4):
                S.op("dve", lambda v, g=g0 + gi, gi=gi: v.scalar_tensor_tensor(
                    out=Toep[:, g, :], in0=C.ident[:, :], scalar=dcol[:, g:g + 1], in1=tmpt[:, gi * 128:(gi + 1) * 128],
                    op0=ALU.mult, op1=ALU.add), reads=[bB, cB, C.identB], writes=[tabB])
        S.barrier()
    if C.cfg.get("s5_stop", 99) <= 4:
        S.barrier(); return
    JB = 32
    U = sb("U", [128, 128, JB], BF16)
    UB = Buf("U", multi=True)
    Sx = sb("Sx", [128, 2, 64, JB], F32)
    SxB = Buf("Sx")
    Hb = sb("Hb", [128, 2, 64, JB], BF16)
    HbB = Buf("Hb")
    zst = sb("zst", [128, 128, JB], BF16)
    zstB = Buf("zst")
    zsts = sb("zsts", [128, 128, 4], BF16)
    Us = sb("Us", [128, 128, 4], BF16)
    ta = sb("ta", [128, 2, 64], F32)
    tb_ = sb("tb", [128, 2, 64], F32)
    gtmp = [sb(f"gt{i}", [128, 512], F32) for i in range(2)]
    gtmpB = [Buf(f"gt{i}") for i in range(2)]
    HB = Buf("H")
    S.op("dve", lambda v: v.memset(Hcar[:], 0.0), writes=[HB])
    S.op("dve", lambda v: v.memset(Hsel[:], 0.0), writes=[HB])

    def load_U(ud, j0, jn, Jtot):
        Ut = U if jn == JB else Us
        for s in range(8):
            S.dma("sp", out=Ut[16 * s:16 * s + 16, :, 0:jn],
                  in_=ud[s, :, j0:j0 + jn].rearrange("(g c) j -> c g j", c=16), writes=[UB])

    def compute_S(jn):
        Ut = U if jn == JB else Us
        for gb in range(8):
            bank = next_ps(C)
            pv = ps[:, bank, 0:2 * 8 * jn].rearrange("p (r g j) -> p r g j", r=2, g=8)

            def mm(pe, gb=gb, pv=pv):
                ins = None
                for h in range(2):
                    for gi in range(8):
                        gp = gb * 8 + gi
                        g = h * 64 + gp
                        for ri in range(2):
                            ins = pe.matmul(pv[h * 64:(h + 1) * 64, ri, gi, :], WTs[:, g, ri, :], Ut[:, g, 0:jn],
                                            start=True, stop=True, tile_position=(0, h * 64))
                return ins
            S.op("pe", mm, reads=[tabB, UB], writes=[C.PB[bank]])
            S.op("act", lambda a, gb=gb, pv=pv: a.activation(out=Sx[:, :, gb * 8:(gb + 1) * 8, 0:jn], in_=pv, func=AF.Copy),
                 writes=[C.PB[bank], SxB])

    def step(hprev, sx_j):
        S.op("dve", lambda v: v.tensor_tensor(out=ta[:], in0=A12[:, 0, :, :], in1=hprev, op=ALU.mult), reads=[HB, SxB, tabB], writes=[HB])
        S.op("dve", lambda v: v.tensor_tensor(out=tb_[:], in0=A12[:, 1, :, :], in1=hprev[:, ::-1, :], op=ALU.mult), reads=[HB, SxB, tabB], writes=[HB])
        S.op("dve", lambda v: v.tensor_tensor(out=ta[:], in0=ta[:], in1=tb_[:], op=ALU.add), reads=[HB], writes=[HB])
        S.op("dve", lambda v: v.tensor_tensor(out=sx_j, in0=ta[:], in1=sx_j, op=ALU.add), reads=[HB], writes=[SxB])

    def scan(jn, want_hb):
        for j in range(jn):
            hprev = Hcar[:] if j == 0 else Sx[:, :, :, j - 1]
            if want_hb:
                S.op("act", lambda a, hprev=hprev, j=j: a.activation(out=Hb[:, :, :, j], in_=hprev, func=AF.Copy),
                     reads=[HB, SxB], writes=[HbB])
            step(hprev, Sx[:, :, :, j])
        S.op("dve", lambda v: v.tensor_copy(out=Hcar[:], in_=Sx[:, :, :, jn - 1]), reads=[SxB], writes=[HB])

    tt1 = sb("tt1", [128, 2, 64, JB // 2], F32)
    tt2 = sb("tt2", [128, 2, 64, JB // 2], F32)

    def tree_scan(jn):
        nl = jn.bit_length() - 1
        for l in range(nl):
            st_ = 1 << (l + 1)
            n = jn // st_
            src = Sx[:, :, :, (1 << l) - 1:jn:st_]
            dst = Sx[:, :, :, st_ - 1:jn:st_]
            a1 = AT[:, l, 0, :, :].unsqueeze(3).broadcast_to([128, 2, 64, n])
            a2 = AT[:, l, 1, :, :].unsqueeze(3).broadcast_to([128, 2, 64, n])
            S.op("dve", lambda v, a1=a1, src=src, n=n: v.tensor_tensor(out=tt1[:, :, :, 0:n], in0=a1, in1=src, op=ALU.mult),
                 reads=[SxB, tabB], writes=[HB])
            S.op("dve", lambda v, a2=a2, src=src, n=n: v.tensor_tensor(out=tt2[:, :, :, 0:n], in0=a2, in1=src[:, ::-1, :, :], op=ALU.mult),
                 reads=[SxB, tabB], writes=[HB])
            S.op("dve", lambda v, n=n: v.tensor_tensor(out=tt1[:, :, :, 0:n], in0=tt1[:, :, :, 0:n], in1=tt2[:, :, :, 0:n], op=ALU.add),
                 reads=[HB], writes=[HB])
            S.op("dve", lambda v, dst=dst, n=n: v.tensor_tensor(out=dst, in0=dst, in1=tt1[:, :, :, 0:n], op=ALU.add),
                 reads=[HB], writes=[SxB])
        S.op("dve", lambda v: v.tensor_tensor(out=ta[:], in0=AT[:, nl, 0, :, :], in1=Hcar[:], op=ALU.mult), reads=[HB, tabB], writes=[HB])
        S.op("dve", lambda v: v.tensor_tensor(out=tb_[:], in0=AT[:, nl, 1, :, :], in1=Hcar[:, ::-1, :], op=ALU.mult), reads=[HB, tabB], writes=[HB])
        S.op("dve", lambda v: v.tensor_tensor(out=ta[:], in0=ta[:], in1=tb_[:], op=ALU.add), reads=[HB], writes=[HB])
        S.op("dve", lambda v: v.tensor_tensor(out=Hcar[:], in0=ta[:], in1=Sx[:, :, :, jn - 1], op=ALU.add), reads=[HB, SxB], writes=[HB])

    def compute_Y(jn, zd, j0):
        zt = zst if jn == JB else zsts
        for gb in range(8):
            bank = next_ps(C)
            pv = ps[:, bank, 0:16 * jn].rearrange("p (g j) -> p g j", g=16)

            def mm(pe, gb=gb, pv=pv):
                ins = None
                for gi in range(16):
                    g = gb * 16 + gi
                    h, gp = g // 64, g % 64
                    hs = slice(h * 64, (h + 1) * 64)
                    pe.matmul(pv[:, gi, :], Toep[:, g, :], (U if jn == JB else Us)[:, g, 0:jn], start=True, stop=False)
                    pe.matmul(pv[:, gi, :], VP[hs, 0, gp, :], Hb[hs, 0, gp, 0:jn], start=False, stop=False)
                    ins = pe.matmul(pv[:, gi, :], VP[hs, 1, gp, :], Hb[hs, 1, gp, 0:jn], start=False, stop=True)
                return ins
            S.op("pe", mm, reads=[tabB, UB, HbB], writes=[C.PB[bank]])
            if C.cfg.get("y_stop", 9) <= 1 and jn == 4:
                continue
            n = 16 * jn
            y = ps[:, bank, 0:n]
            g1, g1B = gtmp[0], gtmpB[0]
            g2, g2B = gtmp[1], gtmpB[1]
            S.op("act", lambda a, y=y: a.activation(out=g1[:, 0:n], in_=y, func=AF.Square), writes=[C.PB[bank], g1B])
            S.op("dve", lambda v: v.tensor_scalar(out=g1[:, 0:n], in0=g1[:, 0:n], scalar1=0.044715, scalar2=1.0,
                                                  op0=ALU.mult, op1=ALU.add), writes=[g1B])
            S.op("dve", lambda v, y=y: v.tensor_tensor(out=g1[:, 0:n], in0=g1[:, 0:n], in1=y, op=ALU.mult),
                 writes=[C.PB[bank], g1B])
            S.op("act", lambda a: a.activation(out=g2[:, 0:n], in_=g1[:, 0:n], func=AF.Sigmoid, scale=GELU_C),
                 reads=[g1B], writes=[g2B])
            S.op("dve", lambda v, y=y, gb=gb: v.tensor_tensor(
                out=zt[:, gb * 16:(gb + 1) * 16, 0:jn], in0=g2[:, 0:n].rearrange("p (g j) -> p g j", g=16),
                in1=y.rearrange("p (g j) -> p g j", g=16), op=ALU.mult),
                reads=[g2B], writes=[C.PB[bank], zstB])
        for t in range(8 if not (C.cfg.get("y_stop", 9) <= 2 and jn == 4) else 0):
            S.dma("sp", out=zd[t, :, j0:j0 + jn].rearrange("(g c) j -> c g j", c=16),
                  in_=zt[16 * t:16 * t + 16, :, 0:jn], reads=[zstB], writes=[C.ZdB])

    for t in range(n_pre):
        for hj in range(2):
            load_U(Ud_pre[t], hj * JB, JB, 64)
            compute_S(JB)
            tree_scan(JB)
        if (t + 2) % 4 == 0:
            c = (t + 2) // 4
            S.op("dve", lambda v, c=c: v.scalar_tensor_tensor(out=Hsel[:], in0=Hcar[:], scalar=hselv[:, c:c + 1], in1=Hsel[:],
                                                              op0=ALU.mult, op1=ALU.add), reads=[HB, cB], writes=[HB])
    if C.cfg.get("s5_stop", 99) <= 5:
        S.barrier(); return
    S.op("dve", lambda v: v.tensor_copy(out=Hcar[:], in_=Hsel[:]), reads=[HB], writes=[HB])
    for t in range(n_own):
        for hj in range(2):
            load_U(Ud_own[t], hj * JB, JB, 64)
            compute_S(JB)
            scan(JB, True)
            compute_Y(JB, Zd_own[t], hj * JB)
        if t == 0:
            S.op("dve", lambda v: v.tensor_scalar(out=Hcar[:], in0=Hcar[:], scalar1=validv[:, 0:1], scalar2=None, op0=ALU.mult),
                 reads=[HB, cB], writes=[HB])
    if C.cfg.get("s5_stop", 99) <= 6:
        S.barrier(); return
    with nc.allow_non_contiguous_dma(reason="state layout"):
        for h in range(2):
            hs = slice(h * 64, (h + 1) * 64)
            S.dma("sp", out=outs["s5_re_p"][hs, :].rearrange("g p -> p g"), in_=Hcar[hs, 0, :], reads=[HB], writes=[C.outB])
            S.dma("sp", out=outs["s5_im_p"][hs, :].rearrange("g p -> p g"), in_=Hcar[hs, 1, :], reads=[HB], writes=[C.outB])
    if C.cfg.get("s5_stop", 99) <= 7:
        S.barrier(); return
    if Ud_smp is not None:
        load_U(Ud_smp, 0, 4, 4)
        compute_S(4)
        for q in range(2):
            with nc.allow_non_contiguous_dma(reason="state layout"):
                for h in range(2):
                    hs = slice(h * 64, (h + 1) * 64)
                    S.dma("sp", out=Hcar[hs, 0, :], in_=I["st_re"][q, hs, :].rearrange("g p -> p g"), writes=[HB])
                    S.dma("sp", out=Hcar[hs, 1, :], in_=I["st_im"][q, hs, :].rearrange("g p -> p g"), writes=[HB])
            for k in range(2):
                j = 2 * q + k
                hprev = Hcar[:] if k == 0 else Sx[:, :, :, j - 1]
                S.op("act", lambda a, hprev=hprev, j=j: a.activation(out=Hb[:, :, :, j], in_=hprev, func=AF.Copy),
                     reads=[HB, SxB], writes=[HbB])
                step(hprev, Sx[:, :, :, j])
            with nc.allow_non_contiguous_dma(reason="state layout"):
                for h in range(2):
                    hs = slice(h * 64, (h + 1) * 64)
                    S.dma("sp", out=outs["s5_re_s"][q, hs, :].rearrange("g p -> p g"), in_=Sx[hs, 0, :, 2 * q + 1],
                          reads=[SxB], writes=[C.outB])
                    S.dma("sp", out=outs["s5_im_s"][q, hs, :].rearrange("g p -> p g"), in_=Sx[hs, 1, :, 2 * q + 1],
                          reads=[SxB], writes=[C.outB])
        if C.cfg.get("s5_stop", 99) <= 8:
            S.barrier(); return
        compute_Y(4, Zd_smp, 0)
    S.barrier()
    stk.close()


def load_tile_s(C, xd, zd, N):
    S = C.S
    for kt in range(KT):
        S.dma("sp", out=C.X[:, kt, 0:N].rearrange("p (s j) -> p s j", s=8),
              in_=xd[:, kt * 128:(kt + 1) * 128, :].rearrange("s p j -> p s j"), reads=[C.X1B], writes=[C.XB[kt]])
        S.dma("sp", out=C.xb[:, kt, 0:N].rearrange("p (s j) -> p s j", s=8),
              in_=zd[:, kt * 128:(kt + 1) * 128, :].rearrange("s p j -> p s j"), reads=[C.ZdB], writes=[C.xbB[kt]])


def gated(C, N, wa_fn, wb_fn, rhs_a, rhs_a_bufs, kta):
    S = C.S
    ps = C.ps
    for mp in range(KT // 2):
        wa, waB = wa_fn(mp)
        wb, wbB = wb_fn(mp)
        for j in range(2):
            m = mp * 2 + j
            ba = next_ps(C)
            bb = next_ps(C)
            S.op("pe", lambda pe, ba=ba, j=j, wa=wa: _mm_group(
                pe, ps[:, ba, 0:N], [(wa[:, kt, j * 128:(j + 1) * 128], rhs_a[:, kt, 0:N]) for kt in range(kta)]),
                reads=[waB] + rhs_a_bufs, writes=[C.PB[ba]])
            S.op("pe", lambda pe, bb=bb, j=j, wb=wb: _mm_group(
                pe, ps[:, bb, 0:N], [(wb[:, kt, j * 128:(j + 1) * 128], C.xb[:, kt, 0:N]) for kt in range(KT)]),
                reads=[wbB] + C.xbB, writes=[C.PB[bb]])
            tmp, tmpB = next_tmp(C)
            S.op("act", lambda a, bb=bb, tmp=tmp: a.activation(out=tmp[:, 0:N], in_=ps[:, bb, 0:N], func=AF.Sigmoid),
                 writes=[C.PB[bb], tmpB])
            S.op("dve", lambda v, ba=ba, tmp=tmp: v.tensor_tensor(out=tmp[:, 0:N], in0=ps[:, ba, 0:N], in1=tmp[:, 0:N], op=ALU.mult),
                 writes=[C.PB[ba], tmpB])
            S.op("dve", lambda v, m=m, tmp=tmp: v.tensor_tensor(out=C.X[:, m, 0:N], in0=C.X[:, m, 0:N], in1=tmp[:, 0:N], op=ALU.add),
                 reads=[tmpB], writes=[C.XB[m]])


def glu_mix(C, N):
    gated(C, N,
          lambda mp: load_wb(C, "glu", mp),
          lambda mp: load_wb(C, "glu", 8 + mp),
          C.xb, C.xbB, KT)


def load_pT(C, psrc, N, order):
    S = C.S
    J = N // 8
    nb = (N + 127) // 128
    for b in range(nb):
        cols = min(128, N - b * 128)
        tok, tokB = C.tok[C.toki], C.tokB[C.toki]
        C.toki = (C.toki + 1) % 2
        if order == "nat":
            S.dma("sp", out=tok[0:cols, 0:256], in_=psrc[b * 128:b * 128 + cols, :], writes=[tokB])
        else:
            prs = psrc.rearrange("(j s) f -> s j f", s=8)
            if N >= 128:
                for ss in range(2):
                    S.dma("sp", out=tok[ss * 64:(ss + 1) * 64, 0:256], in_=prs[2 * b + ss, :, :], writes=[tokB])
            else:
                for s in range(8):
                    S.dma("sp", out=tok[s * J:(s + 1) * J, 0:256], in_=prs[s, :, :], writes=[tokB])
        bank = next_ps(C)

        def tr(pe, bank=bank, tok=tok, cols=cols):
            ins = None
            for k in range(2):
                ins = pe.matmul(C.ps[:, bank, k * 128:k * 128 + cols], tok[0:cols, k * 128:(k + 1) * 128],
                                C.ident[0:cols, 0:cols], start=True, stop=True)
            return ins
        S.op("pe", tr, reads=[tokB, C.identB], writes=[C.PB[bank]])
        S.op("act", lambda a, bank=bank, b=b, cols=cols: a.activation(
            out=C.pT[:, 0:2, b * 128:b * 128 + cols],
            in_=C.ps[:, bank, 0:256].rearrange("p (k c) -> p k c", k=2)[:, :, 0:cols], func=AF.Copy),
            writes=[C.PB[bank], C.pTB])


def ple(C, N, w_proj, gname):
    S = C.S
    S.dma("pool", out=C.Wp[:], in_=w_proj.rearrange("(k p) c -> p k c", p=128), writes=[C.WpB])
    gated(C, N,
          lambda mp: (C.Wp[:, :, mp * 256:(mp + 1) * 256], C.WpB),
          lambda mp: load_wb(C, gname, mp),
          C.pT, [C.pTB], 2)


def linear_residual(C, N, wname):
    S = C.S
    ps = C.ps
    for mp in range(KT // 2):
        ww, wB = load_wb(C, wname, mp)
        for j in range(2):
            m = mp * 2 + j
            bo = next_ps(C)
            S.op("pe", lambda pe, bo=bo, j=j, ww=ww: _mm_group(
                pe, ps[:, bo, 0:N], [(ww[:, kt, j * 128:(j + 1) * 128], C.xb[:, kt, 0:N]) for kt in range(KT)]),
                reads=[wB] + C.xbB, writes=[C.PB[bo]])
            S.op("dve", lambda v, bo=bo, m=m: v.tensor_tensor(out=C.X[:, m, 0:N], in0=ps[:, bo, 0:N], in1=C.X[:, m, 0:N], op=ALU.add),
                 writes=[C.PB[bo], C.XB[m]])


def qkv_project(C, N, Qd, Kd, Vd, kout, vout):
    S = C.S
    ps = C.ps
    for mp in range(16):
        ww, wB = load_wb(C, "qkv", mp)
        for j in range(2):
            m = mp * 2 + j
            bo = next_ps(C)
            S.op("pe", lambda pe, bo=bo, j=j, ww=ww: _mm_group(
                pe, ps[:, bo, 0:N], [(ww[:, kt, j * 128:(j + 1) * 128], C.xb[:, kt, 0:N]) for kt in range(KT)]),
                reads=[wB] + C.xbB, writes=[C.PB[bo]])
            S.op("act", lambda a, bo=bo, m=m: a.activation(out=C.gT[:, m, 0:N], in_=ps[:, bo, 0:N], func=AF.Copy),
                 writes=[C.PB[bo], C.gTB[m]])
    if Qd is not None:
        S.dma("sp", out=Qd.rearrange("(k p) n -> p k n", p=128), in_=C.gT[:, 0:16, 0:N], reads=C.gTB[0:16], writes=[C.QdB])
    S.dma("sp", out=Kd.rearrange("(k p) n -> p k n", p=128), in_=C.gT[:, 16:32, 0:N], reads=C.gTB[16:32], writes=[C.KdB])
    nb = (N + 127) // 128
    jobs = [("v", 16, Vd, vout)]
    if kout is not None:
        jobs.append(("k", 8, None, kout))
    for name, boff, vd, od in jobs:
        for cb in range(8):
            ww, wB = load_wb(C, "qkv", boff + cb)
            for tb in range(nb):
                rows = min(128, N - tb * 128)
                bo = next_ps(C)
                S.op("pe", lambda pe, bo=bo, tb=tb, rows=rows, ww=ww: _mm_group(
                    pe, ps[0:rows, bo, 0:256], [(C.xb[:, kt, tb * 128:tb * 128 + rows], ww[:, kt, :]) for kt in range(KT)]),
                    reads=[wB] + C.xbB, writes=[C.PB[bo]])
                if vd is not None:
                    S.op("dve", lambda v, bo=bo, tb=tb, rows=rows, cb=cb: v.tensor_copy(
                        out=C.gT[0:rows, tb * 4 + cb // 2, (cb % 2) * 256:(cb % 2) * 256 + 256], in_=ps[0:rows, bo, 0:256]),
                        writes=[C.PB[bo], C.gTB[tb * 4 + cb // 2]])
                if od is not None:
                    tmp, tmpB = next_tmp(C)
                    S.op("act", lambda a, bo=bo, rows=rows, tmp=tmp: a.activation(out=tmp[0:rows, 0:256], in_=ps[0:rows, bo, 0:256], func=AF.Copy),
                         writes=[C.PB[bo], tmpB])
                    S.dma("sp", out=od[tb * 128:tb * 128 + rows, cb * 256:(cb + 1) * 256], in_=tmp[0:rows, 0:256],
                          reads=[tmpB], writes=[C.outB])
        if vd is not None:
            for tb in range(nb):
                rows = min(128, N - tb * 128)
                S.dma("sp", out=vd[tb * 128:tb * 128 + rows, :].rearrange("p (q c) -> p q c", q=4),
                      in_=C.gT[0:rows, tb * 4:tb * 4 + 4, :], reads=C.gTB[tb * 4:tb * 4 + 4], writes=[C.VdB])


def phase3(C, kind, t, N, outs):
    S, I = C.S, C.I
    if kind == "own":
        xd, zd = C.X1_own[t], C.Zd_own[t]
        p0 = I["p_own"][0, t * NT:(t + 1) * NT, :]
        x2d, Qd, Kd, Vd = C.X2_own[t], (C.Qd_own[t - 1] if t >= 1 else None), C.Kd_own[t], C.Vd_own[t]
        last = (t == C.n_own - 1)
        kout = outs["k_p"] if last else None
        vout = outs["v_p"] if last else None
    else:
        xd, zd = C.X1_smp, C.Zd_smp
        p0 = I["p_smp"][0, :, :]
        x2d, Qd, Kd, Vd = C.X2_smp, C.Qd_smp, C.Kd_smp, C.Vd_smp
        kout, vout = outs["k_s"], outs["v_s"]
    load_tile_s(C, xd, zd, N)
    glu_mix(C, N)
    layer_norm(C, N, 1)
    ffn(C, N, "f2i0", "f2o0")
    layer_norm(C, N, 2)
    load_pT(C, p0, N, "sj")
    ple(C, N, I["ple_w_proj"][0], "pg0")
    layer_norm(C, N, 3)
    ffn(C, N, "f1i1", "f1o1")
    layer_norm(C, N, 4, perm="int")
    if kind == "smp" or t >= 1:
        S.dma("sp", out=x2d.rearrange("(k p) n -> p k n", p=128), in_=C.X[:, :, 0:N], reads=C.XB, writes=[C.X2B])
    qkv_project(C, N, Qd, Kd, Vd, kout, vout)


ATT_SCALE = 128 ** -0.5


def attention_phase(C, outs):
    from contextlib import ExitStack
    nc, S, I = C.nc, C.S, C.I
    ps = C.ps
    stk = ExitStack()

    def sb(name, shape, dt):
        return stk.enter_context(nc.sbuf_tensor("at_" + name, shape, dt))
    QT = sb("QT", [128, 16, 512], BF16); QB = Buf("QT")
    KTt = sb("KTt", [128, 16, 1024], BF16); KB = Buf("KT", multi=True)
    Vt = sb("Vt", [128, 8, 2048], BF16); VB = Buf("Vt", multi=True)
    oT = sb("oT", [128, 16, 512], BF16); oB = Buf("oT")
    kc = sb("kc", [128, 4, 2048], BF16); kcB = Buf("kc")
    Ksm = sb("Ksm", [128, 16, NS], BF16); KsmB = Buf("Ksm")
    biasT = sb("biasT", [64, 16, 192], F32)
    brev = sb("brev", [64, 16, 192], F32)
    t256 = sb("t256", [64, 16], F32)
    tab = sb("tab", [16, 257], F32)
    tabx = sb("tabx", [16, 255], F32)
    negv = sb("negv", [128, 1], F32)
    idb = sb("idb", [128, 128], BF16)
    cB = Buf("attconst")
    scs = [sb(f"sc{i}", [64, 576], F32) for i in range(2)]
    scB = [Buf(f"sc{i}") for i in range(2)]
    Pes = {0: [sb(f"pe0{i}", [64, 640], BF16) for i in range(2)], 1: [sb(f"pe1{i}", [64, 640], BF16) for i in range(2)],
           2: [sb("pes", [64, 640], BF16)]}
    PeB = {0: [Buf(), Buf()], 1: [Buf(), Buf()], 2: [Buf()]}
    PTs = [sb(f"PT{i}", [128, 5, 64], BF16) for i in range(2)]
    PTB = [Buf(), Buf()]
    stt = [sb(f"stt{i}", [64, 4], F32) for i in range(2)]
    sttB = [Buf(), Buf()]

    S.dma("pool", out=idb[:], in_=I["ident"][:, :], writes=[cB])
    S.dma("sp", out=negv[:], in_=I["valid"][:, :], writes=[cB])
    S.op("dve", lambda v: v.tensor_scalar(out=negv[:], in0=negv[:], scalar1=-1.0, scalar2=1e30, op0=ALU.add, op1=ALU.mult),
         reads=[cB], writes=[cB])
    for par in Pes:
        for pb, pbB in zip(Pes[par], PeB[par]):
            S.op("dve", lambda v, pb=pb: v.memset(pb[:], 0.0), writes=[pbB])
    Ed = nc.dram_tensor("att_Ed", [16, 255], F32, kind="Internal").ap()
    EdB = Buf("Ed")
    S.dma("sp", out=tab[:], in_=I["attn_rel_bias"][:, :], writes=[cB])
    S.op("dve", lambda v: v.tensor_copy(out=tabx[:, 0:192], in_=tab[:, 65:257]), reads=[cB], writes=[cB])
    S.op("dve", lambda v: v.tensor_copy(out=tabx[:, 192:255], in_=tab[:, 256:257].broadcast_to([16, 63])), reads=[cB], writes=[cB])
    S.dma("sp", out=Ed[:, :], in_=tabx[:], reads=[cB], writes=[EdB])
    S.dma("sp", out=brev[:], in_=bass.AP(Ed.tensor, 0, [[1, 64], [255, 16], [1, 192]]), reads=[EdB], writes=[cB])
    S.op("dve", lambda v: v.tensor_tensor(out=biasT[:], in0=brev[:, :, ::-1], in1=brev[:, :, 191:192].broadcast_to([64, 16, 192]),
                                          op=ALU.subtract), reads=[cB], writes=[cB])
    cnt = [0]

    def attend(nq, qcols, kcol0, ncols, vb0, off, par, mask_cols, out_cols, last_rows):
        nblk = (off + ncols + 127) // 128
        for h in range(16):
            i = cnt[0] % 2
            cnt[0] += 1
            sc, scb = scs[i], scB[i]
            pe_l = Pes[par]
            pb, pbB = pe_l[i % len(pe_l)], PeB[par][i % len(pe_l)]
            PT, ptB = PTs[i], PTB[i]
            st, stB = stt[i], sttB[i]
            bA = next_ps(C)
            bB_ = next_ps(C)
            n1 = min(512, ncols)
            n2 = ncols - n1
            S.op("pe", lambda pe, bA=bA, h=h: pe.matmul(ps[0:nq, bA, 0:n1], QT[:, h, qcols], KTt[:, h, kcol0:kcol0 + n1],
                                                       start=True, stop=True), reads=[QB, KB], writes=[C.PB[bA]])
            S.op("pe", lambda pe, bB_=bB_, h=h: pe.matmul(ps[0:nq, bB_, 0:n2], QT[:, h, qcols], KTt[:, h, kcol0 + n1:kcol0 + ncols],
                                                         start=True, stop=True), reads=[QB, KB], writes=[C.PB[bB_]])
            S.op("dve", lambda v, bA=bA, sc=sc: v.tensor_scalar(out=sc[0:nq, 0:384], in0=ps[0:nq, bA, 0:384], scalar1=ATT_SCALE,
                                                                scalar2=None, op0=ALU.mult), writes=[C.PB[bA], scb])
            S.op("dve", lambda v, bA=bA, sc=sc, h=h: v.scalar_tensor_tensor(
                out=sc[0:nq, 384:512], in0=ps[0:nq, bA, 384:512], scalar=ATT_SCALE, in1=biasT[0:nq, h, 0:128],
                op0=ALU.mult, op1=ALU.add), reads=[cB], writes=[C.PB[bA], scb])
            S.op("dve", lambda v, bB_=bB_, sc=sc, h=h: v.scalar_tensor_tensor(
                out=sc[0:nq, 512:512 + n2], in0=ps[0:nq, bB_, 0:n2], scalar=ATT_SCALE, in1=biasT[0:nq, h, 128:128 + n2],
                op0=ALU.mult, op1=ALU.add), reads=[cB], writes=[C.PB[bB_], scb])
            if mask_cols > 0:
                S.op("dve", lambda v, sc=sc: v.tensor_scalar(out=sc[0:nq, 0:mask_cols], in0=sc[0:nq, 0:mask_cols],
                                                             scalar1=negv[0:nq, 0:1], scalar2=None, op0=ALU.add),
                     reads=[cB], writes=[scb])
            S.op("dve", lambda v, sc=sc, st=st: v.reduce_max(out=st[0:nq, 0:1], in_=sc[0:nq, 0:ncols], axis=AX.X),
                 reads=[scb], writes=[stB])
            S.op("dve", lambda v, st=st: v.tensor_scalar(out=st[0:nq, 1:2], in0=st[0:nq, 0:1], scalar1=-1.0, scalar2=None, op0=ALU.mult),
                 writes=[stB])
            S.op("dve", lambda v, st=st: v.memset(st[0:nq, 2:3], 0.0), writes=[stB])
            S.op("act", lambda a, sc=sc, st=st, pb=pb: a.activation(out=pb[0:nq, off:off + ncols], in_=sc[0:nq, 0:ncols], func=AF.Exp,
                                                                    bias=st[0:nq, 1:2], accum_out=st[0:nq, 2:3]),
                 reads=[scb], writes=[stB, pbB])
            S.op("dve", lambda v, st=st: v.reciprocal(out=st[0:nq, 3:4], in_=st[0:nq, 2:3]), writes=[stB])
            S.op("dve", lambda v, st=st, pb=pb: v.tensor_scalar(out=pb[0:nq, off:off + ncols], in0=pb[0:nq, off:off + ncols],
                                                                scalar1=st[0:nq, 3:4], scalar2=None, op0=ALU.mult),
                 reads=[stB], writes=[pbB])
            bT = next_ps(C)

            def trp(pe, bT=bT, pb=pb):
                ins = None
                for kb in range(nblk):
                    ins = pe.matmul(ps[:, bT, kb * 64:kb * 64 + nq], pb[0:nq, kb * 128:(kb + 1) * 128], idb[0:nq, 0:nq],
                                    start=True, stop=True)
                return ins
            S.op("pe", trp, reads=[pbB, cB], writes=[C.PB[bT]])
            S.op("act", lambda a, bT=bT, PT=PT: a.activation(
                out=PT[:, 0:nblk, 0:nq], in_=ps[:, bT, 0:nblk * 64].rearrange("p (k c) -> p k c", c=64)[:, :, 0:nq], func=AF.Copy),
                writes=[C.PB[bT], ptB])
            bO = next_ps(C)

            def pv(pe, bO=bO, PT=PT, h=h):
                ins = None
                for kb in range(nblk):
                    kr = 128 if kb < nblk - 1 else last_rows
                    ins = pe.matmul(ps[:, bO, 0:nq], Vt[0:kr, vb0 + kb, h * 128:(h + 1) * 128], PT[0:kr, kb, 0:nq],
                                    start=(kb == 0), stop=(kb == nblk - 1))
                return ins
            S.op("pe", pv, reads=[ptB, VB], writes=[C.PB[bO]])
            S.op("act", lambda a, bO=bO, h=h: a.activation(out=oT[:, h, out_cols], in_=ps[:, bO, 0:nq], func=AF.Copy),
                 writes=[C.PB[bO], oB])

    for t in range(1, C.n_own):
        S.dma("sp", out=QT[:], in_=C.Qd_own[t - 1].rearrange("(k p) n -> p k n", p=128), reads=[C.QdB], writes=[QB])
        for u in range(2):
            S.dma("sp", out=KTt[:, :, u * 512:(u + 1) * 512], in_=C.Kd_own[t - 1 + u].rearrange("(k p) n -> p k n", p=128),
                  reads=[C.KdB], writes=[KB])
            S.dma("sp", out=Vt[:, u * 4:(u + 1) * 4, :], in_=C.Vd_own[t - 1 + u].rearrange("(b p) f -> p b f", p=128),
                  reads=[C.VdB], writes=[VB])
        for c in range(8):
            mask_cols = (512 - 64 * c) if t == 1 else 0
            attend(64, slice(64 * c, 64 * c + 64), 64 * c, 576, c // 2, 64 * (c % 2), c % 2, mask_cols,
                   slice(64 * c, 64 * c + 64), 128)
        S.dma("sp", out=C.Od_own[t - 1].rearrange("(k p) n -> p k n", p=128), in_=oT[:], reads=[oB], writes=[C.OdB])
    if C.cfg.get("sample", True):
        S.dma("sp", out=QT[:, :, 0:NS], in_=C.Qd_smp.rearrange("(k p) n -> p k n", p=128), reads=[C.QdB], writes=[QB])
        for q in range(2):
            S.dma("pool", out=kc[:], in_=I["cache_k"][q].rearrange("(b p) f -> p b f", p=128), writes=[kcB])
            S.dma("pool", out=Vt[:, 0:4, :], in_=I["cache_v"][q].rearrange("(b p) f -> p b f", p=128), writes=[VB])
            S.dma("sp", out=Vt[0:16, 4, :], in_=C.Vd_smp[q * 16:(q + 1) * 16, :], reads=[C.VdB], writes=[VB])
            if q == 0:
                S.dma("sp", out=Ksm[:], in_=C.Kd_smp.rearrange("(k p) n -> p k n", p=128), reads=[C.KdB], writes=[KsmB])
            S.op("act", lambda a, q=q: a.activation(out=KTt[:, :, 512:528], in_=Ksm[:, :, q * 16:(q + 1) * 16], func=AF.Copy),
                 reads=[KsmB], writes=[KB])
            for h in range(16):
                bank = next_ps(C)

                def trk(pe, bank=bank, h=h):
                    ins = None
                    for b in range(4):
                        ins = pe.matmul(ps[:, bank, b * 128:(b + 1) * 128], kc[:, b, h * 128:(h + 1) * 128], idb[:, :],
                                        start=True, stop=True)
                    return ins
                S.op("pe", trk, reads=[kcB, cB], writes=[C.PB[bank]])
                S.op("act", lambda a, bank=bank, h=h: a.activation(out=KTt[:, h, 0:512], in_=ps[:, bank, :], func=AF.Copy),
                     writes=[C.PB[bank], KB])
            attend(16, slice(q * 16, q * 16 + 16), 0, 528, 0, 0, 2, 0, slice(q * 16, q * 16 + 16), 16)
        S.dma("sp", out=C.Od_smp.rearrange("(k p) n -> p k n", p=128), in_=oT[:, :, 0:NS], reads=[oB], writes=[C.OdB])
    S.barrier()
    stk.close()


def phase4b(C, kind, t, N, outs):
    S, I = C.S, C.I
    ps = C.ps
    if kind == "own":
        x2d, od = C.X2_own[t], C.Od_own[t - 1]
        p1 = I["p_own"][1, t * NT:(t + 1) * NT, :]
        yout = outs["y_p"][(t - 1) * NT:t * NT, :]
    else:
        x2d, od = C.X2_smp, C.Od_smp
        p1 = I["p_smp"][1, :, :]
        yout = outs["y_s"]
    S.dma("sp", out=C.X[:, :, 0:N], in_=x2d.rearrange("(k p) n -> p k n", p=128), reads=[C.X2B], writes=C.XB)
    S.dma("sp", out=C.xb[:, :, 0:N], in_=od.rearrange("(k p) n -> p k n", p=128), reads=[C.OdB], writes=C.xbB)
    linear_residual(C, N, "wo")
    layer_norm(C, N, 5)
    ffn(C, N, "f2i1", "f2o1")
    layer_norm(C, N, 6)
    load_pT(C, p1, N, "nat")
    ple(C, N, I["ple_w_proj"][1], "pg1")
    layer_norm(C, N, 7, final=True)
    nb = (N + 127) // 128
    for b in range(nb):
        rows = min(128, N - b * 128)
        tok, tokB = C.tok[C.toki], C.tokB[C.toki]
        C.toki = (C.toki + 1) % 2
        for k0 in range(0, KT, 4):
            bank = next_ps(C)

            def tr(pe, bank=bank, k0=k0, b=b, rows=rows):
                ins = None
                for j in range(4):
                    ins = pe.matmul(ps[0:rows, bank, j * 128:(j + 1) * 128], C.X[:, k0 + j, b * 128:b * 128 + rows], C.ident[:, :],
                                    start=True, stop=True)
                return ins
            S.op("pe", tr, reads=C.XB[k0:k0 + 4] + [C.identB], writes=[C.PB[bank]])
            S.op("act", lambda a, bank=bank, k0=k0, rows=rows, tok=tok: a.activation(
                out=tok[0:rows, k0 * 128:(k0 + 4) * 128], in_=ps[0:rows, bank, :], func=AF.Copy),
                writes=[C.PB[bank], tokB])
        S.dma("sp", out=yout[b * 128:b * 128 + rows, :], in_=tok[0:rows, :], reads=[tokB], writes=[C.outB])
```

```python
import math
import numpy as np
import concourse.bass as bass
import concourse.mybir as mybir
from concourse.bass_utils import run_bass_kernel_spmd

F32 = mybir.dt.float32
BF16 = mybir.dt.bfloat16
AF = mybir.ActivationFunctionType
ALU = mybir.AluOpType
AX = mybir.AxisListType

D = 2048
KT = 16
FF = 5632
FT = 44
NT = 512
NCORES = 8
ALPHA = float((2 * 2) ** 0.25)
LN_EPS = 1e-5
N_PRE = 27
N_OWN = 5
NS = 32
RING = 16


class Buf:
    __slots__ = ("name", "w", "r", "wl", "multi")

    def __init__(self, name="", multi=False):
        self.name = name
        self.w = None
        self.r = {}
        self.wl = {}
        self.multi = multi


class Sched:
    ENG = ("pe", "act", "dve", "pool", "sp")

    def __init__(self, nc):
        self.nc = nc
        self.h = {"pe": nc.tensor, "act": nc.scalar, "dve": nc.vector, "pool": nc.gpsimd, "sp": nc.sync}
        self.nsem = 0
        self.psem = {}
        self.pekeys = set()
        for e in self.ENG:
            self._newsem(e)
        self.seen = {e: {} for e in self.ENG}
        self.ring = {q: [self._alloc() for _ in range(RING)] for q in ("sp", "pool")}
        self.ring["act"] = [self._alloc() for _ in range(40)]
        self.ridx = {"sp": 0, "pool": 0, "act": 0}
        self.nins = {e: 0 for e in self.ENG}

    def _alloc(self):
        self.nsem += 1
        return [self.nsem, self.nc.alloc_semaphore(f"sm{self.nsem}"), 0]

    def _newsem(self, e):
        k = self._alloc()
        self.psem[e] = k
        if e == "pe":
            self.pekeys.add(k[0])

    def _wait(self, e, key, sem, val):
        if self.seen[e].get(key, 0) >= val:
            return
        self.h[e].wait_ge(sem, val)
        self.nins[e] += 1
        self.seen[e][key] = val

    def _deps(self, e, reads, writes, accum=False):
        need = {}

        def add(t):
            if t[0] not in need or need[t[0]][2] < t[2]:
                need[t[0]] = t
        for b in reads:
            if b.w is not None:
                add(b.w)
            for t in b.wl.values():
                add(t)
        for b in writes:
            if b.w is not None:
                add(b.w)
            if not accum:
                for t in b.wl.values():
                    add(t)
            for t in b.r.values():
                add(t)
        for key, t in need.items():
            if e == "pe" and key in self.pekeys:
                continue
            self._wait(e, t[0], t[1], t[2])

    def op(self, e, fn, reads=(), writes=()):
        self._deps(e, reads, writes)
        ins = fn(self.h[e])
        k = self.psem[e]
        k[2] += 1
        ins.then_inc(k[1], 1)
        self.nins[e] += 1
        tok = (k[0], k[1], k[2])
        for b in writes:
            b.w = tok
            b.r = {}
            b.wl = {}
        for b in reads:
            b.r[e] = tok
        if k[2] >= 30000:
            self._newsem(e)
        return tok

    def dma(self, q, out, in_, reads=(), writes=(), accum=False, **kw):
        if writes and all(b.multi for b in writes):
            accum = True
        self._deps(q, reads, writes, accum)
        i = self.ridx[q]
        self.ridx[q] = (i + 1) % len(self.ring[q])
        slot = self.ring[q][i]
        if slot[2] > 0:
            self._wait(q, slot[0], slot[1], slot[2])
        if slot[2] >= 30000:
            slot = self._alloc()
            self.ring[q][i] = slot
        ins = self.h[q].dma_start(out=out, in_=in_, **kw)
        slot[2] += 16
        ins.then_inc(slot[1], 16)
        self.nins[q] += 1
        tok = (slot[0], slot[1], slot[2])
        for b in writes:
            if accum:
                b.wl[slot[0]] = tok
            else:
                b.w = tok
                b.r = {}
                b.wl = {}
        for b in reads:
            b.r[("d", slot[0])] = tok
        return tok

    def barrier(self):
        toks = []
        for e in self.ENG:
            k = self.psem[e]
            if k[2] > 0:
                toks.append((k[0], k[1], k[2]))
        for q in ("sp", "pool", "act"):
            for slot in self.ring[q]:
                if slot[2] > 0:
                    toks.append((slot[0], slot[1], slot[2]))
        for e in self.ENG:
            for t in toks:
                if t[0] == self.psem[e][0]:
                    continue
                self._wait(e, t[0], t[1], t[2])


class Ctx:
    pass


def _mm_group(pe, out, pairs):
    n = len(pairs)
    ins = None
    for i, (l, r) in enumerate(pairs):
        ins = pe.matmul(out, l, r, start=(i == 0), stop=(i == n - 1))
    return ins


def build(cfg):
    nc = bass.Bass("TRN2", target_bir_lowering=False)
    S = Sched(nc)
    C = Ctx()
    C.nc, C.S, C.cfg = nc, S, cfg
    dbg = cfg.get("debug", False)

    def din(name, shape, dt=F32):
        return nc.dram_tensor(name, list(shape), dt, kind="ExternalInput").ap()

    def dout(name, shape, dt=F32):
        return nc.dram_tensor(name, list(shape), dt, kind="ExternalOutput").ap()

    def dscr(name, shape, dt=F32, out=False):
        return nc.dram_tensor(name, list(shape), dt, kind=("ExternalOutput" if (out and dbg) else "Internal")).ap()

    n_pre = cfg.get("n_pre", N_PRE)
    n_own = cfg.get("n_own", N_OWN)
    C.n_pre, C.n_own = n_pre, n_own

    I = {}
    I["x_full"] = din("x_full", [max(n_pre, 1) * NT, D])
    I["x_own"] = din("x_own", [n_own * NT, D])
    I["x_smp"] = din("x_smp", [NS, D])
    I["ident"] = din("ident", [128, 128])
    I["ffn1_w_in"] = din("ffn1_w_in", [2, D, 2 * FF])
    I["ffn1_w_out"] = din("ffn1_w_out", [2, FF, D])
    I["ln_g"] = din("ln_g", [2, 4, D])
    I["ln_b"] = din("ln_b", [2, 4, D])
    C.I = I

    def sb(name, shape, dt):
        return nc.alloc_sbuf_tensor("sb_" + name, shape, dt)
    C.sb = sb
    C.ident = sb("ident", [128, 128], F32)
    C.identB = Buf("ident")
    C.ones_bf = sb("ones_bf", [128, 128], BF16)
    C.onesB = Buf("ones")
    C.lng = sb("lng", [128, 8, KT], F32)
    C.lnb = sb("lnb", [128, 8, KT], F32)
    C.lnga = sb("lnga", [128, 8, KT], F32)
    C.lnba = sb("lnba", [128, 8, KT], F32)
    C.lnB = Buf("ln")
    C.ps = nc.alloc_psum_tensor("ps", [128, 8, 512], F32)
    C.PB = [Buf(f"ps{i}") for i in range(8)]

    S.dma("sp", out=C.ident[:], in_=I["ident"][:, :], writes=[C.identB])
    S.op("dve", lambda v: v.memset(C.ones_bf[:], 1.0), writes=[C.onesB])
    with nc.allow_non_contiguous_dma(reason="tiny ln param transposes"):
        S.dma("sp", out=C.lng[:], in_=I["ln_g"].rearrange("l s (kt p) -> p (l s) kt", p=128), writes=[C.lnB])
        S.dma("sp", out=C.lnb[:], in_=I["ln_b"].rearrange("l s (kt p) -> p (l s) kt", p=128), writes=[C.lnB])
    S.op("dve", lambda v: v.tensor_scalar(out=C.lnga[:], in0=C.lng[:], scalar1=ALPHA, scalar2=None, op0=ALU.mult),
         reads=[C.lnB], writes=[C.lnB])
    S.op("dve", lambda v: v.tensor_scalar(out=C.lnba[:], in0=C.lnb[:], scalar1=ALPHA, scalar2=None, op0=ALU.mult),
         reads=[C.lnB], writes=[C.lnB])

    C.psi = 0
    C.wsi = 0
    C.tmpi = 0
    C.toki = 0

    C.Ud_pre = dscr("Ud_pre", [max(n_pre, 1), 8, D, 64], BF16, out=True)
    C.Ud_own = dscr("Ud_own", [n_own, 8, D, 64], BF16, out=True)
    C.Ud_smp = dscr("Ud_smp", [8, D, 4], BF16, out=True)
    C.X1_own = dscr("X1_own", [n_own, 8, D, 64], F32, out=True)
    C.X1_smp = dscr("X1_smp", [8, D, 4], F32, out=True)
    C.UdB = Buf("Ud", multi=True)
    C.X1B = Buf("X1d", multi=True)

    C.ZdB = Buf("Zd", multi=True)
    C.outB = Buf("outs", multi=True)
    C.QdB, C.KdB, C.VdB, C.X2B, C.OdB = (Buf("Qd", multi=True), Buf("Kd", multi=True), Buf("Vd", multi=True),
                                            Buf("X2", multi=True), Buf("Od", multi=True))
    for nm, shp in [("s5_a_re", [128, 64]), ("s5_a_im", [128, 64]), ("s5_log_dt", [128]), ("s5_b_re", [128, 64, 16]),
                    ("s5_b_im", [128, 64, 16]), ("s5_c_re", [128, 16, 64]), ("s5_c_im", [128, 16, 64]), ("s5_d", [128, 16]),
                    ("hsel", [128, 8]), ("valid", [128, 1]), ("ident64", [128, 64]), ("mask_ts", [128, 128]),
                    ("st_re", [2, 128, 64]), ("st_im", [2, 128, 64]),
                    ("p_own", [2, n_own * NT, 256]), ("p_smp", [2, NS, 256]),
                    ("cache_k", [2, 512, D]), ("cache_v", [2, 512, D]),
                    ("ffn2_w_in", [2, D, 2 * FF]), ("ffn2_w_out", [2, FF, D]),
                    ("ple_w_proj", [2, 256, D]), ("ple_w_gate", [2, D, D]),
                    ("s5_w_glu", [D, 2 * D]), ("attn_w_qkv", [D, 3 * D]), ("attn_w_o", [D, D]), ("attn_rel_bias", [16, 257])]:
        I[nm] = din(nm, shp)
    O = {}
    O["y_p"] = dout("y_p", [(n_own - 1) * NT, D]); O["y_s"] = dout("y_s", [NS, D])
    O["s5_re_p"] = dout("s5_re_p", [128, 64]); O["s5_im_p"] = dout("s5_im_p", [128, 64])
    O["k_p"] = dout("k_p", [NT, D]); O["v_p"] = dout("v_p", [NT, D])
    O["s5_re_s"] = dout("s5_re_s", [2, 128, 64]); O["s5_im_s"] = dout("s5_im_s", [2, 128, 64])
    O["k_s"] = dout("k_s", [NS, D]); O["v_s"] = dout("v_s", [NS, D])
    C.O = O
    C.Zd_own = dscr("Zd_own", [n_own, 8, D, 64], BF16, out=True)
    C.Zd_smp = dscr("Zd_smp", [8, D, 4], BF16, out=True)
    C.X2_own = dscr("X2_own", [n_own, D, NT], F32, out=True)
    C.X2_smp = dscr("X2_smp", [D, NS], F32, out=True)
    C.Qd_own = dscr("Qd_own", [n_own - 1, D, NT], BF16, out=True)
    C.Qd_smp = dscr("Qd_smp", [D, NS], BF16, out=True)
    C.Kd_own = dscr("Kd_own", [n_own, D, NT], BF16, out=True)
    C.Kd_smp = dscr("Kd_smp", [D, NS], BF16, out=True)
    C.Vd_own = dscr("Vd_own", [n_own, NT, D], BF16, out=True)
    C.Vd_smp = dscr("Vd_smp", [NS, D], BF16, out=True)
    C.Od_own = dscr("Od_own", [n_own - 1, D, NT], BF16, out=True)
    C.Od_smp = dscr("Od_smp", [D, NS], BF16, out=True)
    from contextlib import ExitStack
    mode = cfg.get("mode", "full")
    if mode == "s5test":
        Ud_pre = din("Ud_pre_in", [max(n_pre, 1), 8, D, 64], BF16)
        Ud_own = din("Ud_own_in", [n_own, 8, D, 64], BF16)
        Ud_smp = din("Ud_smp_in", [8, D, 4], BF16)
        s5_phase(C, Ud_pre, Ud_own, Ud_smp, C.Zd_own, C.Zd_smp, O)
        C.nins = dict(S.nins)
        return nc, C
    smp = cfg.get("sample", True)
    stk = ExitStack()
    alloc_rowlocal(C, stk)
    setup_weights(C)
    emit_conv(C, C.conv_upfront)
    tiles = []
    for t in range(n_pre):
        tiles.append(("pre", t, NT))
    for t in range(n_own):
        tiles.append(("own", t, NT))
    if smp:
        tiles.append(("smp", 0, NS))
    for kind, t, N in tiles:
        if kind == "pre":
            src = I["x_full"][t * NT:(t + 1) * NT, :]
        elif kind == "own":
            src = I["x_own"][t * NT:(t + 1) * NT, :]
        else:
            src = I["x_smp"][:, :]
        load_xT(C, src, N)
        ffn(C, N, "f1i0", "f1o0")
        emit_conv(C, 8)
        layer_norm(C, N, 0, perm="deint")
        if kind == "pre":
            ud = C.Ud_pre[t]
        elif kind == "own":
            ud = C.Ud_own[t]
        else:
            ud = C.Ud_smp
        for kt in range(KT):
            S.dma("sp", out=ud[:, kt * 128:(kt + 1) * 128, :].rearrange("s p j -> p s j"),
                  in_=C.xb[:, kt, 0:N].rearrange("p (s j) -> p s j", s=8),
                  reads=[C.xbB[kt]], writes=[C.UdB])
        if kind != "pre":
            xd = C.X1_own[t] if kind == "own" else C.X1_smp
            for kt in range(KT):
                S.dma("sp", out=xd[:, kt * 128:(kt + 1) * 128, :].rearrange("s p j -> p s j"),
                      in_=C.X[:, kt, 0:N].rearrange("p (s j) -> p s j", s=8),
                      reads=[C.XB[kt]], writes=[C.X1B])
    emit_conv(C, 10 ** 6)
    S.barrier()
    stk.close()
    if mode == "p01":
        C.nins = dict(S.nins)
        return nc, C
    s5_phase(C, C.Ud_pre, C.Ud_own, C.Ud_smp if smp else None, C.Zd_own, C.Zd_smp, O)
    stk = ExitStack()
    alloc_rowlocal(C, stk)
    for t in range(n_own):
        phase3(C, "own", t, NT, O)
    if smp:
        phase3(C, "smp", 0, NS, O)
    S.barrier()
    stk.close()
    if mode == "p3":
        C.nins = dict(S.nins)
        return nc, C
    attention_phase(C, O)
    stk = ExitStack()
    alloc_rowlocal(C, stk)
    for t in range(1, n_own):
        phase4b(C, "own", t, NT, O)
    if smp:
        phase4b(C, "smp", 0, NS, O)
    S.barrier()
    stk.close()
    C.nins = dict(S.nins)
    return nc, C


def alloc_rowlocal(C, stk):
    nc = C.nc
    C.rlgen = getattr(C, "rlgen", 0) + 1
    gen = C.rlgen

    def sb(name, shape, dt):
        return stk.enter_context(nc.sbuf_tensor(f"rl{gen}_" + name, shape, dt))
    C.X = sb("X", [128, KT, NT], F32)
    C.XB = [Buf(f"X{k}") for k in range(KT)]
    C.xb = sb("xb", [128, KT, NT], BF16)
    C.xbB = [Buf(f"xb{k}") for k in range(KT)]
    C.gT = sb("gT", [128, FT, NT], BF16)
    C.gTB = [Buf(f"gT{k}") for k in range(FT)]
    C.tmp = [sb(f"tmp{i}", [128, NT], F32) for i in range(3)]
    C.tmpB = [Buf(f"tmp{i}") for i in range(3)]
    C.st = sb("st", [128, 6, NT], F32)
    C.stB = Buf("st")
    C.tok = [sb(f"tok{i}", [128, D], F32) for i in range(2)]
    C.tokB = [Buf(f"tok{i}", multi=True) for i in range(2)]
    C.pT = sb("pT", [128, 2, NT], BF16)
    C.pTB = Buf("pT")
    C.Wp = sb("Wp", [128, 2, D], BF16)
    C.WpB = Buf("Wp")
    NSLOT = 4
    C.wsl = [sb(f"wsl{i}", [128, 6144], BF16) for i in range(NSLOT)]
    C.wslB = [Buf(f"wsl{i}") for i in range(NSLOT)]


def next_tmp(C):
    i = C.tmpi
    C.tmpi = (i + 1) % len(C.tmp)
    return C.tmp[i], C.tmpB[i]


def next_ps(C, n=1):
    i = C.psi
    C.psi = (i + 1) % 8
    return i


def setup_weights(C):
    nc, I = C.nc, C.I
    C.W = {}
    C.convq = []

    def reg(name, ap2d, ktn, cw):
        nblk = ap2d.shape[1] // cw
        scr = nc.dram_tensor("wb_" + name, [nblk, 128, ktn * cw], BF16, kind="Internal").ap()
        C.W[name] = dict(src=ap2d, scr=scr, ktn=ktn, cw=cw, nblk=nblk, buf=Buf("wb_" + name, multi=True))
        for b in range(nblk):
            C.convq.append((name, b))
    reg("f1i0", I["ffn1_w_in"][0], KT, 256)
    reg("f1o0", I["ffn1_w_out"][0], FT, 128)
    C.conv_upfront = len(C.convq)
    reg("glu", I["s5_w_glu"], KT, 256)
    reg("f2i0", I["ffn2_w_in"][0], KT, 256)
    reg("f2o0", I["ffn2_w_out"][0], FT, 128)
    reg("pg0", I["ple_w_gate"][0], KT, 256)
    reg("f1i1", I["ffn1_w_in"][1], KT, 256)
    reg("f1o1", I["ffn1_w_out"][1], FT, 128)
    reg("qkv", I["attn_w_qkv"], KT, 256)
    reg("wo", I["attn_w_o"], KT, 256)
    reg("f2i1", I["ffn2_w_in"][1], KT, 256)
    reg("f2o1", I["ffn2_w_out"][1], FT, 128)
    reg("pg1", I["ple_w_gate"][1], KT, 256)
    C.convi = 0


def emit_conv(C, n):
    S = C.S
    while n > 0 and C.convi < len(C.convq):
        name, b = C.convq[C.convi]
        C.convi += 1
        n -= 1
        w = C.W[name]
        cw = w["cw"]
        S.dma("pool", out=w["scr"][b].rearrange("p (k c) -> p k c", c=cw),
              in_=w["src"][:, b * cw:(b + 1) * cw].rearrange("(k p) c -> p k c", p=128), writes=[w["buf"]])


def load_wb(C, name, blk):
    S = C.S
    w = C.W[name]
    ktn, cw = w["ktn"], w["cw"]
    i = C.wsi
    C.wsi = (i + 1) % len(C.wsl)
    S.dma("pool", out=C.wsl[i][:, 0:ktn * cw], in_=w["scr"][blk], reads=[w["buf"]], writes=[C.wslB[i]])
    return C.wsl[i][:, 0:ktn * cw].rearrange("p (k c) -> p k c", c=cw), C.wslB[i]


def load_w(C, src_ap, ktn, cw):
    S = C.S
    i = C.wsi
    C.wsi = (i + 1) % len(C.wsl)
    view = C.wsl[i][:, 0:ktn * cw].rearrange("p (k c) -> p k c", c=cw)
    S.dma("pool", out=view, in_=src_ap.rearrange("(k p) c -> p k c", p=128), writes=[C.wslB[i]])
    return view, C.wslB[i]


def load_xT(C, src, N):
    S = C.S
    nb = (N + 127) // 128
    for b in range(nb):
        rows = min(128, N - b * 128)
        ti = C.toki
        C.toki = (ti + 1) % 2
        tok, tokB = C.tok[ti], C.tokB[ti]
        S.dma("sp", out=tok[0:rows, :], in_=src[b * 128:b * 128 + rows, :], writes=[tokB])
        for k0 in range(0, KT, 4):
            bank = next_ps(C)

            def tr(pe, bank=bank, k0=k0, tok=tok, rows=rows):
                ins = None
                for j in range(4):
                    ins = pe.matmul(C.ps[:, bank, j * 128:j * 128 + rows],
                                    tok[0:rows, (k0 + j) * 128:(k0 + j + 1) * 128], C.ident[0:rows, 0:rows],
                                    start=True, stop=True)
                return ins
            S.op("pe", tr, reads=[tokB, C.identB], writes=[C.PB[bank]])
            src_ps = C.ps[:, bank, :].rearrange("p (j c) -> p j c", j=4)[:, :, 0:rows]
            S.op("act", lambda a, k0=k0, b=b, rows=rows, src_ps=src_ps: a.activation(
                out=C.X[:, k0:k0 + 4, b * 128:b * 128 + rows], in_=src_ps, func=AF.Copy, scale=ALPHA),
                writes=[C.PB[bank]] + [C.XB[k] for k in range(k0, k0 + 4)])
            S.op("dve", lambda v, k0=k0, b=b, rows=rows, src_ps=src_ps: v.tensor_copy(
                out=C.xb[:, k0:k0 + 4, b * 128:b * 128 + rows], in_=src_ps),
                writes=[C.PB[bank]] + [C.xbB[k] for k in range(k0, k0 + 4)])


def ffn(C, N, wi, wo_):
    S = C.S
    ps = C.ps
    for fp in range(FT // 2):
        wg, wgB = load_wb(C, wi, fp)
        wu, wuB = load_wb(C, wi, FT // 2 + fp)
        for j in range(2):
            f = fp * 2 + j
            bg = next_ps(C)
            bu = next_ps(C)
            S.op("pe", lambda pe, bg=bg, j=j, wg=wg: _mm_group(
                pe, ps[:, bg, 0:N], [(wg[:, kt, j * 128:(j + 1) * 128], C.xb[:, kt, 0:N]) for kt in range(KT)]),
                reads=[wgB] + C.xbB, writes=[C.PB[bg]])
            S.op("pe", lambda pe, bu=bu, j=j, wu=wu: _mm_group(
                pe, ps[:, bu, 0:N], [(wu[:, kt, j * 128:(j + 1) * 128], C.xb[:, kt, 0:N]) for kt in range(KT)]),
                reads=[wuB] + C.xbB, writes=[C.PB[bu]])
            tmp, tmpB = next_tmp(C)
            S.op("act", lambda a, bg=bg, tmp=tmp: a.activation(out=tmp[:, 0:N], in_=ps[:, bg, 0:N], func=AF.Silu),
                 writes=[C.PB[bg], tmpB])
            S.op("dve", lambda v, bu=bu, tmp=tmp, f=f: v.tensor_tensor(
                out=C.gT[:, f, 0:N], in0=ps[:, bu, 0:N], in1=tmp[:, 0:N], op=ALU.mult),
                reads=[tmpB], writes=[C.PB[bu], C.gTB[f]])
    for m in range(KT):
        wo, woB = load_wb(C, wo_, m)
        bo = next_ps(C)
        S.op("pe", lambda pe, bo=bo, wo=wo: _mm_group(
            pe, ps[:, bo, 0:N], [(wo[:, f, :], C.gT[:, f, 0:N]) for f in range(FT)]),
            reads=[woB] + C.gTB, writes=[C.PB[bo]])
        S.op("dve", lambda v, bo=bo, m=m: v.scalar_tensor_tensor(
            out=C.X[:, m, 0:N], in0=ps[:, bo, 0:N], scalar=0.5, in1=C.X[:, m, 0:N], op0=ALU.mult, op1=ALU.add),
            writes=[C.PB[bo], C.XB[m]])


def layer_norm(C, N, idx, perm=None, final=False):
    S = C.S
    ps = C.ps
    rb = C.gT[:, 0:KT, :]
    rsq = C.gT[:, KT:2 * KT, :]
    for kt in range(KT):
        S.op("dve", lambda v, kt=kt: v.tensor_copy(out=rb[:, kt, 0:N], in_=C.X[:, kt, 0:N]),
             reads=[C.XB[kt]], writes=[C.gTB[kt]])
        S.op("act", lambda a, kt=kt: a.activation(out=rsq[:, kt, 0:N], in_=C.X[:, kt, 0:N], func=AF.Square),
             reads=[C.XB[kt]], writes=[C.gTB[KT + kt]])
    b1 = next_ps(C)
    b2 = next_ps(C)
    S.op("pe", lambda pe: _mm_group(pe, ps[:, b1, 0:N], [(C.ones_bf[:], rb[:, kt, 0:N]) for kt in range(KT)]),
         reads=[C.onesB] + C.gTB[0:KT], writes=[C.PB[b1]])
    S.op("pe", lambda pe: _mm_group(pe, ps[:, b2, 0:N], [(C.ones_bf[:], rsq[:, kt, 0:N]) for kt in range(KT)]),
         reads=[C.onesB] + C.gTB[KT:2 * KT], writes=[C.PB[b2]])
    mu = C.st[:, 0, 0:N]
    ex2 = C.st[:, 1, 0:N]
    var = C.st[:, 2, 0:N]
    sd = C.st[:, 3, 0:N]
    rstd = C.st[:, 4, 0:N]
    S.op("act", lambda a: a.activation(out=mu, in_=ps[:, b1, 0:N], func=AF.Copy, scale=1.0 / D),
         writes=[C.PB[b1], C.stB])
    S.op("act", lambda a: a.activation(out=ex2, in_=ps[:, b2, 0:N], func=AF.Copy, scale=1.0 / D),
         writes=[C.PB[b2], C.stB])
    S.op("dve", lambda v: v.tensor_tensor(out=var, in0=mu, in1=mu, op=ALU.mult), reads=[C.stB], writes=[C.stB])
    S.op("dve", lambda v: v.tensor_tensor(out=var, in0=ex2, in1=var, op=ALU.subtract), reads=[C.stB], writes=[C.stB])
    S.op("dve", lambda v: v.tensor_scalar(out=var, in0=var, scalar1=LN_EPS, scalar2=None, op0=ALU.add),
         reads=[C.stB], writes=[C.stB])
    S.op("act", lambda a: a.activation(out=sd, in_=var, func=AF.Sqrt), reads=[C.stB], writes=[C.stB])
    S.op("dve", lambda v: v.reciprocal(out=rstd, in_=sd), reads=[C.stB], writes=[C.stB])

    def pv(ap):
        if perm is None:
            return ap
        if perm == "deint":
            return ap.rearrange("p (s j) -> p j s", s=8)
        return ap.rearrange("p (j s) -> p s j", s=8)

    def pin(ap):
        if perm is None:
            return ap
        if perm == "deint":
            return ap.rearrange("p (j s) -> p j s", s=8)
        return ap.rearrange("p (s j) -> p s j", s=8)

    for kt in range(KT):
        tmp, tmpB = next_tmp(C)
        S.op("dve", lambda v, kt=kt, tmp=tmp: v.tensor_tensor(out=tmp[:, 0:N], in0=C.X[:, kt, 0:N], in1=mu, op=ALU.subtract),
             reads=[C.XB[kt], C.stB], writes=[tmpB])
        S.op("dve", lambda v, tmp=tmp: v.tensor_tensor(out=tmp[:, 0:N], in0=tmp[:, 0:N], in1=rstd, op=ALU.mult),
             reads=[tmpB, C.stB], writes=[tmpB])
        S.op("act", lambda a, kt=kt, tmp=tmp: a.activation(
            out=pv(C.X[:, kt, 0:N]), in_=pin(tmp[:, 0:N]), func=AF.Identity,
            scale=(C.lng if final else C.lnga)[:, idx, kt:kt + 1], bias=(C.lnb if final else C.lnba)[:, idx, kt:kt + 1]),
            reads=[tmpB, C.lnB], writes=[C.XB[kt]])
        S.op("act", lambda a, kt=kt, tmp=tmp: a.activation(
            out=pv(C.xb[:, kt, 0:N]), in_=pin(tmp[:, 0:N]), func=AF.Identity,
            scale=C.lng[:, idx, kt:kt + 1], bias=C.lnb[:, idx, kt:kt + 1]),
            reads=[tmpB, C.lnB], writes=[C.xbB[kt]])


def host_consts():
    mask = np.zeros((128, 128), np.float32)
    for s_ in range(8):
        for t_ in range(8):
            if t_ >= s_:
                mask[s_ * 16:(s_ + 1) * 16, t_ * 16:(t_ + 1) * 16] = 1.0
    ident64 = np.zeros((128, 64), np.float32)
    ident64[np.arange(128), np.arange(128) % 64] = 1.0
    return {"ident": np.eye(128, dtype=np.float32), "ident64": ident64, "mask_ts": mask}


WEIGHT_KEYS = ["ffn1_w_in", "ffn1_w_out", "ffn2_w_in", "ffn2_w_out", "ln_g", "ln_b", "ple_w_proj", "ple_w_gate",
               "s5_a_re", "s5_a_im", "s5_log_dt", "s5_b_re", "s5_b_im", "s5_c_re", "s5_c_im", "s5_d", "s5_w_glu",
               "attn_w_qkv", "attn_w_o", "attn_rel_bias"]


def make_core_inputs(inputs, c, n_pre, n_own, own_start, consts):
    f32 = np.float32
    xp = np.asarray(inputs["x_prompt"])[0]
    pp = np.asarray(inputs["p_prompt"])[:, 0]
    lo = own_start - NT
    hi = own_start + (n_own - 1) * NT
    x_own = np.zeros((n_own * NT, D), f32)
    p_own = np.zeros((2, n_own * NT, 256), f32)
    a = max(lo, 0)
    x_own[a - lo:] = xp[a:hi]
    p_own[:, a - lo:] = pp[:, a:hi]
    m = dict(consts)
    m["x_full"] = np.ascontiguousarray(xp[0:max(n_pre, 1) * NT])
    m["x_own"] = x_own
    m["p_own"] = p_own
    m["x_smp"] = np.ascontiguousarray(np.asarray(inputs["x_sample"])[2 * c:2 * c + 2]).reshape(NS, D)
    m["p_smp"] = np.ascontiguousarray(np.asarray(inputs["p_sample"])[:, 2 * c:2 * c + 2]).reshape(2, NS, 256)
    m["st_re"] = np.ascontiguousarray(np.asarray(inputs["state_s5_re"])[2 * c:2 * c + 2])
    m["st_im"] = np.ascontiguousarray(np.asarray(inputs["state_s5_im"])[2 * c:2 * c + 2])
    m["cache_k"] = np.ascontiguousarray(np.asarray(inputs["cache_k"])[2 * c:2 * c + 2]).reshape(2, 512, D)
    m["cache_v"] = np.ascontiguousarray(np.asarray(inputs["cache_v"])[2 * c:2 * c + 2]).reshape(2, 512, D)
    hsel = np.zeros((128, 8), f32)
    if own_start - NT > 0:
        hsel[:, (own_start - NT + NT) // (4 * NT)] = 1.0
    m["hsel"] = hsel
    m["valid"] = np.full((128, 1), 1.0 if own_start > 0 else 0.0, f32)
    for k in WEIGHT_KEYS:
        m[k] = np.asarray(inputs[k])
    return m


_NC_CACHE = {}


def kernel(**inputs):
    n_pre, n_own = N_PRE, N_OWN
    key = (n_pre, n_own)
    if key not in _NC_CACHE:
        _NC_CACHE[key] = build(dict(n_pre=n_pre, n_own=n_own))
    nc, C = _NC_CACHE[key]
    consts = host_consts()
    in_maps = [make_core_inputs(inputs, c, n_pre, n_own, 2048 * c, consts) for c in range(NCORES)]
    res = run_bass_kernel_spmd(nc, in_maps, core_ids=list(range(NCORES)))
    R = res.results
    f32 = np.float32
    y_p = np.concatenate([np.asarray(R[c]["y_p"], f32) for c in range(NCORES)], axis=0)[None]
    y_s = np.concatenate([np.asarray(R[c]["y_s"], f32).reshape(2, 16, D) for c in range(NCORES)], axis=0)
    last = NCORES - 1
    s5_re_p = np.asarray(R[last]["s5_re_p"], f32)[None]
    s5_im_p = np.asarray(R[last]["s5_im_p"], f32)[None]
    k_p = np.asarray(R[last]["k_p"], f32).reshape(1, NT, 16, 128)
    v_p = np.asarray(R[last]["v_p"], f32).reshape(1, NT, 16, 128)
    s5_re_s = np.concatenate([np.asarray(R[c]["s5_re_s"], f32) for c in range(NCORES)], axis=0)
    s5_im_s = np.concatenate([np.asarray(R[c]["s5_im_s"], f32) for c in range(NCORES)], axis=0)
    k_s = np.concatenate([np.asarray(R[c]["k_s"], f32).reshape(2, 16, 16, 128) for c in range(NCORES)], axis=0)
    v_s = np.concatenate([np.asarray(R[c]["v_s"], f32).reshape(2, 16, 16, 128) for c in range(NCORES)], axis=0)
    return (y_p, y_s, s5_re_p, s5_im_p, k_p, v_p, s5_re_s, s5_im_s, k_s, v_s)


GELU_C = 1.5957691216057308


def s5_param_relayout(C):
    nc, S, I = C.nc, C.S, C.I
    P = {}
    for nm, shp in [("ar", [128, 64]), ("ai", [128, 64]), ("br", [128, 64, 16]), ("bi", [128, 64, 16]),
                    ("cr", [128, 64, 16]), ("ci", [128, 64, 16]), ("dcol", [128, 128])]:
        P[nm] = nc.dram_tensor("s5p_" + nm, shp, F32, kind="Internal").ap()
    C.s5p = P
    C.s5pB = Buf("s5p", multi=True)
    B = C.s5pB
    with nc.allow_non_contiguous_dma(reason="tiny s5 param layout"):
        for h in range(2):
            hs = slice(h * 64, (h + 1) * 64)
            S.dma("sp", out=P["ar"][hs, :], in_=I["s5_a_re"][hs, :].rearrange("g p -> p g"), writes=[B])
            S.dma("sp", out=P["ai"][hs, :], in_=I["s5_a_im"][hs, :].rearrange("g p -> p g"), writes=[B])
            S.dma("sp", out=P["br"][hs, :, :], in_=I["s5_b_re"][hs].rearrange("g p c -> p g c"), writes=[B])
            S.dma("sp", out=P["bi"][hs, :, :], in_=I["s5_b_im"][hs].rearrange("g p c -> p g c"), writes=[B])
            for c in range(16):
                S.dma("sp", out=P["cr"][hs, :, c], in_=I["s5_c_re"][hs, c, :].rearrange("g p -> p g"), writes=[B])
                S.dma("sp", out=P["ci"][hs, :, c], in_=I["s5_c_im"][hs, c, :].rearrange("g p -> p g"), writes=[B])
        for s_ in range(8):
            S.dma("sp", out=P["dcol"][16 * s_:16 * s_ + 16, :], in_=I["s5_d"].rearrange("g c -> c g"), writes=[B])


def s5_phase(C, Ud_pre, Ud_own, Ud_smp, Zd_own, Zd_smp, outs):
    from contextlib import ExitStack
    nc, S, I = C.nc, C.S, C.I
    ps = C.ps
    n_pre, n_own = C.n_pre, C.n_own
    stk = ExitStack()

    def sb(name, shape, dt):
        return stk.enter_context(nc.sbuf_tensor("s5_" + name, shape, dt))

    WTs = sb("WTs", [128, 128, 2, 64], BF16)
    VP = sb("VP", [128, 2, 64, 128], BF16)
    Toep = sb("Toep", [128, 128, 128], BF16)
    A12 = sb("A12", [128, 2, 2, 64], F32)
    AT = sb("AT", [128, 6, 2, 2, 64], F32)
    Hsel = sb("Hsel", [128, 2, 64], F32)
    Hcar = sb("Hcar", [128, 2, 64], F32)
    hselv = sb("hselv", [128, 8], F32)
    validv = sb("validv", [128, 1], F32)
    identb = sb("identb", [128, 64], BF16)
    mask4 = sb("mask4", [128, 4, 128], F32)
    dcol = sb("dcol", [128, 128], F32)
    tabB = Buf("tab")
    cB = Buf("s5consts", multi=True)

    S.dma("sp", out=hselv[:], in_=I["hsel"][:, :], writes=[cB])
    S.dma("sp", out=validv[:], in_=I["valid"][:, :], writes=[cB])
    S.dma("pool", out=identb[:], in_=I["ident64"][:, :], writes=[cB])
    for i in range(4):
        S.dma("sp", out=mask4[:, i, :], in_=I["mask_ts"][:, :], writes=[cB])
    if not hasattr(C, "s5p"):
        s5_param_relayout(C)
    P5 = C.s5p
    S.dma("sp", out=dcol[:], in_=P5["dcol"][:, :], reads=[C.s5pB], writes=[cB])

    with ExitStack() as bstk:
        def tb(name, shape, dt=F32):
            return bstk.enter_context(nc.sbuf_tensor("s5b_" + name, shape, dt))
        ar = tb("ar", [128, 64]); ai = tb("ai", [128, 64]); ldt = tb("ldt", [128, 64])
        br = tb("br", [128, 64, 16]); bi = tb("bi", [128, 64, 16])
        cr = tb("cr", [128, 64, 16]); ci = tb("ci", [128, 64, 16])
        Bcr = tb("Bcr", [128, 64, 16]); Bci = tb("Bci", [128, 64, 16])
        t1 = tb("t1", [128, 64, 16]); t2 = tb("t2", [128, 64, 16])
        PW = tb("PW", [128, 9, 2, 64]); NG = tb("NG", [128, 9, 2, 64])
        sc = [tb(f"sc{i}", [128, 64]) for i in range(8)]
        WS = tb("WS", [128, 2, 64, 128], BF16)
        bB = Buf("s5build", multi=True)
        for dst, nm in [(ar, "ar"), (ai, "ai"), (br, "br"), (bi, "bi"), (cr, "cr"), (ci, "ci")]:
            S.dma("sp", out=dst[:], in_=P5[nm], reads=[C.s5pB], writes=[bB])
        for h in range(2):
            hs = slice(h * 64, (h + 1) * 64)
            S.dma("sp", out=ldt[hs, :], in_=bass.AP(I["s5_log_dt"].tensor, h * 64, [[0, 64], [1, 64]]), writes=[bB])
        if C.cfg.get("s5_stop", 99) <= 1:
            S.barrier(); return
        def V(fn):
            S.op("dve", fn, reads=[bB, cB], writes=[bB])

        def A(fn):
            S.op("act", fn, reads=[bB, cB], writes=[bB])
        dt_, ardt, th, mag, sn, cs, den, nr = sc
        A(lambda a: a.activation(out=dt_[:], in_=ldt[:], func=AF.Exp))
        V(lambda v: v.tensor_tensor(out=ardt[:], in0=ar[:], in1=dt_[:], op=ALU.mult))
        V(lambda v: v.tensor_tensor(out=th[:], in0=ai[:], in1=dt_[:], op=ALU.mult))
        A(lambda a: a.activation(out=mag[:], in_=ardt[:], func=AF.Exp))
        MAGIC = 12582912.0

        def range_reduce(dst, src, shift):
            V(lambda v: v.tensor_scalar(out=den[:], in0=src, scalar1=shift, scalar2=1.0 / (2 * math.pi), op0=ALU.add, op1=ALU.mult))
            V(lambda v: v.tensor_scalar(out=den[:], in0=den[:], scalar1=MAGIC, scalar2=None, op0=ALU.add))
            V(lambda v: v.tensor_scalar(out=den[:], in0=den[:], scalar1=-MAGIC, scalar2=None, op0=ALU.add))
            V(lambda v: v.scalar_tensor_tensor(out=dst, in0=den[:], scalar=-2 * math.pi, in1=src, op0=ALU.mult, op1=ALU.add))
            if shift != 0.0:
                V(lambda v: v.tensor_scalar(out=dst, in0=dst, scalar1=shift, scalar2=None, op0=ALU.add))
        range_reduce(sn[:], th[:], 0.0)
        A(lambda a: a.activation(out=sn[:], in_=sn[:], func=AF.Sin))
        range_reduce(cs[:], th[:], 0.5 * math.pi)
        A(lambda a: a.activation(out=cs[:], in_=cs[:], func=AF.Sin))
        lr = PW[:, 1, 0, :]
        li = PW[:, 1, 1, :]
        V(lambda v: v.memset(PW[:, 0, 0, :], 1.0))
        V(lambda v: v.memset(PW[:, 0, 1, :], 0.0))
        V(lambda v: v.tensor_tensor(out=lr, in0=mag[:], in1=cs[:], op=ALU.mult))
        V(lambda v: v.tensor_tensor(out=li, in0=mag[:], in1=sn[:], op=ALU.mult))
        V(lambda v: v.tensor_tensor(out=den[:], in0=ar[:], in1=ar[:], op=ALU.mult))
        V(lambda v: v.tensor_tensor(out=dt_[:], in0=ai[:], in1=ai[:], op=ALU.mult))
        V(lambda v: v.tensor_tensor(out=den[:], in0=den[:], in1=dt_[:], op=ALU.add))
        V(lambda v: v.reciprocal(out=den[:], in_=den[:]))
        V(lambda v: v.tensor_scalar(out=nr[:], in0=lr, scalar1=-1.0, scalar2=None, op0=ALU.add))
        cfr, cfi = ardt, th
        V(lambda v: v.tensor_tensor(out=cfr[:], in0=nr[:], in1=ar[:], op=ALU.mult))
        V(lambda v: v.tensor_tensor(out=dt_[:], in0=li, in1=ai[:], op=ALU.mult))
        V(lambda v: v.tensor_tensor(out=cfr[:], in0=cfr[:], in1=dt_[:], op=ALU.add))
        V(lambda v: v.tensor_tensor(out=cfr[:], in0=cfr[:], in1=den[:], op=ALU.mult))
        V(lambda v: v.tensor_tensor(out=cfi[:], in0=li, in1=ar[:], op=ALU.mult))
        V(lambda v: v.tensor_tensor(out=dt_[:], in0=nr[:], in1=ai[:], op=ALU.mult))
        V(lambda v: v.tensor_tensor(out=cfi[:], in0=cfi[:], in1=dt_[:], op=ALU.subtract))
        V(lambda v: v.tensor_tensor(out=cfi[:], in0=cfi[:], in1=den[:], op=ALU.mult))

        def bc(ap2):
            return ap2.unsqueeze(2).broadcast_to([128, 64, 16])

        def cmul(out_r, out_i, xr, xi, yr, yi, neg_i=False):
            V(lambda v: v.tensor_tensor(out=t1[:], in0=xr, in1=yr, op=ALU.mult))
            V(lambda v: v.tensor_tensor(out=t2[:], in0=xi, in1=yi, op=ALU.mult))
            V(lambda v: v.tensor_tensor(out=out_r, in0=t1[:], in1=t2[:], op=ALU.subtract))
            V(lambda v: v.tensor_tensor(out=t1[:], in0=xr, in1=yi, op=ALU.mult))
            V(lambda v: v.tensor_tensor(out=t2[:], in0=xi, in1=yr, op=ALU.mult))
            if neg_i:
                V(lambda v: v.scalar_tensor_tensor(out=out_i, in0=t1[:], scalar=-1.0, in1=t2[:], op0=ALU.mult, op1=ALU.subtract))
            else:
                V(lambda v: v.tensor_tensor(out=out_i, in0=t1[:], in1=t2[:], op=ALU.add))
        cmul(Bcr[:], Bci[:], bc(cfr[:]), bc(cfi[:]), br[:], bi[:])
        for k in range(1, 8):
            pr, pi_ = PW[:, k, 0, :], PW[:, k, 1, :]
            qr, qi = PW[:, k + 1, 0, :], PW[:, k + 1, 1, :]
            V(lambda v, pr=pr: v.tensor_tensor(out=sn[:], in0=pr, in1=lr, op=ALU.mult))
            V(lambda v, pi_=pi_: v.tensor_tensor(out=cs[:], in0=pi_, in1=li, op=ALU.mult))
            V(lambda v, qr=qr: v.tensor_tensor(out=qr, in0=sn[:], in1=cs[:], op=ALU.subtract))
            V(lambda v, pr=pr: v.tensor_tensor(out=sn[:], in0=pr, in1=li, op=ALU.mult))
            V(lambda v, pi_=pi_: v.tensor_tensor(out=cs[:], in0=pi_, in1=lr, op=ALU.mult))
            V(lambda v, qi=qi: v.tensor_tensor(out=qi, in0=sn[:], in1=cs[:], op=ALU.add))
        for k in range(1, 9):
            pr, pi_ = PW[:, k, 0, :], PW[:, k, 1, :]
            V(lambda v, pr=pr: v.tensor_tensor(out=sn[:], in0=pr, in1=pr, op=ALU.mult))
            V(lambda v, pi_=pi_: v.tensor_tensor(out=cs[:], in0=pi_, in1=pi_, op=ALU.mult))
            V(lambda v: v.tensor_tensor(out=sn[:], in0=sn[:], in1=cs[:], op=ALU.add))
            V(lambda v: v.reciprocal(out=sn[:], in_=sn[:]))
            V(lambda v, pr=pr, k=k: v.tensor_tensor(out=NG[:, k, 0, :], in0=pr, in1=sn[:], op=ALU.mult))
            V(lambda v, pi_=pi_, k=k: v.scalar_tensor_tensor(out=NG[:, k, 1, :], in0=pi_, scalar=-1.0, in1=sn[:], op0=ALU.mult, op1=ALU.mult))
        V(lambda v: v.tensor_copy(out=A12[:, 0, 0, :], in_=PW[:, 8, 0, :]))
        V(lambda v: v.tensor_copy(out=A12[:, 0, 1, :], in_=PW[:, 8, 0, :]))
        V(lambda v: v.tensor_scalar(out=A12[:, 1, 0, :], in0=PW[:, 8, 1, :], scalar1=-1.0, scalar2=None, op0=ALU.mult))
        V(lambda v: v.tensor_copy(out=A12[:, 1, 1, :], in_=PW[:, 8, 1, :]))
        V(lambda v: v.tensor_copy(out=AT[:, 0, :, :, :], in_=A12[:, :, :, :]))
        for l in range(1, 6):
            pr_, pi__ = AT[:, l - 1, 0, 0, :], AT[:, l - 1, 1, 1, :]
            V(lambda v, pr_=pr_: v.tensor_tensor(out=sn[:], in0=pr_, in1=pr_, op=ALU.mult))
            V(lambda v, pi__=pi__: v.tensor_tensor(out=cs[:], in0=pi__, in1=pi__, op=ALU.mult))
            V(lambda v, l=l: v.tensor_tensor(out=AT[:, l, 0, 0, :], in0=sn[:], in1=cs[:], op=ALU.subtract))
            V(lambda v, l=l: v.tensor_copy(out=AT[:, l, 0, 1, :], in_=AT[:, l, 0, 0, :]))
            V(lambda v, pr_=pr_, pi__=pi__: v.tensor_tensor(out=sn[:], in0=pr_, in1=pi__, op=ALU.mult))
            V(lambda v, l=l: v.tensor_scalar(out=AT[:, l, 1, 1, :], in0=sn[:], scalar1=2.0, scalar2=None, op0=ALU.mult))
            V(lambda v, l=l: v.tensor_scalar(out=AT[:, l, 1, 0, :], in0=sn[:], scalar1=-2.0, scalar2=None, op0=ALU.mult))
        VPv = VP[:].rearrange("q r g (t c) -> q r g t c", c=16)
        for t in range(8):
            cmul(VPv[:, 0, :, t, :], VPv[:, 1, :, t, :], cr[:], ci[:], bc(PW[:, t + 1, 0, :]), bc(PW[:, t + 1, 1, :]), neg_i=True)
        WSv = WS[:].rearrange("q r g (s c) -> q r g s c", c=16)
        for s in range(8):
            cmul(WSv[:, 0, :, s, :], WSv[:, 1, :, s, :], Bcr[:], Bci[:], bc(PW[:, 7 - s, 0, :]), bc(PW[:, 7 - s, 1, :]))
        if C.cfg.get("s5_stop", 99) <= 2:
            S.barrier(); return
        for g0 in range(0, 128, 4):
            bank = next_ps(C)

            def trs(pe, g0=g0, bank=bank):
                ins = None
                for gi in range(4):
                    g = g0 + gi
                    h, gp = g // 64, g % 64
                    for ri in range(2):
                        ins = pe.matmul(ps[:, bank, (gi * 2 + ri) * 64:(gi * 2 + ri + 1) * 64],
                                        WS[h * 64:(h + 1) * 64, ri, gp, :], identb[h * 64:(h + 1) * 64, :],
                                        start=True, stop=True)
                return ins
            S.op("pe", trs, reads=[bB, cB], writes=[C.PB[bank]])
            S.op("act", lambda a, g0=g0, bank=bank: a.activation(
                out=WTs[:, g0:g0 + 4, :, :], in_=ps[:, bank, :].rearrange("p (g r c) -> p g r c", g=4, r=2), func=AF.Copy),
                writes=[C.PB[bank], tabB])
        if C.cfg.get("s5_stop", 99) <= 3:
            S.barrier(); return
        for s in range(8):
            cmul(WSv[:, 0, :, s, :], WSv[:, 1, :, s, :], Bcr[:], Bci[:], bc(NG[:, s + 1, 0, :]), bc(NG[:, s + 1, 1, :]))
        for g0 in range(0, 128, 4):
            bank = next_ps(C)

            def gm(pe, g0=g0, bank=bank):
                ins = None
                for gi in range(4):
                    g = g0 + gi
                    h, gp = g // 64, g % 64
                    hs = slice(h * 64, (h + 1) * 64)
                    for ri in range(2):
                        ins = pe.matmul(ps[:, bank, gi * 128:(gi + 1) * 128], WS[hs, ri, gp, :], VP[hs, ri, gp, :],
                                        start=(ri == 0), stop=(ri == 1))
                return ins
            S.op("pe", gm, reads=[bB, cB], writes=[C.PB[bank]])
            tmpt = t1[:].rearrange("p a b -> p (a b)")[:, 0:512]
            S.op("dve", lambda v, bank=bank: v.tensor_tensor(out=tmpt, in0=ps[:, bank, :],
                                                              in1=mask4[:].rearrange("p a b -> p (a b)"), op=ALU.mult),
                 reads=[cB], writes=[C.PB[bank], bB])
            for gi in range(4):
                S.op("dve", lambda v, g=g0 + gi, gi=gi: v.scalar_tensor_tensor(
                    out=Toep[:, g, :], in0=C.ident[:, :], scalar=dcol[:, g:g + 1], in1=tmpt[:, gi * 128:(gi + 1) * 128],
                    op0=ALU.mult, op1=ALU.add), reads=[bB, cB, C.identB], writes=[tabB])
        S.barrier()
    if C.cfg.get("s5_stop", 99) <= 4:
        S.barrier(); return
    JB = 32
    U = sb("U", [128, 128, JB], BF16)
    UB = Buf("U", multi=True)
    Sx = sb("Sx", [128, 2, 64, JB], F32)
    SxB = Buf("Sx")
    Hb = sb("Hb", [128, 2, 64, JB], BF16)
    HbB = Buf("Hb")
    zst = sb("zst", [128, 128, JB], BF16)
    zstB = Buf("zst")
    zsts = sb("zsts", [128, 128, 4], BF16)
    Us = sb("Us", [128, 128, 4], BF16)
    ta = sb("ta", [128, 2, 64], F32)
    tb_ = sb("tb", [128, 2, 64], F32)
    gtmp = [sb(f"gt{i}", [128, 512], F32) for i in range(2)]
    gtmpB = [Buf(f"gt{i}") for i in range(2)]
    HB = Buf("H")
    S.op("dve", lambda v: v.memset(Hcar[:], 0.0), writes=[HB])
    S.op("dve", lambda v: v.memset(Hsel[:], 0.0), writes=[HB])

    def load_U(ud, j0, jn, Jtot):
        Ut = U if jn == JB else Us
        for s in range(8):
            S.dma("sp", out=Ut[16 * s:16 * s + 16, :, 0:jn],
                  in_=ud[s, :, j0:j0 + jn].rearrange("(g c) j -> c g j", c=16), writes=[UB])

    def compute_S(jn):
        Ut = U if jn == JB else Us
        for gb in range(8):
            bank = next_ps(C)
            pv = ps[:, bank, 0:2 * 8 * jn].rearrange("p (r g j) -> p r g j", r=2, g=8)

            def mm(pe, gb=gb, pv=pv):
                ins = None
                for h in range(2):
                    for gi in range(8):
                        gp = gb * 8 + gi
                        g = h * 64 + gp
                        for ri in range(2):
                            ins = pe.matmul(pv[h * 64:(h + 1) * 64, ri, gi, :], WTs[:, g, ri, :], Ut[:, g, 0:jn],
                                            start=True, stop=True, tile_position=(0, h * 64))
                return ins
            S.op("pe", mm, reads=[tabB, UB], writes=[C.PB[bank]])
            S.op("act", lambda a, gb=gb, pv=pv: a.activation(out=Sx[:, :, gb * 8:(gb + 1) * 8, 0:jn], in_=pv, func=AF.Copy),
                 writes=[C.PB[bank], SxB])

    def step(hprev, sx_j):
        S.op("dve", lambda v: v.tensor_tensor(out=ta[:], in0=A12[:, 0, :, :], in1=hprev, op=ALU.mult), reads=[HB, SxB, tabB], writes=[HB])
        S.op("dve", lambda v: v.tensor_tensor(out=tb_[:], in0=A12[:, 1, :, :], in1=hprev[:, ::-1, :], op=ALU.mult), reads=[HB, SxB, tabB], writes=[HB])
        S.op("dve", lambda v: v.tensor_tensor(out=ta[:], in0=ta[:], in1=tb_[:], op=ALU.add), reads=[HB], writes=[HB])
        S.op("dve", lambda v: v.tensor_tensor(out=sx_j, in0=ta[:], in1=sx_j, op=ALU.add), reads=[HB], writes=[SxB])

    def scan(jn, want_hb):
        for j in range(jn):
            hprev = Hcar[:] if j == 0 else Sx[:, :, :, j - 1]
            if want_hb:
                S.op("act", lambda a, hprev=hprev, j=j: a.activation(out=Hb[:, :, :, j], in_=hprev, func=AF.Copy),
                     reads=[HB, SxB], writes=[HbB])
            step(hprev, Sx[:, :, :, j])
        S.op("dve", lambda v: v.tensor_copy(out=Hcar[:], in_=Sx[:, :, :, jn - 1]), reads=[SxB], writes=[HB])

    tt1 = sb("tt1", [128, 2, 64, JB // 2], F32)
    tt2 = sb("tt2", [128, 2, 64, JB // 2], F32)

    def tree_scan(jn):
        nl = jn.bit_length() - 1
        for l in range(nl):
            st_ = 1 << (l + 1)
            n = jn // st_
            src = Sx[:, :, :, (1 << l) - 1:jn:st_]
            dst = Sx[:, :, :, st_ - 1:jn:st_]
            a1 = AT[:, l, 0, :, :].unsqueeze(3).broadcast_to([128, 2, 64, n])
            a2 = AT[:, l, 1, :, :].unsqueeze(3).broadcast_to([128, 2, 64, n])
            S.op("dve", lambda v, a1=a1, src=src, n=n: v.tensor_tensor(out=tt1[:, :, :, 0:n], in0=a1, in1=src, op=ALU.mult),
                 reads=[SxB, tabB], writes=[HB])
            S.op("dve", lambda v, a2=a2, src=src, n=n: v.tensor_tensor(out=tt2[:, :, :, 0:n], in0=a2, in1=src[:, ::-1, :, :], op=ALU.mult),
                 reads=[SxB, tabB], writes=[HB])
            S.op("dve", lambda v, n=n: v.tensor_tensor(out=tt1[:, :, :, 0:n], in0=tt1[:, :, :, 0:n], in1=tt2[:, :, :, 0:n], op=ALU.add),
                 reads=[HB], writes=[HB])
            S.op("dve", lambda v, dst=dst, n=n: v.tensor_tensor(out=dst, in0=dst, in1=tt1[:, :, :, 0:n], op=ALU.add),
                 reads=[HB], writes=[SxB])
        S.op("dve", lambda v: v.tensor_tensor(out=ta[:], in0=AT[:, nl, 0, :, :], in1=Hcar[:], op=ALU.mult), reads=[HB, tabB], writes=[HB])
        S.op("dve", lambda v: v.tensor_tensor(out=tb_[:], in0=AT[:, nl, 1, :, :], in1=Hcar[:, ::-1, :], op=ALU.mult), reads=[HB, tabB], writes=[HB])
        S.op("dve", lambda v: v.tensor_tensor(out=ta[:], in0=ta[:], in1=tb_[:], op=ALU.add), reads=[HB], writes=[HB])
        S.op("dve", lambda v: v.tensor_tensor(out=Hcar[:], in0=ta[:], in1=Sx[:, :, :, jn - 1], op=ALU.add), reads=[HB, SxB], writes=[HB])

    def compute_Y(jn, zd, j0):
        zt = zst if jn == JB else zsts
        for gb in range(8):
            bank = next_ps(C)
            pv = ps[:, bank, 0:16 * jn].rearrange("p (g j) -> p g j", g=16)

            def mm(pe, gb=gb, pv=pv):
                ins = None
                for gi in range(16):
                    g = gb * 16 + gi
                    h, gp = g // 64, g % 64
                    hs = slice(h * 64, (h + 1) * 64)
                    pe.matmul(pv[:, gi, :], Toep[:, g, :], (U if jn == JB else Us)[:, g, 0:jn], start=True, stop=False)
                    pe.matmul(pv[:, gi, :], VP[hs, 0, gp, :], Hb[hs, 0, gp, 0:jn], start=False, stop=False)
                    ins = pe.matmul(pv[:, gi, :], VP[hs, 1, gp, :], Hb[hs, 1, gp, 0:jn], start=False, stop=True)
                return ins
            S.op("pe", mm, reads=[tabB, UB, HbB], writes=[C.PB[bank]])
            if C.cfg.get("y_stop", 9) <= 1 and jn == 4:
                continue
            n = 16 * jn
            y = ps[:, bank, 0:n]
            g1, g1B = gtmp[0], gtmpB[0]
            g2, g2B = gtmp[1], gtmpB[1]
            S.op("act", lambda a, y=y: a.activation(out=g1[:, 0:n], in_=y, func=AF.Square), writes=[C.PB[bank], g1B])
            S.op("dve", lambda v: v.tensor_scalar(out=g1[:, 0:n], in0=g1[:, 0:n], scalar1=0.044715, scalar2=1.0,
                                                  op0=ALU.mult, op1=ALU.add), writes=[g1B])
            S.op("dve", lambda v, y=y: v.tensor_tensor(out=g1[:, 0:n], in0=g1[:, 0:n], in1=y, op=ALU.mult),
                 writes=[C.PB[bank], g1B])
            S.op("act", lambda a: a.activation(out=g2[:, 0:n], in_=g1[:, 0:n], func=AF.Sigmoid, scale=GELU_C),
                 reads=[g1B], writes=[g2B])
            S.op("dve", lambda v, y=y, gb=gb: v.tensor_tensor(
                out=zt[:, gb * 16:(gb + 1) * 16, 0:jn], in0=g2[:, 0:n].rearrange("p (g j) -> p g j", g=16),
                in1=y.rearrange("p (g j) -> p g j", g=16), op=ALU.mult),
                reads=[g2B], writes=[C.PB[bank], zstB])
        for t in range(8 if not (C.cfg.get("y_stop", 9) <= 2 and jn == 4) else 0):
            S.dma("sp", out=zd[t, :, j0:j0 + jn].rearrange("(g c) j -> c g j", c=16),
                  in_=zt[16 * t:16 * t + 16, :, 0:jn], reads=[zstB], writes=[C.ZdB])

    for t in range(n_pre):
        for hj in range(2):
            load_U(Ud_pre[t], hj * JB, JB, 64)
            compute_S(JB)
            tree_scan(JB)
        if (t + 2) % 4 == 0:
            c = (t + 2) // 4
            S.op("dve", lambda v, c=c: v.scalar_tensor_tensor(out=Hsel[:], in0=Hcar[:], scalar=hselv[:, c:c + 1], in1=Hsel[:],
                                                              op0=ALU.mult, op1=ALU.add), reads=[HB, cB], writes=[HB])
    if C.cfg.get("s5_stop", 99) <= 5:
        S.barrier(); return
    S.op("dve", lambda v: v.tensor_copy(out=Hcar[:], in_=Hsel[:]), reads=[HB], writes=[HB])
    for t in range(n_own):
        for hj in range(2):
            load_U(Ud_own[t], hj * JB, JB, 64)
            compute_S(JB)
            scan(JB, True)
            compute_Y(JB, Zd_own[t], hj * JB)
        if t == 0:
            S.op("dve", lambda v: v.tensor_scalar(out=Hcar[:], in0=Hcar[:], scalar1=validv[:, 0:1], scalar2=None, op0=ALU.mult),
                 reads=[HB, cB], writes=[HB])
    if C.cfg.get("s5_stop", 99) <= 6:
        S.barrier(); return
    with nc.allow_non_contiguous_dma(reason="state layout"):
        for h in range(2):
            hs = slice(h * 64, (h + 1) * 64)
            S.dma("sp", out=outs["s5_re_p"][hs, :].rearrange("g p -> p g"), in_=Hcar[hs, 0, :], reads=[HB], writes=[C.outB])
            S.dma("sp", out=outs["s5_im_p"][hs, :].rearrange("g p -> p g"), in_=Hcar[hs, 1, :], reads=[HB], writes=[C.outB])
    if C.cfg.get("s5_stop", 99) <= 7:
        S.barrier(); return
    if Ud_smp is not None:
        load_U(Ud_smp, 0, 4, 4)
        compute_S(4)
        for q in range(2):
            with nc.allow_non_contiguous_dma(reason="state layout"):
                for h in range(2):
                    hs = slice(h * 64, (h + 1) * 64)
                    S.dma("sp", out=Hcar[hs, 0, :], in_=I["st_re"][q, hs, :].rearrange("g p -> p g"), writes=[HB])
                    S.dma("sp", out=Hcar[hs, 1, :], in_=I["st_im"][q, hs, :].rearrange("g p -> p g"), writes=[HB])
            for k in range(2):
                j = 2 * q + k
                hprev = Hcar[:] if k == 0 else Sx[:, :, :, j - 1]
                S.op("act", lambda a, hprev=hprev, j=j: a.activation(out=Hb[:, :, :, j], in_=hprev, func=AF.Copy),
                     reads=[HB, SxB], writes=[HbB])
                step(hprev, Sx[:, :, :, j])
            with nc.allow_non_contiguous_dma(reason="state layout"):
                for h in range(2):
                    hs = slice(h * 64, (h + 1) * 64)
                    S.dma("sp", out=outs["s5_re_s"][q, hs, :].rearrange("g p -> p g"), in_=Sx[hs, 0, :, 2 * q + 1],
                          reads=[SxB], writes=[C.outB])
                    S.dma("sp", out=outs["s5_im_s"][q, hs, :].rearrange("g p -> p g"), in_=Sx[hs, 1, :, 2 * q + 1],
                          reads=[SxB], writes=[C.outB])
        if C.cfg.get("s5_stop", 99) <= 8:
            S.barrier(); return
        compute_Y(4, Zd_smp, 0)
    S.barrier()
    stk.close()


def load_tile_s(C, xd, zd, N):
    S = C.S
    for kt in range(KT):
        S.dma("sp", out=C.X[:, kt, 0:N].rearrange("p (s j) -> p s j", s=8),
              in_=xd[:, kt * 128:(kt + 1) * 128, :].rearrange("s p j -> p s j"), reads=[C.X1B], writes=[C.XB[kt]])
        S.dma("sp", out=C.xb[:, kt, 0:N].rearrange("p (s j) -> p s j", s=8),
              in_=zd[:, kt * 128:(kt + 1) * 128, :].rearrange("s p j -> p s j"), reads=[C.ZdB], writes=[C.xbB[kt]])


def gated(C, N, wa_fn, wb_fn, rhs_a, rhs_a_bufs, kta):
    S = C.S
    ps = C.ps
    for mp in range(KT // 2):
        wa, waB = wa_fn(mp)
        wb, wbB = wb_fn(mp)
        for j in range(2):
            m = mp * 2 + j
            ba = next_ps(C)
            bb = next_ps(C)
            S.op("pe", lambda pe, ba=ba, j=j, wa=wa: _mm_group(
                pe, ps[:, ba, 0:N], [(wa[:, kt, j * 128:(j + 1) * 128], rhs_a[:, kt, 0:N]) for kt in range(kta)]),
                reads=[waB] + rhs_a_bufs, writes=[C.PB[ba]])
            S.op("pe", lambda pe, bb=bb, j=j, wb=wb: _mm_group(
                pe, ps[:, bb, 0:N], [(wb[:, kt, j * 128:(j + 1) * 128], C.xb[:, kt, 0:N]) for kt in range(KT)]),
                reads=[wbB] + C.xbB, writes=[C.PB[bb]])
            tmp, tmpB = next_tmp(C)
            S.op("act", lambda a, bb=bb, tmp=tmp: a.activation(out=tmp[:, 0:N], in_=ps[:, bb, 0:N], func=AF.Sigmoid),
                 writes=[C.PB[bb], tmpB])
            S.op("dve", lambda v, ba=ba, tmp=tmp: v.tensor_tensor(out=tmp[:, 0:N], in0=ps[:, ba, 0:N], in1=tmp[:, 0:N], op=ALU.mult),
                 writes=[C.PB[ba], tmpB])
            S.op("dve", lambda v, m=m, tmp=tmp: v.tensor_tensor(out=C.X[:, m, 0:N], in0=C.X[:, m, 0:N], in1=tmp[:, 0:N], op=ALU.add),
                 reads=[tmpB], writes=[C.XB[m]])


def glu_mix(C, N):
    gated(C, N,
          lambda mp: load_wb(C, "glu", mp),
          lambda mp: load_wb(C, "glu", 8 + mp),
          C.xb, C.xbB, KT)


def load_pT(C, psrc, N, order):
    S = C.S
    J = N // 8
    nb = (N + 127) // 128
    for b in range(nb):
        cols = min(128, N - b * 128)
        tok, tokB = C.tok[C.toki], C.tokB[C.toki]
        C.toki = (C.toki + 1) % 2
        if order == "nat":
            S.dma("sp", out=tok[0:cols, 0:256], in_=psrc[b * 128:b * 128 + cols, :], writes=[tokB])
        else:
            prs = psrc.rearrange("(j s) f -> s j f", s=8)
            if N >= 128:
                for ss in range(2):
                    S.dma("sp", out=tok[ss * 64:(ss + 1) * 64, 0:256], in_=prs[2 * b + ss, :, :], writes=[tokB])
            else:
                for s in range(8):
                    S.dma("sp", out=tok[s * J:(s + 1) * J, 0:256], in_=prs[s, :, :], writes=[tokB])
        bank = next_ps(C)

        def tr(pe, bank=bank, tok=tok, cols=cols):
            ins = None
            for k in range(2):
                ins = pe.matmul(C.ps[:, bank, k * 128:k * 128 + cols], tok[0:cols, k * 128:(k + 1) * 128],
                                C.ident[0:cols, 0:cols], start=True, stop=True)
            return ins
        S.op("pe", tr, reads=[tokB, C.identB], writes=[C.PB[bank]])
        S.op("act", lambda a, bank=bank, b=b, cols=cols: a.activation(
            out=C.pT[:, 0:2, b * 128:b * 128 + cols],
            in_=C.ps[:, bank, 0:256].rearrange("p (k c) -> p k c", k=2)[:, :, 0:cols], func=AF.Copy),
            writes=[C.PB[bank], C.pTB])


def ple(C, N, w_proj, gname):
    S = C.S
    S.dma("pool", out=C.Wp[:], in_=w_proj.rearrange("(k p) c -> p k c", p=128), writes=[C.WpB])
    gated(C, N,
          lambda mp: (C.Wp[:, :, mp * 256:(mp + 1) * 256], C.WpB),
          lambda mp: load_wb(C, gname, mp),
          C.pT, [C.pTB], 2)


def linear_residual(C, N, wname):
    S = C.S
    ps = C.ps
    for mp in range(KT // 2):
        ww, wB = load_wb(C, wname, mp)
        for j in range(2):
            m = mp * 2 + j
            bo = next_ps(C)
            S.op("pe", lambda pe, bo=bo, j=j, ww=ww: _mm_group(
                pe, ps[:, bo, 0:N], [(ww[:, kt, j * 128:(j + 1) * 128], C.xb[:, kt, 0:N]) for kt in range(KT)]),
                reads=[wB] + C.xbB, writes=[C.PB[bo]])
            S.op("dve", lambda v, bo=bo, m=m: v.tensor_tensor(out=C.X[:, m, 0:N], in0=ps[:, bo, 0:N], in1=C.X[:, m, 0:N], op=ALU.add),
                 writes=[C.PB[bo], C.XB[m]])


def qkv_project(C, N, Qd, Kd, Vd, kout, vout):
    S = C.S
    ps = C.ps
    for mp in range(16):
        ww, wB = load_wb(C, "qkv", mp)
        for j in range(2):
            m = mp * 2 + j
            bo = next_ps(C)
            S.op("pe", lambda pe, bo=bo, j=j, ww=ww: _mm_group(
                pe, ps[:, bo, 0:N], [(ww[:, kt, j * 128:(j + 1) * 128], C.xb[:, kt, 0:N]) for kt in range(KT)]),
                reads=[wB] + C.xbB, writes=[C.PB[bo]])
            S.op("act", lambda a, bo=bo, m=m: a.activation(out=C.gT[:, m, 0:N], in_=ps[:, bo, 0:N], func=AF.Copy),
                 writes=[C.PB[bo], C.gTB[m]])
    if Qd is not None:
        S.dma("sp", out=Qd.rearrange("(k p) n -> p k n", p=128), in_=C.gT[:, 0:16, 0:N], reads=C.gTB[0:16], writes=[C.QdB])
    S.dma("sp", out=Kd.rearrange("(k p) n -> p k n", p=128), in_=C.gT[:, 16:32, 0:N], reads=C.gTB[16:32], writes=[C.KdB])
    nb = (N + 127) // 128
    jobs = [("v", 16, Vd, vout)]
    if kout is not None:
        jobs.append(("k", 8, None, kout))
    for name, boff, vd, od in jobs:
        for cb in range(8):
            ww, wB = load_wb(C, "qkv", boff + cb)
            for tb in range(nb):
                rows = min(128, N - tb * 128)
                bo = next_ps(C)
                S.op("pe", lambda pe, bo=bo, tb=tb, rows=rows, ww=ww: _mm_group(
                    pe, ps[0:rows, bo, 0:256], [(C.xb[:, kt, tb * 128:tb * 128 + rows], ww[:, kt, :]) for kt in range(KT)]),
                    reads=[wB] + C.xbB, writes=[C.PB[bo]])
                if vd is not None:
                    S.op("dve", lambda v, bo=bo, tb=tb, rows=rows, cb=cb: v.tensor_copy(
                        out=C.gT[0:rows, tb * 4 + cb // 2, (cb % 2) * 256:(cb % 2) * 256 + 256], in_=ps[0:rows, bo, 0:256]),
                        writes=[C.PB[bo], C.gTB[tb * 4 + cb // 2]])
                if od is not None:
                    tmp, tmpB = next_tmp(C)
                    S.op("act", lambda a, bo=bo, rows=rows, tmp=tmp: a.activation(out=tmp[0:rows, 0:256], in_=ps[0:rows, bo, 0:256], func=AF.Copy),
                         writes=[C.PB[bo], tmpB])
                    S.dma("sp", out=od[tb * 128:tb * 128 + rows, cb * 256:(cb + 1) * 256], in_=tmp[0:rows, 0:256],
                          reads=[tmpB], writes=[C.outB])
        if vd is not None:
            for tb in range(nb):
                rows = min(128, N - tb * 128)
                S.dma("sp", out=vd[tb * 128:tb * 128 + rows, :].rearrange("p (q c) -> p q c", q=4),
                      in_=C.gT[0:rows, tb * 4:tb * 4 + 4, :], reads=C.gTB[tb * 4:tb * 4 + 4], writes=[C.VdB])


def phase3(C, kind, t, N, outs):
    S, I = C.S, C.I
    if kind == "own":
        xd, zd = C.X1_own[t], C.Zd_own[t]
        p0 = I["p_own"][0, t * NT:(t + 1) * NT, :]
        x2d, Qd, Kd, Vd = C.X2_own[t], (C.Qd_own[t - 1] if t >= 1 else None), C.Kd_own[t], C.Vd_own[t]
        last = (t == C.n_own - 1)
        kout = outs["k_p"] if last else None
        vout = outs["v_p"] if last else None
    else:
        xd, zd = C.X1_smp, C.Zd_smp
        p0 = I["p_smp"][0, :, :]
        x2d, Qd, Kd, Vd = C.X2_smp, C.Qd_smp, C.Kd_smp, C.Vd_smp
        kout, vout = outs["k_s"], outs["v_s"]
    load_tile_s(C, xd, zd, N)
    glu_mix(C, N)
    layer_norm(C, N, 1)
    ffn(C, N, "f2i0", "f2o0")
    layer_norm(C, N, 2)
    load_pT(C, p0, N, "sj")
    ple(C, N, I["ple_w_proj"][0], "pg0")
    layer_norm(C, N, 3)
    ffn(C, N, "f1i1", "f1o1")
    layer_norm(C, N, 4, perm="int")
    if kind == "smp" or t >= 1:
        S.dma("sp", out=x2d.rearrange("(k p) n -> p k n", p=128), in_=C.X[:, :, 0:N], reads=C.XB, writes=[C.X2B])
    qkv_project(C, N, Qd, Kd, Vd, kout, vout)


ATT_SCALE = 128 ** -0.5


def attention_phase(C, outs):
    from contextlib import ExitStack
    nc, S, I = C.nc, C.S, C.I
    ps = C.ps
    stk = ExitStack()

    def sb(name, shape, dt):
        return stk.enter_context(nc.sbuf_tensor("at_" + name, shape, dt))
    QT = sb("QT", [128, 16, 512], BF16); QB = Buf("QT")
    KTt = sb("KTt", [128, 16, 1024], BF16); KB = Buf("KT", multi=True)
    Vt = sb("Vt", [128, 8, 2048], BF16); VB = Buf("Vt", multi=True)
    oT = sb("oT", [128, 16, 512], BF16); oB = Buf("oT")
    kc = sb("kc", [128, 4, 2048], BF16); kcB = Buf("kc")
    Ksm = sb("Ksm", [128, 16, NS], BF16); KsmB = Buf("Ksm")
    biasT = sb("biasT", [128, 16, 192], F32)
    brev = sb("brev", [128, 16, 192], F32)
    BM = sb("BM", [128, 16, 256], F32)
    sc2 = [sb(f"sc2{i}", [128, 640], F32) for i in range(2)]
    sc2B = [Buf(), Buf()]
    Pe2 = [sb(f"pe2{i}", [128, 640], BF16) for i in range(2)]
    Pe2B = [Buf(), Buf()]
    PT2 = [sb(f"PT2{i}", [128, 5, 128], BF16) for i in range(2)]
    PT2B = [Buf(), Buf()]
    st2 = [sb(f"st2{i}", [128, 4], F32) for i in range(2)]
    st2B = [Buf(), Buf()]
    t256 = sb("t256", [64, 16], F32)
    tab = sb("tab", [16, 257], F32)
    tabx = sb("tabx", [16, 255], F32)
    negv = sb("negv", [128, 1], F32)
    idb = sb("idb", [128, 128], BF16)
    cB = Buf("attconst")
    scs = [sb(f"sc{i}", [64, 576], F32) for i in range(2)]
    scB = [Buf(f"sc{i}") for i in range(2)]
    Pes = {0: [sb(f"pe0{i}", [64, 640], BF16) for i in range(2)], 1: [sb(f"pe1{i}", [64, 640], BF16) for i in range(2)],
           2: [sb("pes", [64, 640], BF16)]}
    PeB = {0: [Buf(), Buf()], 1: [Buf(), Buf()], 2: [Buf()]}
    PTs = [sb(f"PT{i}", [128, 5, 64], BF16) for i in range(2)]
    PTB = [Buf(), Buf()]
    stt = [sb(f"stt{i}", [64, 4], F32) for i in range(2)]
    sttB = [Buf(), Buf()]

    S.dma("pool", out=idb[:], in_=I["ident"][:, :], writes=[cB])
    S.dma("sp", out=negv[:], in_=I["valid"][:, :], writes=[cB])
    S.op("dve", lambda v: v.tensor_scalar(out=negv[:], in0=negv[:], scalar1=-1.0, scalar2=1e30, op0=ALU.add, op1=ALU.mult),
         reads=[cB], writes=[cB])
    for par in Pes:
        for pb, pbB in zip(Pes[par], PeB[par]):
            S.op("dve", lambda v, pb=pb: v.memset(pb[:], 0.0), writes=[pbB])
    Ed = nc.dram_tensor("att_Ed", [16, 255], F32, kind="Internal").ap()
    EdB = Buf("Ed")
    S.dma("sp", out=tab[:], in_=I["attn_rel_bias"][:, :], writes=[cB])
    S.op("dve", lambda v: v.tensor_copy(out=tabx[:, 0:192], in_=tab[:, 65:257]), reads=[cB], writes=[cB])
    S.op("dve", lambda v: v.tensor_copy(out=tabx[:, 192:255], in_=tab[:, 256:257].broadcast_to([16, 63])), reads=[cB], writes=[cB])
    S.dma("sp", out=Ed[:, :], in_=tabx[:], reads=[cB], writes=[EdB])
    S.dma("sp", out=brev[0:64], in_=bass.AP(Ed.tensor, 0, [[1, 64], [255, 16], [1, 192]]), reads=[EdB], writes=[cB])
    S.dma("sp", out=brev[64:128], in_=bass.AP(Ed.tensor, 0, [[1, 64], [255, 16], [1, 192]]), reads=[EdB], writes=[cB])
    S.op("dve", lambda v: v.tensor_tensor(out=biasT[:], in0=brev[:, :, ::-1], in1=brev[:, :, 191:192].broadcast_to([128, 16, 192]),
                                          op=ALU.subtract), reads=[cB], writes=[cB])
    S.op("dve", lambda v: v.tensor_copy(out=BM[0:64, :, 0:192], in_=biasT[0:64]), reads=[cB], writes=[cB])
    S.op("dve", lambda v: v.memset(BM[0:64, :, 192:256], -1e30), reads=[cB], writes=[cB])
    S.op("dve", lambda v: v.memset(BM[64:128, :, 0:64], 0.0), reads=[cB], writes=[cB])
    S.op("dve", lambda v: v.tensor_copy(out=BM[64:128, :, 64:256], in_=biasT[64:128]), reads=[cB], writes=[cB])
    cnt = [0]

    def attend(nq, qcols, kcol0, ncols, vb0, off, par, mask_cols, out_cols, last_rows):
        nblk = (off + ncols + 127) // 128
        for h in range(16):
            i = cnt[0] % 2
            cnt[0] += 1
            sc, scb = scs[i], scB[i]
            pe_l = Pes[par]
            pb, pbB = pe_l[i % len(pe_l)], PeB[par][i % len(pe_l)]
            PT, ptB = PTs[i], PTB[i]
            st, stB = stt[i], sttB[i]
            bA = next_ps(C)
            bB_ = next_ps(C)
            n1 = min(512, ncols)
            n2 = ncols - n1
            S.op("pe", lambda pe, bA=bA, h=h: pe.matmul(ps[0:nq, bA, 0:n1], QT[:, h, qcols], KTt[:, h, kcol0:kcol0 + n1],
                                                       start=True, stop=True), reads=[QB, KB], writes=[C.PB[bA]])
            S.op("pe", lambda pe, bB_=bB_, h=h: pe.matmul(ps[0:nq, bB_, 0:n2], QT[:, h, qcols], KTt[:, h, kcol0 + n1:kcol0 + ncols],
                                                         start=True, stop=True), reads=[QB, KB], writes=[C.PB[bB_]])
            S.op("dve", lambda v, bA=bA, sc=sc: v.tensor_scalar(out=sc[0:nq, 0:384], in0=ps[0:nq, bA, 0:384], scalar1=ATT_SCALE,
                                                                scalar2=None, op0=ALU.mult), writes=[C.PB[bA], scb])
            S.op("dve", lambda v, bA=bA, sc=sc, h=h: v.scalar_tensor_tensor(
                out=sc[0:nq, 384:512], in0=ps[0:nq, bA, 384:512], scalar=ATT_SCALE, in1=biasT[0:nq, h, 0:128],
                op0=ALU.mult, op1=ALU.add), reads=[cB], writes=[C.PB[bA], scb])
            S.op("dve", lambda v, bB_=bB_, sc=sc, h=h: v.scalar_tensor_tensor(
                out=sc[0:nq, 512:512 + n2], in0=ps[0:nq, bB_, 0:n2], scalar=ATT_SCALE, in1=biasT[0:nq, h, 128:128 + n2],
                op0=ALU.mult, op1=ALU.add), reads=[cB], writes=[C.PB[bB_], scb])
            if mask_cols > 0:
                S.op("dve", lambda v, sc=sc: v.tensor_scalar(out=sc[0:nq, 0:mask_cols], in0=sc[0:nq, 0:mask_cols],
                                                             scalar1=negv[0:nq, 0:1], scalar2=None, op0=ALU.add),
                     reads=[cB], writes=[scb])
            S.op("dve", lambda v, sc=sc, st=st: v.reduce_max(out=st[0:nq, 0:1], in_=sc[0:nq, 0:ncols], axis=AX.X),
                 reads=[scb], writes=[stB])
            S.op("dve", lambda v, st=st: v.tensor_scalar(out=st[0:nq, 1:2], in0=st[0:nq, 0:1], scalar1=-1.0, scalar2=None, op0=ALU.mult),
                 writes=[stB])
            S.op("dve", lambda v, st=st: v.memset(st[0:nq, 2:3], 0.0), writes=[stB])
            S.op("act", lambda a, sc=sc, st=st, pb=pb: a.activation(out=pb[0:nq, off:off + ncols], in_=sc[0:nq, 0:ncols], func=AF.Exp,
                                                                    bias=st[0:nq, 1:2], accum_out=st[0:nq, 2:3]),
                 reads=[scb], writes=[stB, pbB])
            S.op("dve", lambda v, st=st: v.reciprocal(out=st[0:nq, 3:4], in_=st[0:nq, 2:3]), writes=[stB])
            S.op("dve", lambda v, st=st, pb=pb: v.tensor_scalar(out=pb[0:nq, off:off + ncols], in0=pb[0:nq, off:off + ncols],
                                                                scalar1=st[0:nq, 3:4], scalar2=None, op0=ALU.mult),
                 reads=[stB], writes=[pbB])
            bT = next_ps(C)

            def trp(pe, bT=bT, pb=pb):
                ins = None
                for kb in range(nblk):
                    ins = pe.matmul(ps[:, bT, kb * 64:kb * 64 + nq], pb[0:nq, kb * 128:(kb + 1) * 128], idb[0:nq, 0:nq],
                                    start=True, stop=True)
                return ins
            S.op("pe", trp, reads=[pbB, cB], writes=[C.PB[bT]])
            S.op("act", lambda a, bT=bT, PT=PT: a.activation(
                out=PT[:, 0:nblk, 0:nq], in_=ps[:, bT, 0:nblk * 64].rearrange("p (k c) -> p k c", c=64)[:, :, 0:nq], func=AF.Copy),
                writes=[C.PB[bT], ptB])
            bO = next_ps(C)

            def pv(pe, bO=bO, PT=PT, h=h):
                ins = None
                for kb in range(nblk):
                    kr = 128 if kb < nblk - 1 else last_rows
                    ins = pe.matmul(ps[:, bO, 0:nq], Vt[0:kr, vb0 + kb, h * 128:(h + 1) * 128], PT[0:kr, kb, 0:nq],
                                    start=(kb == 0), stop=(kb == nblk - 1))
                return ins
            S.op("pe", pv, reads=[ptB, VB], writes=[C.PB[bO]])
            S.op("act", lambda a, bO=bO, h=h: a.activation(out=oT[:, h, out_cols], in_=ps[:, bO, 0:nq], func=AF.Copy),
                 writes=[C.PB[bO], oB])

    def attend2(c, mask_cols):
        k0 = 64 * c
        vb0 = c // 2
        for h in range(16):
            i = cnt[0] % 2
            cnt[0] += 1
            sc, scb = sc2[i], sc2B[i]
            pb, pbB = Pe2[i], Pe2B[i]
            PT, ptB = PT2[i], PT2B[i]
            st, stB = st2[i], st2B[i]
            bA = next_ps(C)
            bB_ = next_ps(C)
            S.op("pe", lambda pe, bA=bA, h=h: pe.matmul(ps[:, bA, 0:512], QT[:, h, k0:k0 + 128], KTt[:, h, k0:k0 + 512],
                                                       start=True, stop=True), reads=[QB, KB], writes=[C.PB[bA]])
            S.op("pe", lambda pe, bB_=bB_, h=h: pe.matmul(ps[:, bB_, 0:128], QT[:, h, k0:k0 + 128], KTt[:, h, k0 + 512:k0 + 640],
                                                         start=True, stop=True), reads=[QB, KB], writes=[C.PB[bB_]])
            S.op("dve", lambda v, bA=bA, sc=sc: v.tensor_scalar(out=sc[:, 0:384], in0=ps[:, bA, 0:384], scalar1=ATT_SCALE,
                                                                scalar2=None, op0=ALU.mult), writes=[C.PB[bA], scb])
            S.op("dve", lambda v, bA=bA, sc=sc, h=h: v.scalar_tensor_tensor(
                out=sc[:, 384:512], in0=ps[:, bA, 384:512], scalar=ATT_SCALE, in1=BM[:, h, 0:128],
                op0=ALU.mult, op1=ALU.add), reads=[cB], writes=[C.PB[bA], scb])
            S.op("dve", lambda v, bB_=bB_, sc=sc, h=h: v.scalar_tensor_tensor(
                out=sc[:, 512:640], in0=ps[:, bB_, 0:128], scalar=ATT_SCALE, in1=BM[:, h, 128:256],
                op0=ALU.mult, op1=ALU.add), reads=[cB], writes=[C.PB[bB_], scb])
            S.op("dve", lambda v, sc=sc: v.memset(sc[64:128, 0:64], -1e30), writes=[scb])
            if mask_cols > 0:
                S.op("dve", lambda v, sc=sc: v.tensor_scalar(out=sc[:, 0:mask_cols], in0=sc[:, 0:mask_cols],
                                                             scalar1=negv[:, 0:1], scalar2=None, op0=ALU.add),
                     reads=[cB], writes=[scb])
            S.op("dve", lambda v, sc=sc, st=st: v.reduce_max(out=st[:, 0:1], in_=sc[:, :], axis=AX.X), reads=[scb], writes=[stB])
            S.op("dve", lambda v, st=st: v.tensor_scalar(out=st[:, 1:2], in0=st[:, 0:1], scalar1=-1.0, scalar2=None, op0=ALU.mult),
                 writes=[stB])
            S.op("dve", lambda v, st=st: v.memset(st[:, 2:3], 0.0), writes=[stB])
            S.op("act", lambda a, sc=sc, st=st, pb=pb: a.activation(out=pb[:, :], in_=sc[:, :], func=AF.Exp,
                                                                    bias=st[:, 1:2], accum_out=st[:, 2:3]),
                 reads=[scb], writes=[stB, pbB])
            S.op("dve", lambda v, st=st: v.reciprocal(out=st[:, 3:4], in_=st[:, 2:3]), writes=[stB])
            S.op("dve", lambda v, st=st, pb=pb: v.tensor_scalar(out=pb[:, :], in0=pb[:, :], scalar1=st[:, 3:4], scalar2=None,
                                                                op0=ALU.mult), reads=[stB], writes=[pbB])
            bT1 = next_ps(C)
            bT2 = next_ps(C)

            def trp(pe, bT1=bT1, pb=pb):
                ins = None
                for kb in range(4):
                    ins = pe.matmul(ps[:, bT1, kb * 128:(kb + 1) * 128], pb[:, kb * 128:(kb + 1) * 128], idb[:, :],
                                    start=True, stop=True)
                return ins
            S.op("pe", trp, reads=[pbB, cB], writes=[C.PB[bT1]])
            S.op("pe", lambda pe, bT2=bT2, pb=pb: pe.matmul(ps[:, bT2, 0:128], pb[:, 512:640], idb[:, :], start=True, stop=True),
                 reads=[pbB, cB], writes=[C.PB[bT2]])
            S.op("act", lambda a, bT1=bT1, PT=PT: a.activation(
                out=PT[:, 0:4, :], in_=ps[:, bT1, :].rearrange("p (k c) -> p k c", c=128), func=AF.Copy),
                writes=[C.PB[bT1], ptB])
            S.op("act", lambda a, bT2=bT2, PT=PT: a.activation(out=PT[:, 4, :], in_=ps[:, bT2, 0:128], func=AF.Copy),
                 writes=[C.PB[bT2], ptB])
            bO = next_ps(C)

            def pv(pe, bO=bO, PT=PT, h=h):
                ins = None
                for kb in range(5):
                    ins = pe.matmul(ps[:, bO, 0:128], Vt[:, vb0 + kb, h * 128:(h + 1) * 128], PT[:, kb, :],
                                    start=(kb == 0), stop=(kb == 4))
                return ins
            S.op("pe", pv, reads=[ptB, VB], writes=[C.PB[bO]])
            S.op("act", lambda a, bO=bO, h=h: a.activation(out=oT[:, h, k0:k0 + 128], in_=ps[:, bO, 0:128], func=AF.Copy),
                 writes=[C.PB[bO], oB])

    for t in range(1, C.n_own):
        S.dma("sp", out=QT[:], in_=C.Qd_own[t - 1].rearrange("(k p) n -> p k n", p=128), reads=[C.QdB], writes=[QB])
        for u in range(2):
            S.dma("sp", out=KTt[:, :, u * 512:(u + 1) * 512], in_=C.Kd_own[t - 1 + u].rearrange("(k p) n -> p k n", p=128),
                  reads=[C.KdB], writes=[KB])
            S.dma("sp", out=Vt[:, u * 4:(u + 1) * 4, :], in_=C.Vd_own[t - 1 + u].rearrange("(b p) f -> p b f", p=128),
                  reads=[C.VdB], writes=[VB])
        for c in range(0, 8, 2):
            attend2(c, (512 - 64 * c) if t == 1 else 0)
        S.dma("sp", out=C.Od_own[t - 1].rearrange("(k p) n -> p k n", p=128), in_=oT[:], reads=[oB], writes=[C.OdB])
    if C.cfg.get("sample", True):
        S.dma("sp", out=QT[:, :, 0:NS], in_=C.Qd_smp.rearrange("(k p) n -> p k n", p=128), reads=[C.QdB], writes=[QB])
        for q in range(2):
            S.dma("pool", out=kc[:], in_=I["cache_k"][q].rearrange("(b p) f -> p b f", p=128), writes=[kcB])
            S.dma("pool", out=Vt[:, 0:4, :], in_=I["cache_v"][q].rearrange("(b p) f -> p b f", p=128), writes=[VB])
            S.dma("sp", out=Vt[0:16, 4, :], in_=C.Vd_smp[q * 16:(q + 1) * 16, :], reads=[C.VdB], writes=[VB])
            if q == 0:
                S.dma("sp", out=Ksm[:], in_=C.Kd_smp.rearrange("(k p) n -> p k n", p=128), reads=[C.KdB], writes=[KsmB])
            S.op("act", lambda a, q=q: a.activation(out=KTt[:, :, 512:528], in_=Ksm[:, :, q * 16:(q + 1) * 16], func=AF.Copy),
                 reads=[KsmB], writes=[KB])
            for h in range(16):
                bank = next_ps(C)

                def trk(pe, bank=bank, h=h):
                    ins = None
                    for b in range(4):
                        ins = pe.matmul(ps[:, bank, b * 128:(b + 1) * 128], kc[:, b, h * 128:(h + 1) * 128], idb[:, :],
                                        start=True, stop=True)
                    return ins
                S.op("pe", trk, reads=[kcB, cB], writes=[C.PB[bank]])
                S.op("act", lambda a, bank=bank, h=h: a.activation(out=KTt[:, h, 0:512], in_=ps[:, bank, :], func=AF.Copy),
                     writes=[C.PB[bank], KB])
            attend(16, slice(q * 16, q * 16 + 16), 0, 528, 0, 0, 2, 0, slice(q * 16, q * 16 + 16), 16)
        S.dma("sp", out=C.Od_smp.rearrange("(k p) n -> p k n", p=128), in_=oT[:, :, 0:NS], reads=[oB], writes=[C.OdB])
    S.barrier()
    stk.close()


def phase4b(C, kind, t, N, outs):
    S, I = C.S, C.I
    ps = C.ps
    if kind == "own":
        x2d, od = C.X2_own[t], C.Od_own[t - 1]
        p1 = I["p_own"][1, t * NT:(t + 1) * NT, :]
        yout = outs["y_p"][(t - 1) * NT:t * NT, :]
    else:
        x2d, od = C.X2_smp, C.Od_smp
        p1 = I["p_smp"][1, :, :]
        yout = outs["y_s"]
    S.dma("sp", out=C.X[:, :, 0:N], in_=x2d.rearrange("(k p) n -> p k n", p=128), reads=[C.X2B], writes=C.XB)
    S.dma("sp", out=C.xb[:, :, 0:N], in_=od.rearrange("(k p) n -> p k n", p=128), reads=[C.OdB], writes=C.xbB)
    linear_residual(C, N, "wo")
    layer_norm(C, N, 5)
    ffn(C, N, "f2i1", "f2o1")
    layer_norm(C, N, 6)
    load_pT(C, p1, N, "nat")
    ple(C, N, I["ple_w_proj"][1], "pg1")
    layer_norm(C, N, 7, final=True)
    nb = (N + 127) // 128
    for b in range(nb):
        rows = min(128, N - b * 128)
        tok, tokB = C.tok[C.toki], C.tokB[C.toki]
        C.toki = (C.toki + 1) % 2
        for k0 in range(0, KT, 4):
            bank = next_ps(C)

            def tr(pe, bank=bank, k0=k0, b=b, rows=rows):
                ins = None
                for j in range(4):
                    ins = pe.matmul(ps[0:rows, bank, j * 128:(j + 1) * 128], C.X[:, k0 + j, b * 128:b * 128 + rows], C.ident[:, :],
                                    start=True, stop=True)
                return ins
            S.op("pe", tr, reads=C.XB[k0:k0 + 4] + [C.identB], writes=[C.PB[bank]])
            S.op("act", lambda a, bank=bank, k0=k0, rows=rows, tok=tok: a.activation(
                out=tok[0:rows, k0 * 128:(k0 + 4) * 128], in_=ps[0:rows, bank, :], func=AF.Copy),
                writes=[C.PB[bank], tokB])
        S.dma("sp", out=yout[b * 128:b * 128 + rows, :], in_=tok[0:rows, :], reads=[tokB], writes=[C.outB])
```
